# Optimizing a Trainium2 kernel written in Bass

```python
import math
import jax, jax.numpy as jnp
from jax import lax
import numpy as np

D_MODEL = 1024
BATCH = 4
SEQ = 4096
DEPTH = 4

CTX_LEN = 256
GRID_W = 64
HG_WIDTH = D_MODEL // 2
HG_DIM = 128
HG_HEADS = HG_WIDTH // HG_DIM
CHUNK = 64
DA_WIDTH = D_MODEL // 4
DA_HEADS = 4
DA_V = DA_WIDTH // DA_HEADS
DA_QK = DA_V // 2
DA_QK_WIDTH = DA_HEADS * 2 * DA_QK
Q_BLOCK = 128
ROPE_THETA = 10000.0
ROPE_AXIS_DIM = DA_QK // 2
FT_WIDTH = D_MODEL // 4
FT_GROUPS = 4
MIX_WIDTH = HG_WIDTH + DA_WIDTH + FT_WIDTH
IN_SPLITS = (HG_WIDTH, HG_WIDTH, HG_WIDTH, HG_WIDTH, HG_WIDTH, DA_QK_WIDTH, DA_QK_WIDTH, DA_WIDTH, FT_WIDTH)
IN_WIDTH = sum(IN_SPLITS)
FF_HIDDEN = ((8 * D_MODEL // 3 + 255) // 256) * 256
N_MOD = 6
EPS = 1e-6

kernel_name = "hymba_style_hgrn2_diffattn_fnet_dit"


def rms_norm(x, g):
    xf = x.astype(jnp.float32)
    y = xf * lax.rsqrt(jnp.mean(xf * xf, axis=-1, keepdims=True) + EPS)
    return (y * g.astype(jnp.float32)).astype(x.dtype)


def adaln(x, g, shift, scale):
    return rms_norm(x, g) * (1 + scale) + shift


def modulation(cond, w, b):
    return jnp.split(jax.nn.silu(cond) @ w + b, N_MOD, axis=-1)


def split_heads(a, h):
    return a.reshape(a.shape[:-1] + (h, a.shape[-1] // h))


def split_projection(p):
    offsets = [int(o) for o in np.cumsum(IN_SPLITS)[:-1]]
    return jnp.split(p, offsets, axis=-1)


def axial_rope(t_len):
    n_rows = t_len // GRID_W
    rows = jnp.broadcast_to(jnp.arange(n_rows)[:, None], (n_rows, GRID_W)).reshape(-1)
    cols = jnp.broadcast_to(jnp.arange(GRID_W)[None, :], (n_rows, GRID_W)).reshape(-1)
    inv_freq = 1.0 / (ROPE_THETA ** (jnp.arange(0, ROPE_AXIS_DIM, 2, dtype=jnp.float32) / ROPE_AXIS_DIM))
    ang = jnp.stack([rows, cols], axis=-1).astype(jnp.float32)[:, :, None] * inv_freq
    return jnp.cos(ang), jnp.sin(ang)


def apply_rope(x, cos, sin):
    xs = x.reshape(x.shape[:-1] + (2, 2, ROPE_AXIS_DIM // 2)).astype(jnp.float32)
    cs, sn = cos[None, :, None, None], sin[None, :, None, None]
    x1, x2 = xs[..., 0, :], xs[..., 1, :]
    out = jnp.stack([x1 * cs - x2 * sn, x2 * cs + x1 * sn], axis=-2)
    return out.reshape(x.shape).astype(x.dtype)


def gla_chunkwise(q, k, v, log_f, s0):
    b_, t_, h_, _ = q.shape
    dv = v.shape[-1]
    n = t_ // CHUNK

    def chunks(a):
        return a.astype(jnp.float32).reshape(b_, n, CHUNK, h_, a.shape[-1]).transpose(1, 0, 3, 2, 4)

    incl = jnp.tril(jnp.ones((CHUNK, CHUNK), dtype=bool))[:, :, None]

    def step(state, inp):
        qi, ki, vi, gi = inp
        bcum = jnp.cumsum(gi, axis=2)
        o_inter = jnp.einsum('bhtk,bhkv->bhtv', qi * jnp.exp(bcum), state)
        rel = bcum[:, :, :, None, :] - bcum[:, :, None, :, :]
        decay = jnp.exp(jnp.where(incl, rel, -jnp.inf))
        scores = jnp.einsum('bhtk,bhsk,bhtsk->bhts', qi, ki, decay)
        o_intra = jnp.einsum('bhts,bhsv->bhtv', scores, vi)
        blast = bcum[:, :, -1:, :]
        state = state * jnp.exp(blast[:, :, 0, :, None]) + jnp.einsum(
            'bhsk,bhsv->bhkv', ki * jnp.exp(blast - bcum), vi)
        return state, o_inter + o_intra

    s_fin, o = lax.scan(step, s0, (chunks(q), chunks(k), chunks(v), chunks(log_f)))
    return o.transpose(1, 0, 3, 2, 4).reshape(b_, t_, h_, dv), s_fin


def hgrn2_scan(q, v, z, lb, s0, reverse):
    f = lb + (1.0 - lb) * jax.nn.sigmoid(z.astype(jnp.float32))
    k = split_heads(1.0 - f, HG_HEADS)
    log_f = split_heads(jnp.log(f), HG_HEADS)
    if reverse:
        q, k, v, log_f = jnp.flip(q, 1), jnp.flip(k, 1), jnp.flip(v, 1), jnp.flip(log_f, 1)
    o, s = gla_chunkwise(q, k, v, log_f, s0)
    if reverse:
        o = jnp.flip(o, 1)
    return o, s


def diff_softmax_attend(q, keys, vals, lam):
    s = jnp.einsum('bqhmd,bkhmd->bhmqk', q, keys).astype(jnp.float32) * (DA_QK ** -0.5)
    p = jax.nn.softmax(s, axis=-1)
    w = p[:, :, 0] - lam * p[:, :, 1]
    return jnp.einsum('bhqk,bkhv->bqhv', w.astype(vals.dtype), vals)


def fourier_mix(u):
    uh = split_heads(u, FT_GROUPS).astype(jnp.float32)
    y = jnp.fft.fft2(uh, axes=(1, 3), norm="ortho").real
    return y.reshape(u.shape).astype(u.dtype)


def mixer(hl, hc, w_in, w_out, lb_f, lb_b, hg_onorm, da_lam, da_subln, lam_init, cos, sin, with_ctx):
    pl = split_projection(hl @ w_in)
    pc = split_projection(hc @ w_in)
    bsz, t_lat = hl.shape[0], hl.shape[1]

    s0 = jnp.zeros((bsz, HG_HEADS, HG_DIM, HG_DIM), jnp.float32)
    q_c, v_c = split_heads(jax.nn.silu(pc[0]), HG_HEADS), split_heads(pc[1], HG_HEADS)
    q_l, v_l = split_heads(jax.nn.silu(pl[0]), HG_HEADS), split_heads(pl[1], HG_HEADS)
    oc_f, st_f = hgrn2_scan(q_c, v_c, pc[3], lb_f, s0, False)
    oc_b, st_b = hgrn2_scan(q_c, v_c, pc[4], lb_b, s0, True)
    ol_f, _ = hgrn2_scan(q_l, v_l, pl[3], lb_f, st_f, False)
    ol_b, _ = hgrn2_scan(q_l, v_l, pl[4], lb_b, st_b, True)

    def hgrn_out(o, g):
        return (rms_norm(o, hg_onorm).reshape(g.shape) * jax.nn.silu(g)).astype(g.dtype)

    def qk_heads(a):
        return a.reshape(a.shape[:-1] + (DA_HEADS, 2, DA_QK))

    lam_f = da_lam.astype(jnp.float32)
    lam = jnp.exp(jnp.sum(lam_f[0] * lam_f[1])) - jnp.exp(jnp.sum(lam_f[2] * lam_f[3])) + lam_init
    dq_l, dk_l = apply_rope(qk_heads(pl[5]), cos, sin), apply_rope(qk_heads(pl[6]), cos, sin)
    dv_l = split_heads(pl[7], DA_HEADS)
    dq_c, dk_c, dv_c = qk_heads(pc[5]), qk_heads(pc[6]), split_heads(pc[7], DA_HEADS)
    keys = jnp.concatenate([dk_l, dk_c], axis=1)
    vals = jnp.concatenate([dv_l, dv_c], axis=1)
    nb = t_lat // Q_BLOCK
    qb = dq_l.reshape((bsz, nb, Q_BLOCK) + dq_l.shape[2:]).transpose(1, 0, 2, 3, 4, 5)
    od_l = lax.map(lambda qq: diff_softmax_attend(qq, keys, vals, lam), qb)
    od_l = od_l.transpose(1, 0, 2, 3, 4).reshape(bsz, t_lat, DA_HEADS, DA_V)

    def diff_out(o):
        return (rms_norm(o, da_subln) * (1.0 - lam_init)).reshape(o.shape[:2] + (DA_WIDTH,))

    yl = jnp.concatenate([hgrn_out(ol_f + ol_b, pl[2]), diff_out(od_l).astype(hl.dtype),
                          fourier_mix(pl[8])], axis=-1) @ w_out
    if not with_ctx:
        return yl, None
    od_c = diff_softmax_attend(dq_c, dk_c, dv_c, lam)
    yc = jnp.concatenate([hgrn_out(oc_f + oc_b, pc[2]), diff_out(od_c).astype(hc.dtype),
                          fourier_mix(pc[8])], axis=-1) @ w_out
    return yl, yc


def swiglu(h, w_i, w_o):
    g, u = jnp.split(h @ w_i, 2, axis=-1)
    return (jax.nn.silu(g) * u) @ w_o


def setup_inputs(seed: int = 0) -> dict:
    key = jax.random.key(seed)
    ks = jax.random.split(key, 15)
    f32 = jnp.float32
    nrm = lambda k, s: jax.random.normal(k, s, f32)
    return {
        "x": nrm(ks[0], (BATCH, SEQ, D_MODEL)),
        "c": nrm(ks[1], (BATCH, D_MODEL)),
        "ctx": nrm(ks[2], (BATCH, CTX_LEN, D_MODEL)),
        "c_ctx": nrm(ks[3], (D_MODEL,)),
        "w_mod": nrm(ks[4], (DEPTH, D_MODEL, N_MOD * D_MODEL)) * (0.5 * D_MODEL ** -0.5),
        "b_mod": nrm(ks[5], (DEPTH, N_MOD * D_MODEL)) * 0.01,
        "norm_g": 1.0 + 0.02 * nrm(ks[6], (DEPTH, 4, D_MODEL)),
        "w_in": nrm(ks[7], (DEPTH, D_MODEL, IN_WIDTH)) * D_MODEL ** -0.5,
        "w_out": nrm(ks[8], (DEPTH, MIX_WIDTH, D_MODEL)) * MIX_WIDTH ** -0.5,
        "hg_lb_logits": 0.1 * nrm(ks[9], (2, DEPTH, HG_WIDTH)),
        "hg_onorm": 1.0 + 0.02 * nrm(ks[10], (DEPTH, HG_DIM)),
        "da_lambda": 0.1 * nrm(ks[11], (DEPTH, 4, DA_QK)),
        "da_subln": 1.0 + 0.02 * nrm(ks[12], (DEPTH, DA_V)),
        "w_ffn_in": nrm(ks[13], (DEPTH, D_MODEL, 2 * FF_HIDDEN)) * D_MODEL ** -0.5,
        "w_ffn_out": nrm(ks[14], (DEPTH, FF_HIDDEN, D_MODEL)) * FF_HIDDEN ** -0.5,
    }


def reference(x, c, ctx, c_ctx, w_mod, b_mod, norm_g, w_in, w_out, hg_lb_logits, hg_onorm,
              da_lambda, da_subln, w_ffn_in, w_ffn_out):
    cos, sin = axial_rope(x.shape[1])
    lb_p = jax.nn.softmax(hg_lb_logits.astype(jnp.float32), axis=1)
    lower_bounds = jnp.cumsum(lb_p, axis=1) - lb_p[:, :1]
    xl, xc = x, ctx
    for l in range(DEPTH):
        with_ctx = l < DEPTH - 1
        lam_init = 0.8 - 0.6 * math.exp(-0.3 * l)
        ml = modulation(c[:, None, :], w_mod[l], b_mod[l])
        mc = modulation(c_ctx[None, None, :], w_mod[l], b_mod[l])
        g = norm_g[l]
        hl = adaln(xl, g[0], ml[0], ml[1])
        hc = adaln(xc, g[0], mc[0], mc[1])
        yl, yc = mixer(hl, hc, w_in[l], w_out[l], lower_bounds[0, l], lower_bounds[1, l], hg_onorm[l],
                       da_lambda[l], da_subln[l], lam_init, cos, sin, with_ctx)
        xl = xl + ml[2] * rms_norm(yl, g[1])
        xl = xl + ml[5] * rms_norm(swiglu(adaln(xl, g[2], ml[3], ml[4]), w_ffn_in[l], w_ffn_out[l]), g[3])
        if with_ctx:
            xc = xc + mc[2] * rms_norm(yc, g[1])
            xc = xc + mc[5] * rms_norm(swiglu(adaln(xc, g[2], mc[3], mc[4]), w_ffn_in[l], w_ffn_out[l]), g[3])
    return xl
```

```python
import math
from contextlib import ExitStack

import numpy as np
import ml_dtypes
import concourse.bass as bass
import concourse.mybir as mybir
from concourse.bass_utils import run_bass_kernel_spmd

F32 = mybir.dt.float32
BF16 = mybir.dt.bfloat16
AF = mybir.ActivationFunctionType
ALU = mybir.AluOpType
AX = mybir.AxisListType

D = 1024
T_LAT = 4096
T_CTX = 256
N = T_LAT + T_CTX
NT = N // 128
DEPTH = 4
HGW = 512
INW = 3584
INX = INW + 512
FFH = 2816
EPS = 1e-6
CH = 32
NCH = N // CH
SEG = 512
LAM_INIT = [0.8 - 0.6 * math.exp(-0.3 * l) for l in range(DEPTH)]
ATT_SCALE = 32 ** -0.5

NDS = 24
NDQ = {"sp": 12, "pool": 6, "act": 2}


class T:
    __slots__ = ("w", "r")

    def __init__(self):
        self.w = None
        self.r = {}


class Buf:
    def __init__(self, t):
        self.t = t
        self.h = T()

    def __getitem__(self, k):
        return self.t[k]


class Ring:
    def __init__(self, bufs):
        self.b = bufs
        self.i = 0

    def next(self):
        b = self.b[self.i]
        self.i = (self.i + 1) % len(self.b)
        return b


class G:
    def __init__(self, nc, es):
        self.nc = nc
        self.E = {"pe": nc.tensor, "act": nc.scalar, "dve": nc.vector, "pool": nc.gpsimd, "sp": nc.sync}
        self.sem = {e: es.enter_context(nc.semaphore("c_" + e)) for e in ("pe", "act", "dve", "pool")}
        self.cnt = dict.fromkeys(self.sem, 0)
        self.seen = {e: {} for e in self.E}
        self.dsem = {q: [es.enter_context(nc.semaphore(f"d{q}{i}")) for i in range(NDQ[q])] for q in ("sp", "pool", "act")}
        self.dval = {q: [0] * NDQ[q] for q in self.dsem}
        self.dnext = dict.fromkeys(self.dsem, 0)
        self.nins = 0

    def _wait(self, e, deps):
        need = {}
        for (s, v) in deps:
            if need.get(s, 0) < v:
                need[s] = v
        seen = self.seen[e]
        own = self.sem.get(e)
        for s, v in need.items():
            if seen.get(s, 0) >= v:
                continue
            if e == "pe" and s is own:
                continue
            self.E[e].wait_ge(s, v)
            seen[s] = v

    @staticmethod
    def _deps(r, w):
        deps = []
        for t in r:
            if t.w:
                deps.append(t.w)
        for t in w:
            if t.w:
                deps.append(t.w)
            deps.extend(t.r.items())
        return deps

    @staticmethod
    def _mark(r, w, s, v):
        for t in r:
            if t.r.get(s, 0) < v:
                t.r[s] = v
        for t in w:
            t.w = (s, v)
            t.r = {}

    def op(self, e, fn, r=(), w=(), inc=True):
        r = [x.h if isinstance(x, Buf) else x for x in r]
        w = [x.h if isinstance(x, Buf) else x for x in w]
        self._wait(e, self._deps(r, w))
        ins = fn(self.E[e])
        self.nins += 1
        s = self.sem[e]
        if inc:
            self.cnt[e] += 1
            v = self.cnt[e]
            ins.then_inc(s, 1)
        else:
            v = self.cnt[e] + 1
        self._mark(r, w, s, v)
        return ins

    def dma(self, q, out, in_, r=(), w=(), **kw):
        r = [x.h if isinstance(x, Buf) else x for x in r]
        w = [x.h if isinstance(x, Buf) else x for x in w]
        i = self.dnext[q]
        self.dnext[q] = (i + 1) % NDQ[q]
        s = self.dsem[q][i]
        deps = self._deps(r, w)
        if self.dval[q][i]:
            deps.append((s, self.dval[q][i]))
        self._wait(q, deps)
        ins = self.E[q].dma_start(out=out, in_=in_, **kw)
        self.nins += 1
        self.dval[q][i] += 16
        v = self.dval[q][i]
        ins.then_inc(s, 16)
        self._mark(r, w, s, v)
        return ins

    def barrier(self, engines=("pe", "act", "dve", "pool", "sp")):
        deps = [(self.sem[x], self.cnt[x]) for x in self.sem if self.cnt[x]]
        for q in self.dsem:
            for i in range(NDQ[q]):
                if self.dval[q][i]:
                    deps.append((self.dsem[q][i], self.dval[q][i]))
        for e in engines:
            self._wait(e, deps)

    def mm(self, out, lhsT, rhs, start, stop, r=(), w=(), inc=True, **kw):
        return self.op("pe", lambda E: E.matmul(out, lhsT=lhsT, rhs=rhs, start=start, stop=stop, **kw), r, w, inc)

    def act(self, out, in_, func, r=(), w=(), e="act", **kw):
        return self.op(e, lambda E: E.activation(out=out, in_=in_, func=func, **kw), r, w)

    def tt(self, out, in0, in1, op, r=(), w=(), e="dve"):
        return self.op(e, lambda E: E.tensor_tensor(out=out, in0=in0, in1=in1, op=op), r, w)

    def ts(self, out, in0, s1, s2, op0, op1=None, r=(), w=(), e="dve"):
        if op1 is None:
            return self.op(e, lambda E: E.tensor_scalar(out=out, in0=in0, scalar1=s1, scalar2=None, op0=op0), r, w)
        return self.op(e, lambda E: E.tensor_scalar(out=out, in0=in0, scalar1=s1, scalar2=s2, op0=op0, op1=op1), r, w)

    def stt(self, out, in0, scalar, in1, op0, op1, r=(), w=(), e="dve"):
        return self.op(e, lambda E: E.scalar_tensor_tensor(out=out, in0=in0, scalar=scalar, in1=in1, op0=op0, op1=op1), r, w)

    def cp(self, out, in_, r=(), w=(), e="dve"):
        if e == "act":
            return self.op(e, lambda E: E.copy(out=out, in_=in_), r, w)
        return self.op(e, lambda E: E.tensor_copy(out=out, in_=in_), r, w)


def build(n_layers=DEPTH, stop_after=None, debug=False):
    nc = bass.Bass("TRN2", target_bir_lowering=False)
    skind = "ExternalOutput" if debug else "Internal"
    LD = n_layers if debug else DEPTH

    def din(name, shape, dt=F32):
        return nc.dram_tensor(name, list(shape), dt, kind="ExternalInput").ap()

    def dscr(name, shape, dt=F32):
        return nc.dram_tensor(name, list(shape), dt, kind=skind).ap()

    xin = din("xin", [N, D])
    cond = din("cond", [2, D])
    w_mod = din("w_mod", [LD, D, 6 * D])
    b_mod = din("b_mod", [LD, 6 * D])
    norm_g = din("norm_g", [LD, 4 * D])
    w_in = din("w_in", [LD, D, INX])
    w_out = din("w_out", [LD, D, D])
    lb_logits = din("lb_logits", [8, HGW])
    hg_onorm = din("hg_onorm", [1, DEPTH * 128])
    da_lambda = din("da_lambda", [1, DEPTH * 4 * 32])
    da_subln = din("da_subln", [1, DEPTH * 64])
    w_ffn_in = din("w_ffn_in", [LD, D, 2 * FFH])
    w_ffn_out = din("w_ffn_out", [LD, FFH, D])
    ropeC = din("ropeC", [128, N])
    ropeS = din("ropeS", [128, N])
    dftC = din("dftC", [32, 128, 32 * 128], BF16)
    dftS = din("dftS", [32, 128, 32 * 128], BF16)
    dftCc = din("dftCc", [2, 128, 2 * 128], BF16)
    dftSc = din("dftSc", [2, 128, 2 * 128], BF16)
    chC = din("chC", [128, 128], BF16)
    chS = din("chS", [128, 128], BF16)
    ident_in = din("ident", [128, 128], BF16)
    identf_in = din("identf", [128, 128], F32)
    maskf_in = din("maskf", [32, 32], F32)
    maskb_in = din("maskb", [32, 32], F32)
    yout = nc.dram_tensor("yout", [T_LAT, D], F32, kind="ExternalOutput").ap()

    XS = dscr("XS", [N, D])
    X1 = dscr("X1", [N, D])
    MODV = dscr("MODV", [DEPTH * 2 * 6, D])
    QHT = dscr("QHT", [4, 128, N])
    ZFT = dscr("ZFT", [4, 128, N])
    ZBT = dscr("ZBT", [4, 128, N])
    VH = dscr("VH", [N, HGW], BF16)
    GH = dscr("GH", [N, HGW])
    QRT = dscr("QRT", [2, 128, N], BF16)
    KRT = dscr("KRT", [2, 128, N], BF16)
    VD = dscr("VD", [N, 256], BF16)
    FAB = dscr("FAB", [N, 512], BF16)
    OF = dscr("OF", [N, HGW])
    OB = dscr("OB", [N, HGW])
    YC = dscr("YC", [N, D], BF16)
    H2T = dscr("H2T", [8, 128, N], BF16)
    QRTM = dscr("QRTM", [2, 4, 128, N], BF16)

    dh = {}

    def H(name, key=0):
        k = (name, key)
        if k not in dh:
            dh[k] = T()
        return dh[k]

    def HA(name, keys):
        return [H(name, k) for k in keys]

    es = ExitStack()
    with es:
        g = G(nc, es)
        psall = es.enter_context(nc.psum_tensor("psall", [128, 8, 512], F32))
        PSH = [T() for _ in range(8)]

        uid = [0]

        def sb(st, name, shape, dt=F32):
            uid[0] += 1
            return Buf(st.enter_context(nc.sbuf_tensor(f"s{uid[0]}_{name}", list(shape), dt)))

        ident = sb(es, "ident", [128, 128], BF16)
        identf = sb(es, "identf", [128, 128], F32)
        LB = sb(es, "LB", [128, 4, 8])
        OML = sb(es, "OML", [128, 4, 8])
        LAM = sb(es, "LAM", [128, DEPTH])
        NLAM = sb(es, "NLAM", [128, DEPTH])
        ONORM = sb(es, "ONORM", [128, DEPTH * 128])
        SUBLN = sb(es, "SUBLN", [128, DEPTH * 64])
        g.dma("sp", ident[:, :], ident_in[:, :], w=[ident])
        g.dma("sp", identf[:, :], identf_in[:, :], w=[identf])
        g.dma("sp", ONORM[:, :], hg_onorm[0:1, :].partition_broadcast(128), w=[ONORM])
        g.dma("sp", SUBLN[:, :], da_subln[0:1, :].partition_broadcast(128), w=[SUBLN])
        with ExitStack() as stz:
            zt = sb(stz, "zt", [128, N], BF16)
            g.op("pool", lambda E: E.memset(zt[:, :], 0.0), w=[zt])
            for ti in range(2):
                for gq in range(4):
                    g.dma("sp", QRTM[ti, gq, :, :], zt[:, :], r=[zt], w=[H("QRTMz")])
            g.barrier()

        with ExitStack() as st:
            lg = sb(st, "lg", [8, HGW])
            ex = sb(st, "ex", [128, 4, 8])
            sm = sb(st, "sm", [128, 4, 2])
            dl = sb(st, "dl", [128, DEPTH * 4 * 32])
            pr = sb(st, "pr", [128, DEPTH * 2 * 32])
            pe2 = sb(st, "pe2", [128, DEPTH * 2])
            g.dma("sp", lg[:, :], lb_logits[:, :], w=[lg])
            for h in range(4):
                g.mm(psall[:, 0, h * 8:(h + 1) * 8], lhsT=lg[0:8, h * 128:(h + 1) * 128], rhs=identf[0:8, 0:8],
                     start=True, stop=True, r=[lg, identf], w=[PSH[0]])
            g.act(ex[:, :, :], psall[:, 0, 0:32].rearrange("p (h x) -> p h x", h=4), AF.Exp, r=[PSH[0]], w=[ex])
            exv = ex[:, :, :].rearrange("p h (d l) -> p h d l", d=2)
            g.op("dve", lambda E: E.tensor_reduce(out=sm[:, :, :], in_=exv, axis=AX.X, op=ALU.add), r=[ex], w=[sm])
            g.op("dve", lambda E: E.reciprocal(out=sm[:, :, :], in_=sm[:, :, :]), r=[sm], w=[sm])
            g.tt(exv, exv, sm[:, :, :].unsqueeze(3).to_broadcast([128, 4, 2, 4]), ALU.mult, r=[ex, sm], w=[ex])
            lbv = LB[:, :, :].rearrange("p h (d l) -> p h d l", d=2)
            g.op("dve", lambda E: E.memset(lbv[:, :, :, 0:1], 0.0), w=[LB])
            for l in range(1, 4):
                g.tt(lbv[:, :, :, l:l + 1], lbv[:, :, :, l - 1:l], exv[:, :, :, l:l + 1], ALU.add, r=[ex, LB], w=[LB])
            g.ts(OML[:, :, :], LB[:, :, :], -1.0, 1.0, ALU.mult, ALU.add, r=[LB], w=[OML])
            g.dma("sp", dl[:, :], da_lambda[0:1, :].partition_broadcast(128), w=[dl])
            dlv = dl[:, :].rearrange("p (l a b x) -> p l a b x", l=DEPTH, a=2, b=2)
            prv = pr[:, :].rearrange("p (l a x) -> p l a x", l=DEPTH, a=2)
            g.tt(prv, dlv[:, :, :, 0, :], dlv[:, :, :, 1, :], ALU.mult, r=[dl], w=[pr])
            pe2v = pe2[:, :].rearrange("p (l a) -> p l a", a=2)
            g.op("dve", lambda E: E.tensor_reduce(out=pe2v, in_=prv, axis=AX.X, op=ALU.add), r=[pr], w=[pe2])
            g.act(pe2[:, :], pe2[:, :], AF.Exp, r=[pe2], w=[pe2])
            g.tt(LAM[:, :], pe2v[:, :, 0], pe2v[:, :, 1], ALU.subtract, r=[pe2], w=[LAM])
            for l in range(DEPTH):
                g.ts(LAM[:, l:l + 1], LAM[:, l:l + 1], float(LAM_INIT[l]), None, ALU.add, r=[LAM], w=[LAM])
            g.ts(NLAM[:, :], LAM[:, :], -1.0, None, ALU.mult, r=[LAM], w=[NLAM])
            for l in range(DEPTH):
                g.ts(SUBLN[:, l * 64:(l + 1) * 64], SUBLN[:, l * 64:(l + 1) * 64], float(1.0 - LAM_INIT[l]), None,
                     ALU.mult, r=[SUBLN], w=[SUBLN])

            cT = sb(st, "cT", [128, 8, 2])
            cstage = sb(st, "cstage", [2, D])
            g.dma("sp", cstage[:, :], cond[:, :], w=[cstage])
            g.act(cstage[:, :], cstage[:, :], AF.Silu, r=[cstage], w=[cstage])
            for kc in range(8):
                g.mm(psall[:, 1, kc * 2:(kc + 1) * 2], lhsT=cstage[0:2, kc * 128:(kc + 1) * 128], rhs=identf[0:2, 0:2],
                     start=True, stop=True, r=[cstage, identf], w=[PSH[1]])
            g.cp(cT[:, :, :], psall[:, 1, 0:16].rearrange("p (k r) -> p k r", r=2), r=[PSH[1]], w=[cT])
            wring = Ring([sb(st, f"wm{i}", [128, 8, 512]) for i in range(2)])
            modsb = sb(st, "modsb", [2, 6 * D])
            bmsb = sb(st, "bmsb", [2, 6 * D])
            ngsb = sb(st, "ngsb", [2, 4 * D])
            vst = sb(st, "vst", [2, 6 * D])
            pi = 2
            for l in range(n_layers):
                g.dma("sp", bmsb[:, :], b_mod[l:l + 1, :].partition_broadcast(2), w=[bmsb])
                g.dma("sp", ngsb[:, :], norm_g[l:l + 1, :].partition_broadcast(2), w=[ngsb])
                wv = w_mod[l].rearrange("(kc p) n -> p kc n", p=128)
                for j in range(12):
                    wb = wring.next()
                    g.dma("sp", wb[:, :, :], wv[:, :, j * 512:(j + 1) * 512], w=[wb])
                    pb = 2 + (pi % 2)
                    pi += 1
                    for kc in range(8):
                        g.mm(psall[0:2, pb, :], lhsT=cT[:, kc, :], rhs=wb[:, kc, :], start=(kc == 0), stop=(kc == 7),
                             r=[cT, wb], w=[PSH[pb]], inc=(kc == 7))
                    g.tt(modsb[:, j * 512:(j + 1) * 512], psall[0:2, pb, :], bmsb[:, j * 512:(j + 1) * 512], ALU.add,
                         r=[PSH[pb], bmsb], w=[modsb])
                m = lambda i: modsb[:, i * D:(i + 1) * D]
                ng = lambda i: ngsb[:, i * D:(i + 1) * D]
                vs = lambda i: vst[:, i * D:(i + 1) * D]
                g.stt(vs(0), m(1), 1.0, ng(0), ALU.add, ALU.mult, r=[modsb, ngsb], w=[vst])
                g.cp(vs(1), m(0), r=[modsb], w=[vst])
                g.tt(vs(2), m(2), ng(1), ALU.mult, r=[modsb, ngsb], w=[vst])
                g.stt(vs(3), m(4), 1.0, ng(2), ALU.add, ALU.mult, r=[modsb, ngsb], w=[vst])
                g.cp(vs(4), m(3), r=[modsb], w=[vst])
                g.tt(vs(5), m(5), ng(3), ALU.mult, r=[modsb, ngsb], w=[vst])
                for rr in range(2):
                    g.dma("pool", MODV[(l * 2 + rr) * 6:(l * 2 + rr) * 6 + 6, :].rearrange("(o i) d -> o i d", o=1),
                          vst[rr:rr + 1, :].rearrange("p (i d) -> p i d", i=6), r=[vst], w=[H("MODV", l)])
            g.barrier()
        if stop_after == "pro":
            return _finish(nc, g, xin, yout, es)

        def modrow(l, rr, i):
            k = (l * 2 + rr) * 6 + i
            return MODV[k:k + 1, :].partition_broadcast(128)

        def rstd_from_ss(ss, n, r, w):
            g.ts(ss, ss, 1.0 / n, EPS, ALU.mult, ALU.add, r=r, w=w)
            g.act(ss, ss, AF.Sqrt, r=w, w=w)
            g.op("dve", lambda E: E.reciprocal(out=ss, in_=ss), r=w, w=w)

        TOKBLKS = [(i * 512, 512, 0) for i in range(8)] + [(T_LAT, 256, 1)]

        for l in range(n_layers):
            xsrc = xin if l == 0 else XS
            with ExitStack() as st:
                wb = sb(st, "wb", [128, 8, INX], BF16)
                wbh = [T() for _ in range(8)]
                wv = w_in[l].rearrange("(kc p) n -> p kc n", p=128)
                for kc in range(8):
                    for c4 in range(0, INX, 1024):
                        g.dma("pool", wb[:, kc, c4:c4 + 1024], wv[:, kc, c4:c4 + 1024], w=[wbh[kc]])
                G1 = [sb(st, f"G1_{rr}", [128, D]) for rr in range(2)]
                S1 = [sb(st, f"S1_{rr}", [128, D]) for rr in range(2)]
                for rr in range(2):
                    g.dma("sp", G1[rr][:, :], modrow(l, rr, 0), r=[H("MODV", l)], w=[G1[rr]])
                    g.dma("sp", S1[rr][:, :], modrow(l, rr, 1), r=[H("MODV", l)], w=[S1[rr]])
                chCs = sb(st, "chCs", [128, 128], BF16)
                chSs = sb(st, "chSs", [128, 128], BF16)
                g.dma("sp", chCs[:, :], chC[:, :], w=[chCs])
                g.dma("sp", chSs[:, :], chS[:, :], w=[chSs])
                xring = Ring([sb(st, f"xa{i}", [128, D]) for i in range(2)])
                sqj = sb(st, "sqj", [128, D])
                ssr = Ring([sb(st, f"ss{i}", [128, 1]) for i in range(2)])
                t1r = Ring([sb(st, f"t1_{i}", [128, D]) for i in range(2)])
                hbr = Ring([sb(st, f"hb{i}", [128, D], BF16) for i in range(2)])
                hTr = Ring([sb(st, f"hT{i}", [128, 8, 512], BF16) for i in range(2)])
                rcr = Ring([sb(st, f"rc{i}", [128, 512]) for i in range(2)])
                rsr = Ring([sb(st, f"rs{i}", [128, 512]) for i in range(2)])
                stf = Ring([sb(st, f"stf{i}", [128, 512]) for i in range(4)])
                stb = Ring([sb(st, f"stb{i}", [128, 512], BF16) for i in range(4)])
                rt1 = Ring([sb(st, f"rt1_{i}", [128, 512]) for i in range(2)])
                rt2 = Ring([sb(st, f"rt2_{i}", [128, 512]) for i in range(2)])
                uTr = Ring([sb(st, f"uT{i}", [128, 512], BF16) for i in range(4)])
                fmb = Ring([2, 3, 4, 5])
                tmb = Ring([6, 7])
                if stop_after == ("Aw", l):
                    return _finish(nc, g, xin, yout, es)
                for (t0, W, rr) in TOKBLKS:
                    ntl = W // 128
                    hT = hTr.next()
                    for i in range(ntl):
                        r0 = t0 + i * 128
                        xa = xring.next()
                        g.dma("sp", xa[:, :], xsrc[r0:r0 + 128, :], r=[H("XS", r0 // 128)], w=[xa])
                        ss = ssr.next()
                        g.act(sqj[:, :], xa[:, :], AF.Square, r=[xa], w=[sqj])
                        g.op("dve", lambda E: E.reduce_sum(out=ss[:, :], in_=sqj[:, :], axis=AX.X), r=[sqj], w=[ss])
                        rstd_from_ss(ss[:, :], D, [ss], [ss])
                        t1 = t1r.next()
                        g.stt(t1[:, :], xa[:, :], ss[:, 0:1], G1[rr][:, :], ALU.mult, ALU.mult, r=[xa, ss, G1[rr]], w=[t1])
                        hb = hbr.next()
                        g.tt(hb[:, :], t1[:, :], S1[rr][:, :], ALU.add, r=[t1, S1[rr]], w=[hb], e="pool")
                        for half in range(2):
                            for k4 in range(4):
                                kc = half * 4 + k4
                                g.mm(psall[:, half, k4 * 128:(k4 + 1) * 128], lhsT=hb[:, kc * 128:(kc + 1) * 128],
                                     rhs=ident[:, :], start=True, stop=True, r=[hb, ident], w=[PSH[half]], inc=(k4 == 3))
                            g.cp(hT[:, half * 4:half * 4 + 4, i * 128:(i + 1) * 128],
                                 psall[:, half, :].rearrange("p (k t) -> p k t", k=4), r=[PSH[half]], w=[hT],
                                 e=("act" if half else "dve"))
                    if stop_after == ("Ah", l):
                        return _finish(nc, g, xin, yout, es)
                    rc = rcr.next()
                    rs = rsr.next()
                    g.dma("sp", rc[:, 0:W], ropeC[:, t0:t0 + W], w=[rc])
                    g.dma("sp", rs[:, 0:W], ropeS[:, t0:t0 + W], w=[rs])

                    def fm(c0):
                        pb = fmb.next()
                        for kc in range(8):
                            g.mm(psall[:, pb, 0:W], lhsT=wb[:, kc, c0:c0 + 128], rhs=hT[:, kc, 0:W], start=(kc == 0),
                                 stop=(kc == 7), r=[wbh[kc], hT], w=[PSH[pb]], inc=(kc == 7))
                        return pb

                    for (c00, dst, fn) in ((0, QHT, AF.Silu), (1536, ZFT, AF.Copy), (2048, ZBT, AF.Copy)):
                        for h in range(4):
                            pb = fm(c00 + h * 128)
                            sf = stf.next()
                            if fn == AF.Silu:
                                g.act(sf[:, 0:W], psall[:, pb, 0:W], fn, r=[PSH[pb]], w=[sf])
                            else:
                                g.cp(sf[:, 0:W], psall[:, pb, 0:W], r=[PSH[pb]], w=[sf], e="dve")
                            g.dma("pool", dst[h, :, t0:t0 + W], sf[:, 0:W], r=[sf], w=[H(dst.name + str(h), t0)])
                    if stop_after == ("Af", l):
                        return _finish(nc, g, xin, yout, es)
                    for (c00, csw, dst) in ((2560, INW, QRT), (2816, INW + 256, KRT)):
                        for ti in range(2):
                            pa = fm(c00 + ti * 128)
                            ps_ = fm(csw + ti * 128)
                            a1 = rt1.next()
                            a2 = rt2.next()
                            g.tt(a1[:, 0:W], psall[:, pa, 0:W], rc[:, 0:W], ALU.mult, r=[PSH[pa], rc], w=[a1])
                            g.tt(a2[:, 0:W], psall[:, ps_, 0:W], rs[:, 0:W], ALU.mult, r=[PSH[ps_], rs], w=[a2])
                            sbb = stb.next()
                            g.tt(sbb[:, 0:W], a1[:, 0:W], a2[:, 0:W], ALU.add, r=[a1, a2], w=[sbb], e="pool")
                            g.dma("pool", dst[ti, :, t0:t0 + W], sbb[:, 0:W], r=[sbb], w=[H(dst.name + str(ti), t0)])
                            if dst is QRT:
                                for gq in range(4):
                                    g.dma("pool", QRTM[ti, gq, gq * 32:(gq + 1) * 32, t0:t0 + W], sbb[gq * 32:(gq + 1) * 32, 0:W],
                                          r=[sbb, H("QRTMz")], w=[H("QRTM" + str(ti) + str(gq), t0)])
                    if stop_after == ("Ar", l):
                        return _finish(nc, g, xin, yout, es)
                    uts = []
                    for ti in range(2):
                        pb = fm(3328 + ti * 128)
                        uT = uTr.next()
                        g.cp(uT[:, 0:W], psall[:, pb, 0:W], r=[PSH[pb]], w=[uT], e="act")
                        uts.append(uT)
                    for i in range(ntl):
                        pb = tmb.next()
                        for ab, tab in enumerate((chCs, chSs)):
                            for ti in range(2):
                                g.mm(psall[:, pb, ab * 256 + ti * 128: ab * 256 + (ti + 1) * 128],
                                     lhsT=uts[ti][:, i * 128:(i + 1) * 128], rhs=tab[:, :], start=True, stop=True,
                                     r=[uts[ti], tab], w=[PSH[pb]], inc=(ab == 1 and ti == 1))
                        sbb = stb.next()
                        g.cp(sbb[:, :], psall[:, pb, :], r=[PSH[pb]], w=[sbb], e="dve")
                        r0 = t0 + i * 128
                        g.dma("pool", FAB[r0:r0 + 128, :], sbb[:, :], r=[sbb], w=[H("FAB", r0 // 128)])
                    if stop_after == ("At", l):
                        return _finish(nc, g, xin, yout, es)
                    for i in range(ntl):
                        r0 = t0 + i * 128

                        def tm(c0, ncol):
                            pb = tmb.next()
                            for kc in range(8):
                                g.mm(psall[:, pb, 0:ncol], lhsT=hT[:, kc, i * 128:(i + 1) * 128], rhs=wb[:, kc, c0:c0 + ncol],
                                     start=(kc == 0), stop=(kc == 7), r=[wbh[kc], hT], w=[PSH[pb]], inc=(kc == 7))
                            return pb
                        pb = tm(512, 512)
                        sbb = stb.next()
                        g.cp(sbb[:, :], psall[:, pb, :], r=[PSH[pb]], w=[sbb], e="dve")
                        g.dma("pool", VH[r0:r0 + 128, :], sbb[:, :], r=[sbb], w=[H("VH", r0 // 128)])
                        pb = tm(1024, 512)
                        sf = stf.next()
                        g.act(sf[:, :], psall[:, pb, :], AF.Silu, r=[PSH[pb]], w=[sf])
                        g.dma("pool", GH[r0:r0 + 128, :], sf[:, :], r=[sf], w=[H("GH", r0 // 128)])
                        pb = tm(3072, 256)
                        sbb = stb.next()
                        g.cp(sbb[:, 0:256], psall[:, pb, 0:256], r=[PSH[pb]], w=[sbb], e="dve")
                        g.dma("pool", VD[r0:r0 + 128, :], sbb[:, 0:256], r=[sbb], w=[H("VD", r0 // 128)])
                g.barrier()
            if stop_after == ("A", l):
                return _finish(nc, g, xin, yout, es)

            mixer_phase(nc, g, l, psall, PSH, H, sb, ident, identf, LB, OML, ONORM, NLAM, SUBLN, QHT, ZFT, ZBT, VH, GH, OF, OB, YC,
                        QRT, KRT, VD, maskf_in, maskb_in, QRTM)
            if stop_after in (("H", l), ("T", l)):
                return _finish(nc, g, xin, yout, es)
            with ExitStack() as st:
                wo = sb(st, "wo", [128, 8, D], BF16)
                wov1 = w_out[l].rearrange("(kc p) n -> p kc n", p=128)
                for kc in range(8):
                    g.dma("pool", wo[:, kc, :], wov1[:, kc, :], w=[wo])
                GG1 = [sb(st, f"GG1_{rr}", [128, D]) for rr in range(2)]
                G2 = [sb(st, f"G2_{rr}", [128, D]) for rr in range(2)]
                S2 = [sb(st, f"S2_{rr}", [128, D]) for rr in range(2)]
                for rr in range(2):
                    g.dma("sp", GG1[rr][:, :], modrow(l, rr, 2), r=[H("MODV", l)], w=[GG1[rr]])
                    g.dma("sp", G2[rr][:, :], modrow(l, rr, 3), r=[H("MODV", l)], w=[G2[rr]])
                    g.dma("sp", S2[rr][:, :], modrow(l, rr, 4), r=[H("MODV", l)], w=[S2[rr]])
                ycr = Ring([sb(st, f"yc{i}", [128, D], BF16) for i in range(2)])
                xr = Ring([sb(st, f"xc{i}", [128, D]) for i in range(2)])
                ycTr = Ring([sb(st, f"ycT{i}", [128, 8, 128], BF16) for i in range(2)])
                sqj = sb(st, "sqj", [128, D])
                ssr = Ring([sb(st, f"ss{i}", [128, 1]) for i in range(4)])
                t1r = Ring([sb(st, f"t1_{i}", [128, D]) for i in range(2)])
                x1r = Ring([sb(st, f"x1_{i}", [128, D]) for i in range(2)])
                h2r = Ring([sb(st, f"h2_{i}", [128, D], BF16) for i in range(2)])
                h2Tr = Ring([sb(st, f"h2T{i}", [128, 8, 128], BF16) for i in range(2)])
                tpb = Ring([(0, 1)])
                ypb = Ring([(4, 5), (6, 7)])
                fg = fft_gen(nc, g, l, st, psall, PSH, H, sb, FAB, YC, dftC, dftS, dftCc, dftSc, (2, 3))
                next(fg)
                for tt_ in range(NT):
                    if tt_ + 1 < NT:
                        next(fg)
                    r0 = tt_ * 128
                    rr = 0 if r0 < T_LAT else 1
                    yc = ycr.next()
                    xc = xr.next()
                    g.dma("sp", yc[:, :], YC[r0:r0 + 128, :], r=[H("YCh", tt_), H("YCa", tt_), H("YCf", tt_)], w=[yc])
                    g.dma("sp", xc[:, :], xsrc[r0:r0 + 128, :], r=[H("XS", tt_)], w=[xc])
                    pbs = tpb.next()
                    ycT = ycTr.next()
                    for half in range(2):
                        pb = pbs[half]
                        for k4 in range(4):
                            kc = half * 4 + k4
                            g.mm(psall[:, pb, k4 * 128:(k4 + 1) * 128], lhsT=yc[:, kc * 128:(kc + 1) * 128], rhs=ident[:, :],
                                 start=True, stop=True, r=[yc, ident], w=[PSH[pb]], inc=(k4 == 3))
                        g.cp(ycT[:, half * 4:half * 4 + 4, :], psall[:, pb, :].rearrange("p (k t) -> p k t", k=4),
                             r=[PSH[pb]], w=[ycT], e=("act" if half else "dve"))
                    yb = ypb.next()
                    for nb in range(2):
                        for kc in range(8):
                            g.mm(psall[:, yb[nb], :], lhsT=ycT[:, kc, :], rhs=wo[:, kc, nb * 512:(nb + 1) * 512],
                                 start=(kc == 0), stop=(kc == 7), r=[ycT, wo], w=[PSH[yb[nb]]], inc=(kc == 7))
                    ypv = psall[:, yb[0]:yb[0] + 2, :]
                    ss = ssr.next()
                    g.act(sqj[:, :].rearrange("p (a b) -> p a b", a=2), ypv, AF.Square, r=[PSH[yb[0]], PSH[yb[1]]], w=[sqj])
                    g.op("dve", lambda E: E.reduce_sum(out=ss[:, :], in_=sqj[:, :], axis=AX.X), r=[sqj], w=[ss])
                    rstd_from_ss(ss[:, :], D, [ss], [ss])
                    t1 = t1r.next()
                    g.stt(t1[:, :].rearrange("p (a b) -> p a b", a=2), ypv, ss[:, 0:1],
                          GG1[rr][:, :].rearrange("p (a b) -> p a b", a=2), ALU.mult, ALU.mult,
                          r=[PSH[yb[0]], PSH[yb[1]], ss, GG1[rr]], w=[t1])
                    x1 = x1r.next()
                    g.tt(x1[:, :], t1[:, :], xc[:, :], ALU.add, r=[t1, xc], w=[x1], e="pool")
                    g.dma("pool", X1[r0:r0 + 128, :], x1[:, :], r=[x1], w=[H("X1", tt_)])
                    ss2 = ssr.next()
                    g.act(sqj[:, :], x1[:, :], AF.Square, r=[x1], w=[sqj])
                    g.op("dve", lambda E: E.reduce_sum(out=ss2[:, :], in_=sqj[:, :], axis=AX.X), r=[sqj], w=[ss2])
                    rstd_from_ss(ss2[:, :], D, [ss2], [ss2])
                    t2 = t1r.next()
                    g.stt(t2[:, :], x1[:, :], ss2[:, 0:1], G2[rr][:, :], ALU.mult, ALU.mult, r=[x1, ss2, G2[rr]], w=[t2])
                    h2 = h2r.next()
                    g.tt(h2[:, :], t2[:, :], S2[rr][:, :], ALU.add, r=[t2, S2[rr]], w=[h2], e="pool")
                    pbs = tpb.next()
                    h2T = h2Tr.next()
                    for half in range(2):
                        pb = pbs[half]
                        for k4 in range(4):
                            kc = half * 4 + k4
                            g.mm(psall[:, pb, k4 * 128:(k4 + 1) * 128], lhsT=h2[:, kc * 128:(kc + 1) * 128], rhs=ident[:, :],
                                 start=True, stop=True, r=[h2, ident], w=[PSH[pb]], inc=(k4 == 3))
                        g.cp(h2T[:, half * 4:half * 4 + 4, :], psall[:, pb, :].rearrange("p (k t) -> p k t", k=4),
                             r=[PSH[pb]], w=[h2T], e=("act" if half else "dve"))
                    g.dma("pool", H2T[:, :, r0:r0 + 128].rearrange("k p t -> p k t"), h2T[:, :, :], r=[h2T], w=[H("H2T", tt_)])
                g.barrier()
            if stop_after == ("C1", l):
                return _finish(nc, g, xin, yout, es)

            with ExitStack() as st:
                wi = sb(st, "wi", [128, 8, 2 * FFH], BF16)
                wih = [T() for _ in range(8)]
                wiv = w_ffn_in[l].rearrange("(kc p) n -> p kc n", p=128)
                for kc in range(8):
                    for c4 in range(0, 2 * FFH, 1408):
                        g.dma("pool", wi[:, kc, c4:c4 + 1408], wiv[:, kc, c4:c4 + 1408], w=[wih[kc]])
                wo2 = sb(st, "wo2", [128, 22, D], BF16)
                wo2h = [T() for _ in range(2)]
                wov = w_ffn_out[l].rearrange("(hc p) n -> p hc n", p=128)
                for j in range(2):
                    for hc in range(j * 11, (j + 1) * 11):
                        g.dma("pool", wo2[:, hc, :], wov[:, hc, :], w=[wo2h[j]])
                GG2 = [sb(st, f"GG2_{rr}", [128, D]) for rr in range(2)]
                for rr in range(2):
                    g.dma("sp", GG2[rr][:, :], modrow(l, rr, 5), r=[H("MODV", l)], w=[GG2[rr]])
                hr = Ring([sb(st, f"h2c{i}", [128, 8, 256], BF16) for i in range(2)])
                x1r = Ring([sb(st, f"x1c{i}", [128, D]) for i in range(3)])
                sgr = Ring([sb(st, f"sg{i}", [128, 256]) for i in range(2)])
                acr = Ring([sb(st, f"ac{i}", [128, 256], BF16) for i in range(3)])
                sqj = sb(st, "sqj", [128, D])
                ssr = Ring([sb(st, f"ss{i}", [128, 1]) for i in range(2)])
                t1r = Ring([sb(st, f"t1_{i}", [128, D]) for i in range(2)])
                x2r = Ring([sb(st, f"x2_{i}", [128, D]) for i in range(2)])
                gub = Ring([(4, 5), (6, 7)])
                for b in range(N // 256):
                    c0 = b * 256
                    rr = 0 if c0 < T_LAT else 1
                    hT = hr.next()
                    g.dma("sp", hT[:, :, :], H2T[:, :, c0:c0 + 256].rearrange("k p t -> p k t"),
                          r=[H("H2T", 2 * b), H("H2T", 2 * b + 1)], w=[hT])
                    for hc in range(22):
                        pg, pu = gub.next()
                        for kc in range(8):
                            g.mm(psall[:, pg, 0:256], lhsT=wi[:, kc, hc * 128:(hc + 1) * 128], rhs=hT[:, kc, :],
                                 start=(kc == 0), stop=(kc == 7), r=[wih[kc], hT], w=[PSH[pg]], inc=(kc == 7))
                        for kc in range(8):
                            g.mm(psall[:, pu, 0:256], lhsT=wi[:, kc, FFH + hc * 128:FFH + (hc + 1) * 128], rhs=hT[:, kc, :],
                                 start=(kc == 0), stop=(kc == 7), r=[wih[kc], hT], w=[PSH[pu]], inc=(kc == 7))
                        sg = sgr.next()
                        g.act(sg[:, :], psall[:, pg, 0:256], AF.Silu, r=[PSH[pg]], w=[sg])
                        ac = acr.next()
                        g.tt(ac[:, :], sg[:, :], psall[:, pu, 0:256], ALU.mult, r=[sg, PSH[pu]], w=[ac])
                        for i in range(2):
                            for nb in range(2):
                                g.mm(psall[:, i * 2 + nb, :], lhsT=ac[:, i * 128:(i + 1) * 128],
                                     rhs=wo2[:, hc, nb * 512:(nb + 1) * 512], start=(hc == 0), stop=(hc == 21),
                                     r=[ac, wo2h[hc // 11]], w=[PSH[i * 2 + nb]], inc=(i == 1 and nb == 1),
                                     skip_group_check=True)
                    for i in range(2):
                        tt_ = 2 * b + i
                        r0 = tt_ * 128
                        x1 = x1r.next()
                        g.dma("sp", x1[:, :], X1[r0:r0 + 128, :], r=[H("X1", tt_)], w=[x1])
                        opv = psall[:, 2 * i:2 * i + 2, :]
                        ss = ssr.next()
                        g.act(sqj[:, :].rearrange("p (a b) -> p a b", a=2), opv, AF.Square,
                              r=[PSH[2 * i], PSH[2 * i + 1]], w=[sqj])
                        g.op("dve", lambda E: E.reduce_sum(out=ss[:, :], in_=sqj[:, :], axis=AX.X), r=[sqj], w=[ss])
                        rstd_from_ss(ss[:, :], D, [ss], [ss])
                        t1 = t1r.next()
                        g.stt(t1[:, :].rearrange("p (a b) -> p a b", a=2), opv, ss[:, 0:1],
                              GG2[rr][:, :].rearrange("p (a b) -> p a b", a=2), ALU.mult, ALU.mult,
                              r=[PSH[2 * i], PSH[2 * i + 1], ss, GG2[rr]], w=[t1])
                        x2 = x2r.next()
                        g.tt(x2[:, :], t1[:, :], x1[:, :], ALU.add, r=[t1, x1], w=[x2], e="pool")
                        if l == n_layers - 1:
                            if r0 < T_LAT:
                                g.dma("pool", yout[r0:r0 + 128, :], x2[:, :], r=[x2], w=[H("yout", tt_)])
                        else:
                            g.dma("pool", XS[r0:r0 + 128, :], x2[:, :], r=[x2], w=[H("XS", tt_)])
                g.barrier()
        return _finish(nc, g, xin, yout, es)


def _finish(nc, g, xin, yout, es):
    g.barrier()
    return nc


HSEG = 256
DEBUG_PROG = False


def mixer_phase(nc, g, l, psall, PSH, H, sb, ident, identf, LB, OML, ONORM, NLAM, SUBLN, QHT, ZFT, ZBT, VH, GH, OF, OB, YC,
                QRT, KRT, VD, maskf_in, maskb_in, QRTM):
    with ExitStack() as st:
        gens = [attn_gen(nc, g, l, st, psall, PSH, H, sb, NLAM, SUBLN, QRT, KRT, VD, YC, identf, QRTM),
                hgrn_gen(nc, g, l, st, psall, PSH, H, sb, ident, LB, OML, ONORM, QHT, ZFT, ZBT, VH, GH, OF, OB, YC,
                         maskf_in, maskb_in)]
        prog = [0.0 for _ in gens]
        if DEBUG_PROG:
            print('sbuf remaining before gens start', nc.sbuf_bytes_remaining)
        alive = [True for _ in gens]
        while any(alive):
            i = min((p, k) for k, p in enumerate(prog) if alive[k])[1]
            try:
                first = prog[i] == 0.0
                prog[i] = next(gens[i])
                if DEBUG_PROG and first:
                    print('gen', i, 'allocated; sbuf remaining', nc.sbuf_bytes_remaining)
            except StopIteration:
                alive[i] = False
                if DEBUG_PROG:
                    print("gen", i, "finished at prog", prog)
        g.barrier()


def hgrn_gen(nc, g, l, st, psall, PSH, H, sb, ident, LB, OML, ONORM, QHT, ZFT, ZBT, VH, GH, OF, OB, YC,
             maskf_in, maskb_in):
    SG = HSEG
    NCS = SG // CH
    lat_segs = [(i * SG, SG) for i in range(T_LAT // SG)]
    segs = {0: [(T_LAT, T_CTX)] + lat_segs, 1: [(T_LAT, T_CTX)] + lat_segs[::-1]}
    nsteps = len(segs[0])
    mask = [sb(st, "maskf", [32, 4, 32]), sb(st, "maskb", [32, 4, 32])]
    ones = sb(st, "ones", [128, SG])
    S = [sb(st, f"S{d_}", [128, 4, 128]) for d_ in range(2)]
    Sb = [sb(st, f"Sb{d_}", [128, 4, 128], BF16) for d_ in range(2)]
    qs = Ring([sb(st, f"q{i}", [128, SG]) for i in range(2)])
    A1 = Ring([sb(st, f"A1_{i}", [128, SG]) for i in range(2)])
    A2 = Ring([sb(st, f"A2_{i}", [128, SG]) for i in range(2)])
    A3 = Ring([sb(st, f"A3_{i}", [128, SG]) for i in range(2)])
    A4 = Ring([sb(st, f"A4_{i}", [128, SG]) for i in range(2)])
    A5 = Ring([sb(st, f"A5_{i}", [128, SG]) for i in range(2)])
    keys = [(h, d_) for h in range(4) for d_ in range(2)]
    qt_ = {k: sb(st, f"qt{k[0]}{k[1]}", [128, SG], BF16) for k in keys}
    kt_ = {k: sb(st, f"kt{k[0]}{k[1]}", [128, SG], BF16) for k in keys}
    kh_ = {k: sb(st, f"kh{k[0]}{k[1]}", [128, SG], BF16) for k in keys}
    ktok = {k: sb(st, f"ktok{k[0]}{k[1]}", [32, NCS, 128], BF16) for k in keys}
    ed = {k: sb(st, f"ed{k[0]}{k[1]}", [128, NCS]) for k in keys}
    vck = [sb(st, f"vck{d_}", [32, NCS, HGW], BF16) for d_ in range(2)]
    ost = [sb(st, f"ost{d_}", [32, NCS, HGW]) for d_ in range(2)]
    scs = [Ring([sb(st, f"scs{d_}{i}", [32, 128], BF16) for i in range(2)]) for d_ in range(2)]
    ofr = Ring([sb(st, f"of{i}", [128, HGW]) for i in range(2)])
    obr = Ring([sb(st, f"ob{i}", [128, HGW]) for i in range(2)])
    ggr = Ring([sb(st, f"gg{i}", [128, HGW]) for i in range(2)])
    sqr = Ring([sb(st, f"sq{i}", [128, HGW]) for i in range(2)])
    ssr = Ring([sb(st, f"ss{i}", [128, 4]) for i in range(2)])
    ybr = Ring([sb(st, f"yb{i}", [128, HGW], BF16) for i in range(2)])
    BK = {0: (4, 5, 6), 1: (4, 5, 6)}
    ODST = (OF, OB)
    ZSRC = (ZFT, ZBT)
    TOTAL = float(nsteps + 2)
    ucnt = [0]
    UT = 4470 + 272

    def P():
        ucnt[0] += 1
        return ucnt[0] / UT

    for d_, src in enumerate((maskf_in, maskb_in)):
        for h in range(4):
            g.dma("sp", mask[d_][:, h, :], src[:, :], w=[mask[d_]])
            yield P()
    g.op("pool", lambda E: E.memset(ones[:, :], 1.0), w=[ones])
    yield P()
    for d_ in range(2):
        g.op("pool", lambda E: E.memset(S[d_][:, :, :], 0.0), w=[S[d_]])
        yield P()
        g.op("pool", lambda E: E.memset(Sb[d_][:, :, :], 0.0), w=[Sb[d_]])
        yield P()
    for step in range(nsteps):
        base = float(step)
        for d_ in range(2):
            t0, W = segs[d_][step]
            nc_ = W // CH
            g.dma("sp", vck[d_][:, 0:nc_, :], VH[t0:t0 + W, :].rearrange("(c s) n -> s c n", s=CH),
                  r=[H("VH", k) for k in range(t0 // 128, (t0 + W) // 128)], w=[vck[d_]])
            yield P()
        for ki_, (h, d_) in enumerate(keys):
            t0, W = segs[d_][step]
            nc_ = W // CH
            ld = d_ * 4 + l
            q = qs.next()
            a1 = A1.next(); a2 = A2.next(); a3 = A3.next(); a4 = A4.next(); a5 = A5.next()
            hk = (t0 // 512) * 512 if t0 < T_LAT else T_LAT
            g.dma("sp", q[:, 0:W], QHT[h, :, t0:t0 + W], r=[H("QHT" + str(h), hk)], w=[q])
            yield P()
            g.dma("sp", a1[:, 0:W], ZSRC[d_][h, :, t0:t0 + W], r=[H(ZSRC[d_].name + str(h), hk)], w=[a1])
            yield P()
            g.act(a1[:, 0:W], a1[:, 0:W], AF.Exp, r=[a1], w=[a1], scale=-1.0)
            yield P()
            g.ts(a1[:, 0:W], a1[:, 0:W], 1.0, None, ALU.add, r=[a1], w=[a1])
            yield P()
            g.op("dve", lambda E: E.reciprocal(out=a1[:, 0:W], in_=a1[:, 0:W]), r=[a1], w=[a1])
            yield P()
            g.ts(a1[:, 0:W], a1[:, 0:W], OML[:, h, ld:ld + 1], LB[:, h, ld:ld + 1], ALU.mult, ALU.add,
                 r=[a1, OML, LB], w=[a1])
            yield P()
            g.ts(a2[:, 0:W], a1[:, 0:W], -1.0, 1.0, ALU.mult, ALU.add, r=[a1], w=[a2], e="pool")
            yield P()
            g.act(a1[:, 0:W], a1[:, 0:W], AF.Ln, r=[a1], w=[a1])
            yield P()
            g.op("dve", lambda E: E.tensor_tensor_scan(out=a3[:, 0:W], data0=ones[:, 0:W], data1=a1[:, 0:W],
                                                       initial=0.0, op0=ALU.mult, op1=ALU.add),
                 r=[ones, a1], w=[a3])
            yield P()
            g.tt(a1[:, 0:W], a3[:, 0:W], a1[:, 0:W], ALU.subtract, r=[a3, a1], w=[a1], e="pool")
            yield P()
            Pv = a3[:, 0:W].rearrange("p (c j) -> p c j", j=CH)
            Qv = a1[:, 0:W].rearrange("p (c j) -> p c j", j=CH)
            a4v = a4[:, 0:W].rearrange("p (c j) -> p c j", j=CH)
            a5v = a5[:, 0:W].rearrange("p (c j) -> p c j", j=CH)
            bc = lambda ap: ap.to_broadcast([128, nc_, CH])
            if d_ == 0:
                g.tt(a4v, Pv, bc(Qv[:, :, 0:1]), ALU.subtract, r=[a3, a1], w=[a4])
                yield P()
                g.tt(a5v, bc(Pv[:, :, CH - 1:CH]), Pv, ALU.subtract, r=[a3], w=[a5], e="pool")
                yield P()
            else:
                g.tt(a4v, bc(Pv[:, :, CH - 1:CH]), Qv, ALU.subtract, r=[a3, a1], w=[a4])
                yield P()
                g.tt(a5v, Qv, bc(Qv[:, :, 0:1]), ALU.subtract, r=[a1], w=[a5], e="pool")
                yield P()
            e_ = ed[h, d_]
            g.tt(e_[:, 0:nc_], Pv[:, :, CH - 1], Qv[:, :, 0], ALU.subtract, r=[a3, a1], w=[e_])
            yield P()
            g.act(e_[:, 0:nc_], e_[:, 0:nc_], AF.Exp, r=[e_], w=[e_])
            yield P()
            g.ts(a4[:, 0:W], a4[:, 0:W], -80.0, None, ALU.max, r=[a4], w=[a4])
            yield P()
            g.act(a3[:, 0:W], a4[:, 0:W], AF.Exp, r=[a4], w=[a3])
            yield P()
            g.act(a1[:, 0:W], a4[:, 0:W], AF.Exp, r=[a4], w=[a1], scale=-1.0)
            yield P()
            g.act(a5[:, 0:W], a5[:, 0:W], AF.Exp, r=[a5], w=[a5])
            yield P()
            g.tt(qt_[h, d_][:, 0:W], q[:, 0:W], a3[:, 0:W], ALU.mult, r=[q, a3], w=[qt_[h, d_]])
            yield P()
            g.tt(kt_[h, d_][:, 0:W], a2[:, 0:W], a1[:, 0:W], ALU.mult, r=[a2, a1], w=[kt_[h, d_]], e="pool")
            yield P()
            g.tt(kh_[h, d_][:, 0:W], a2[:, 0:W], a5[:, 0:W], ALU.mult, r=[a2, a5], w=[kh_[h, d_]])
            yield P()
            for c4 in range(0, nc_, 4):
                pb = 4
                for cc in range(4):
                    c = c4 + cc
                    g.mm(psall[0:32, pb, cc * 128:(cc + 1) * 128], lhsT=kh_[h, d_][:, c * CH:(c + 1) * CH],
                         rhs=ident[:, :], start=True, stop=True, r=[kh_[h, d_], ident], w=[PSH[pb]], inc=(cc == 3))
                g.cp(ktok[h, d_][:, c4:c4 + 4, :], psall[0:32, pb, :].rearrange("p (c k) -> p c k", c=4),
                     r=[PSH[pb]], w=[ktok[h, d_]], e="dve")
                yield P()
        ncs = [segs[d_][step][1] // CH for d_ in range(2)]
        for ci in range(max(ncs)):
            for d_ in range(2):
                nc_ = ncs[d_]
                if ci >= nc_:
                    continue
                c = ci if d_ == 0 else nc_ - 1 - ci
                cs = slice(c * CH, (c + 1) * CH)
                bx, by, bz = BK[d_]
                for h in range(4):
                    g.mm(psall[0:32, bx, h * 32:(h + 1) * 32], lhsT=kt_[h, d_][:, cs], rhs=qt_[h, d_][:, cs],
                         start=True, stop=True, r=[kt_[h, d_], qt_[h, d_]], w=[PSH[bx]], inc=(h == 3))
                for h in range(4):
                    g.mm(psall[:, by, h * 128:(h + 1) * 128], lhsT=ktok[h, d_][:, c, :],
                         rhs=vck[d_][:, c, h * 128:(h + 1) * 128], start=True, stop=True,
                         r=[ktok[h, d_], vck[d_]], w=[PSH[by]], inc=(h == 3))
                yield P()
                sc = scs[d_].next()
                g.tt(sc[:, :], psall[0:32, bx, 0:128], mask[d_][:, :, :].rearrange("p h t -> p (h t)"), ALU.mult,
                     r=[PSH[bx], mask[d_]], w=[sc])
                for h in range(4):
                    g.stt(S[d_][:, h, :], S[d_][:, h, :], ed[h, d_][:, c:c + 1], psall[:, by, h * 128:(h + 1) * 128],
                          ALU.mult, ALU.add, r=[S[d_], ed[h, d_], PSH[by]], w=[S[d_]])
                yield P()
                for h in range(4):
                    g.mm(psall[0:32, bz, h * 128:(h + 1) * 128], lhsT=qt_[h, d_][:, cs], rhs=Sb[d_][:, h, :],
                         start=True, stop=False, r=[qt_[h, d_], Sb[d_]], w=[PSH[bz]], inc=False)
                    g.mm(psall[0:32, bz, h * 128:(h + 1) * 128], lhsT=sc[:, h * 32:(h + 1) * 32],
                         rhs=vck[d_][:, c, h * 128:(h + 1) * 128], start=False, stop=True,
                         r=[sc, vck[d_]], w=[PSH[bz]], inc=(h == 3))
                yield P()
                g.cp(ost[d_][:, c, :], psall[0:32, bz, :], r=[PSH[bz]], w=[ost[d_]], e="dve")
                g.cp(Sb[d_][:, :, :], S[d_][:, :, :], r=[S[d_]], w=[Sb[d_]], e="pool")
                yield P()
        for d_ in range(2):
            t0, W = segs[d_][step]
            nc_ = W // CH
            g.dma("pool", ODST[d_][t0:t0 + W, :].rearrange("(c s) v -> s c v", s=CH),
                  ost[d_][:, 0:nc_, :], r=[ost[d_]],
                  w=[H(ODST[d_].name, k) for k in range(t0 // 128, (t0 + W) // 128)])
            yield P()
    for tt_ in range(NT):
        r0 = tt_ * 128
        of = ofr.next(); ob = obr.next(); gg = ggr.next(); sq = sqr.next(); ss = ssr.next(); yb = ybr.next()
        g.dma("sp", of[:, :], OF[r0:r0 + 128, :], r=[H("OF", tt_)], w=[of])
        yield P()
        g.dma("sp", ob[:, :], OB[r0:r0 + 128, :], r=[H("OB", tt_)], w=[ob])
        yield P()
        g.dma("sp", gg[:, :], GH[r0:r0 + 128, :], r=[H("GH", tt_)], w=[gg])
        yield P()
        g.tt(of[:, :], of[:, :], ob[:, :], ALU.add, r=[of, ob], w=[of], e="pool")
        yield P()
        g.tt(sq[:, :], of[:, :], of[:, :], ALU.mult, r=[of], w=[sq], e="pool")
        yield P()
        g.op("dve", lambda E: E.tensor_reduce(out=ss[:, :], in_=sq[:, :].rearrange("p (h v) -> p h v", h=4),
                                              axis=AX.X, op=ALU.add), r=[sq], w=[ss])
        yield P()
        g.ts(ss[:, :], ss[:, :], 1.0 / 128, EPS, ALU.mult, ALU.add, r=[ss], w=[ss])
        yield P()
        g.act(ss[:, :], ss[:, :], AF.Sqrt, r=[ss], w=[ss])
        yield P()
        g.op("dve", lambda E: E.reciprocal(out=ss[:, :], in_=ss[:, :]), r=[ss], w=[ss])
        yield P()
        ofv = of[:, :].rearrange("p (h v) -> p h v", h=4)
        g.tt(ofv, ofv, ss[:, :].unsqueeze(2).to_broadcast([128, 4, 128]), ALU.mult, r=[of, ss], w=[of])
        yield P()
        g.tt(ofv, ofv, ONORM[:, l * 128:(l + 1) * 128].unsqueeze(1).to_broadcast([128, 4, 128]), ALU.mult,
             r=[of, ONORM], w=[of], e="pool")
        yield P()
        g.tt(yb[:, :], of[:, :], gg[:, :], ALU.mult, r=[of, gg], w=[yb])
        yield P()
        g.dma("pool", YC[r0:r0 + 128, 0:HGW], yb[:, :], r=[yb], w=[H("YCh", tt_)])
        yield P()


def attn_gen(nc, g, l, st, psall, PSH, H, sb, NLAM, SUBLN, QRT, KRT, VD, YC, identf, QRTM):
    kr = sb(st, "kr", [128, 2, N], BF16)
    qmr = Ring([sb(st, f"qm{i}", [128, 2, 4, 512], BF16) for i in range(2)])
    vp = sb(st, "vp", [128, NT, 4, 65], BF16)
    ptr = Ring([sb(st, f"pt{i}", [128, 512], BF16) for i in range(3)])
    accS = [Ring([sb(st, f"accS{m}{i}", [65, 512]) for i in range(2)]) for m in range(2)]
    rsr = Ring([sb(st, f"rs{i}", [128, 8]) for i in range(2)])
    tmr = Ring([sb(st, f"tm{i}", [128, 4, 64]) for i in range(2)])
    O2r = Ring([sb(st, f"O2_{i}", [128, 16, 64]) for i in range(2)])
    SQr = Ring([sb(st, f"SQ_{i}", [128, 16, 64]) for i in range(1)])
    SSr = Ring([sb(st, f"SS_{i}", [128, 16]) for i in range(2)])
    ysr = Ring([sb(st, f"ys{i}", [128, 4, 256], BF16) for i in range(2)])
    allblk = [b * 512 for b in range(8)] + [T_LAT]
    for ti in range(2):
        g.dma("sp", kr[:, ti, :], KRT[ti, :, :], r=[H("KRT" + str(ti), t0) for t0 in allblk], w=[kr])
    g.op("dve", lambda E: E.memset(vp[:, :, :, 64:65], 1.0), w=[vp])
    vph = [T() for _ in range(NT)]
    qmb = {}

    def load_qm(bi):
        q0, W, _ = qblocks[bi]
        qm = qmr.next()
        for ti in range(2):
            g.dma("sp", qm[:, ti, :, 0:W], QRTM[ti, :, :, q0:q0 + W].rearrange("g p n -> p g n"),
                  r=[H("QRTM" + str(ti) + str(gq), q0) for gq in range(4)] + [H("QRTMz")], w=[qm])
        qmb[bi] = qm

    for kt in range(NT):
        g.dma("sp", vp[:, kt, :, 0:64], VD[kt * 128:(kt + 1) * 128, :].rearrange("p (h d) -> p h d", h=4),
              r=[H("VD", kt), vp.h], w=[vph[kt]])
    qblocks = [(b * 512, 512, list(range(32)) + [32, 33]) for b in range(8)] + [(T_LAT, 256, [32, 33])]
    load_qm(0)
    yield 0.0
    spb = Ring([0, 1])
    acc = (2, 3)
    TB = 7
    LA = 1
    steps = []
    for bi, (q0, W, kts) in enumerate(qblocks):
        for h in range(4):
            for ki, kt in enumerate(kts):
                for m in range(2):
                    steps.append((bi, h, ki, kt, m))
    NS = float(len(steps))
    state = {}

    def issue_score(i):
        bi, h, ki, kt, m = steps[i]
        q0, W, kts = qblocks[bi]
        ti = h // 2
        gq = 2 * (h % 2) + m
        pb = spb.next()
        if bi not in qmb:
            load_qm(bi)
        if h == 0 and ki == 0 and m == 0 and bi + 1 < len(qblocks) and (bi + 1) not in qmb:
            load_qm(bi + 1)
        qm = qmb[bi]
        g.mm(psall[:, pb, 0:W], lhsT=kr[:, ti, kt * 128:(kt + 1) * 128], rhs=qm[:, ti, gq, 0:W],
             start=True, stop=True, r=[kr, qm], w=[PSH[pb]])
        state[i] = pb

    for i in range(min(LA, len(steps))):
        issue_score(i)
    cur = {}
    for i, (bi, h, ki, kt, m) in enumerate(steps):
        q0, W, kts = qblocks[bi]
        nq = W // 128
        if ki == 0 and m == 0 and h == 0:
            cur["O2"] = O2r.next(); cur["SQ"] = SQr.next(); cur["SS"] = SSr.next(); cur["ys"] = ysr.next()
        if i + LA < len(steps):
            issue_score(i + LA)
        pb = state.pop(i)
        pt = ptr.next()
        g.act(pt[:, 0:W], psall[:, pb, 0:W], AF.Exp, r=[PSH[pb]], w=[pt], scale=ATT_SCALE)
        g.mm(psall[0:65, acc[m], 0:W], lhsT=vp[:, kt, h, :], rhs=pt[:, 0:W], start=(ki == 0), stop=(ki == len(kts) - 1),
             r=[pt, vph[kt]], w=[PSH[acc[m]]])
        if ki == len(kts) - 1 and m == 1:
            O2 = cur["O2"]; SQ = cur["SQ"]; SS = cur["SS"]; ys = cur["ys"]
            yield (i + 0.5) / NS
            aS = [accS[0].next(), accS[1].next()]
            for mm_ in range(2):
                g.cp(aS[mm_][:, 0:W], psall[0:65, acc[mm_], 0:W], r=[PSH[acc[mm_]]], w=[aS[mm_]], e="dve")
            rs = rsr.next()
            tm = tmr.next()
            O2h = O2[:, :, :].rearrange("p (q h) d -> p q h d", h=4)[:, 0:nq, h, :]
            for mm_ in range(2):
                yield (i + 0.6 + 0.2 * mm_) / NS
                for qt in range(nq):
                    g.mm(psall[:, TB, qt * 65:(qt + 1) * 65], lhsT=aS[mm_][0:65, qt * 128:(qt + 1) * 128],
                         rhs=identf[0:65, 0:65], start=True, stop=True, r=[aS[mm_], identf], w=[PSH[TB]],
                         inc=(qt == nq - 1))
                yield (i + 0.7 + 0.2 * mm_) / NS
                tv = psall[:, TB, 0:nq * 65].rearrange("p (q c) -> p q c", c=65)
                g.op("dve", lambda E: E.reciprocal(out=rs[:, mm_ * 4:mm_ * 4 + nq].unsqueeze(2), in_=tv[:, :, 64:65]),
                     r=[PSH[TB]], w=[rs])
                if mm_ == 0:
                    g.tt(O2h, tv[:, :, 0:64], rs[:, 0:nq].unsqueeze(2).to_broadcast([128, nq, 64]), ALU.mult,
                         r=[PSH[TB], rs], w=[O2])
                else:
                    g.ts(rs[:, 4:4 + nq], rs[:, 4:4 + nq], NLAM[:, l:l + 1], None, ALU.mult, r=[rs, NLAM], w=[rs])
                    g.tt(tm[:, 0:nq, :], tv[:, :, 0:64], rs[:, 4:4 + nq].unsqueeze(2).to_broadcast([128, nq, 64]), ALU.mult,
                         r=[PSH[TB], rs], w=[tm])
                    g.tt(O2h, O2h, tm[:, 0:nq, :], ALU.add, r=[O2, tm], w=[O2], e="pool")
            if h == 3:
                n16 = nq * 4
                g.tt(SQ[:, 0:n16, :], O2[:, 0:n16, :], O2[:, 0:n16, :], ALU.mult, r=[O2], w=[SQ], e="pool")
                g.op("dve", lambda E: E.tensor_reduce(out=SS[:, 0:n16], in_=SQ[:, 0:n16, :], axis=AX.X, op=ALU.add),
                     r=[SQ], w=[SS])
                g.ts(SS[:, 0:n16], SS[:, 0:n16], 1.0 / 64, EPS, ALU.mult, ALU.add, r=[SS], w=[SS])
                g.act(SS[:, 0:n16], SS[:, 0:n16], AF.Sqrt, r=[SS], w=[SS])
                g.op("dve", lambda E: E.reciprocal(out=SS[:, 0:n16], in_=SS[:, 0:n16]), r=[SS], w=[SS])
                g.tt(O2[:, 0:n16, :], O2[:, 0:n16, :], SS[:, 0:n16].unsqueeze(2).to_broadcast([128, n16, 64]), ALU.mult,
                     r=[O2, SS], w=[O2])
                g.tt(ys[:, 0:nq, :].rearrange("p q (h d) -> p (q h) d", h=4), O2[:, 0:n16, :],
                     SUBLN[:, l * 64:(l + 1) * 64].unsqueeze(1).to_broadcast([128, n16, 64]), ALU.mult,
                     r=[O2, SUBLN], w=[ys], e="pool")
                g.dma("pool", YC[q0:q0 + W, 512:768].rearrange("(q p) c -> p q c", p=128), ys[:, 0:nq, :], r=[ys],
                      w=[H("YCa", k) for k in range(q0 // 128, (q0 + W) // 128)])
        yield (i + 1) / NS


def fft_gen(nc, g, l, st, psall, PSH, H, sb, FAB, YC, dftC, dftS, dftCc, dftSc, banks):
    ab = sb(st, "ab", [128, NT, 512], BF16)
    g.dma("sp", ab[:, :, :], FAB[:, :].rearrange("(t p) c -> p t c", p=128), r=[H("FAB", k) for k in range(NT)], w=[ab])
    ctr = Ring([sb(st, f"ct{i}", [128, 32, 128], BF16) for i in range(2)])
    str_ = Ring([sb(st, f"st{i}", [128, 32, 128], BF16) for i in range(2)])
    ybr = Ring([sb(st, f"yb{i}", [128, 256], BF16) for i in range(2)])
    pbr = Ring(list(banks))
    for j in range(NT):
        ct = ctr.next()
        sn = str_.next()
        if j < 32:
            nti, tb = 32, 0
            g.dma("sp", ct[:, :, :], dftC[j].rearrange("p (t c) -> p t c", t=32), w=[ct])
            g.dma("sp", sn[:, :, :], dftS[j].rearrange("p (t c) -> p t c", t=32), w=[sn])
        else:
            nti, tb = 2, 32
            g.dma("sp", ct[:, 0:2, :], dftCc[j - 32].rearrange("p (t c) -> p t c", t=2), w=[ct])
            g.dma("sp", sn[:, 0:2, :], dftSc[j - 32].rearrange("p (t c) -> p t c", t=2), w=[sn])
        pb = pbr.next()
        for ti in range(nti):
            g.mm(psall[:, pb, 0:256], lhsT=ct[:, ti, :], rhs=ab[:, tb + ti, 0:256], start=(ti == 0), stop=False,
                 r=[ct, ab], w=[PSH[pb]], inc=False)
            g.mm(psall[:, pb, 0:256], lhsT=sn[:, ti, :], rhs=ab[:, tb + ti, 256:512], start=False, stop=(ti == nti - 1),
                 r=[sn, ab], w=[PSH[pb]], inc=(ti == nti - 1))
        yb = ybr.next()
        g.cp(yb[:, :], psall[:, pb, 0:256], r=[PSH[pb]], w=[yb], e="act")
        g.dma("pool", YC[j * 128:(j + 1) * 128, 768:1024], yb[:, :], r=[yb], w=[H("YCf", j)])
        yield j


_CONST = {}


def _consts():
    if _CONST:
        return _CONST
    bf = ml_dtypes.bfloat16
    inv_freq = 1.0 / (10000.0 ** (np.arange(0, 16, 2, dtype=np.float32) / 16.0))
    t = np.arange(T_LAT)
    pos = np.stack([t // 64, t % 64], axis=0).astype(np.float32)
    ang = pos[:, None, :] * inv_freq[None, :, None].astype(np.float32)
    cos = np.cos(ang).astype(np.float32)
    sin = np.sin(ang).astype(np.float32)
    C = np.ones((128, N), np.float32)
    S = np.zeros((128, N), np.float32)
    for p in range(128):
        d = p % 32
        axis, half, F = d // 16, (d // 8) % 2, d % 8
        C[p, :T_LAT] = cos[axis, F]
        S[p, :T_LAT] = sin[axis, F] * (-1.0 if half == 0 else 1.0)
    _CONST["ropeC"] = C
    _CONST["ropeS"] = S

    def dft_tables(T_, scale):
        tt = np.arange(T_, dtype=np.int64)
        prod = (tt[:, None] * tt[None, :]) % T_
        angm = 2.0 * np.pi * prod.astype(np.float64) / T_
        Cm = (np.cos(angm) * scale).astype(np.float32)
        Sm = (-np.sin(angm) * scale).astype(np.float32)
        nt = T_ // 128

        def blk(M):
            return np.ascontiguousarray(M.reshape(nt, 128, nt, 128).transpose(2, 1, 0, 3)).reshape(nt, 128, nt * 128).astype(bf)
        return blk(Cm), blk(Sm)
    _CONST["dftC"], _CONST["dftS"] = dft_tables(T_LAT, 1.0 / 64.0)
    _CONST["dftCc"], _CONST["dftSc"] = dft_tables(T_CTX, 1.0 / 16.0)
    cc = np.arange(64)
    a64 = 2.0 * np.pi * ((cc[:, None] * cc[None, :]) % 64) / 64.0
    chC = np.zeros((128, 128), np.float32)
    chS = np.zeros((128, 128), np.float32)
    for gI in range(2):
        chC[gI * 64:(gI + 1) * 64, gI * 64:(gI + 1) * 64] = np.cos(a64) / 8.0
        chS[gI * 64:(gI + 1) * 64, gI * 64:(gI + 1) * 64] = np.sin(a64) / 8.0
    _CONST["chC"] = chC.astype(bf)
    _CONST["chS"] = chS.astype(bf)
    _CONST["ident"] = np.eye(128, dtype=np.float32).astype(bf)
    _CONST["identf"] = np.eye(128, dtype=np.float32)
    s_i = np.arange(32)
    _CONST["maskf"] = (s_i[:, None] <= s_i[None, :]).astype(np.float32)
    _CONST["maskb"] = (s_i[:, None] >= s_i[None, :]).astype(np.float32)
    return _CONST


def _swap_cols():
    idx = np.arange(256)
    d = idx % 32
    half = (d // 8) % 2
    return idx + np.where(half == 0, 8, -8)


def prepare_inputs(x, c, ctx, c_ctx, w_mod, b_mod, norm_g, w_in, w_out, hg_lb_logits, hg_onorm,
                   da_lambda, da_subln, w_ffn_in, w_ffn_out):
    f = lambda a: np.ascontiguousarray(np.asarray(a, dtype=np.float32))
    x, c, ctx, c_ctx = f(x), f(c), f(ctx), f(c_ctx)
    w_in = f(w_in)
    sw = _swap_cols()
    w_in_x = np.concatenate([w_in, w_in[:, :, 2560 + sw], w_in[:, :, 2816 + sw]], axis=2)
    shared = {
        "w_mod": f(w_mod), "b_mod": f(b_mod), "norm_g": f(norm_g).reshape(DEPTH, 4 * D),
        "w_in": np.ascontiguousarray(w_in_x), "w_out": f(w_out),
        "lb_logits": f(hg_lb_logits).reshape(8, HGW), "hg_onorm": f(hg_onorm).reshape(1, -1),
        "da_lambda": f(da_lambda).reshape(1, -1), "da_subln": f(da_subln).reshape(1, -1),
        "w_ffn_in": f(w_ffn_in), "w_ffn_out": f(w_ffn_out),
    }
    shared.update(_consts())
    in_maps = []
    for core in range(8):
        b = core % 4
        m = dict(shared)
        m["xin"] = np.ascontiguousarray(np.concatenate([x[b], ctx[b]], axis=0))
        m["cond"] = np.ascontiguousarray(np.stack([c[b], c_ctx], axis=0))
        in_maps.append(m)
    return in_maps


_NC_CACHE = {}


def kernel(x, c, ctx, c_ctx, w_mod, b_mod, norm_g, w_in, w_out, hg_lb_logits, hg_onorm,
           da_lambda, da_subln, w_ffn_in, w_ffn_out):
    in_maps = prepare_inputs(x, c, ctx, c_ctx, w_mod, b_mod, norm_g, w_in, w_out, hg_lb_logits, hg_onorm,
                             da_lambda, da_subln, w_ffn_in, w_ffn_out)
    nc = build()
    res = run_bass_kernel_spmd(nc, in_maps, core_ids=list(range(8)))
    out = np.stack([np.asarray(res.results[b]["yout"], dtype=np.float32) for b in range(4)], axis=0)
    return out
```

```python
import math
from contextlib import ExitStack

import numpy as np
import ml_dtypes
import concourse.bass as bass
import concourse.mybir as mybir
from concourse.bass_utils import run_bass_kernel_spmd

F32 = mybir.dt.float32
BF16 = mybir.dt.bfloat16
AF = mybir.ActivationFunctionType
ALU = mybir.AluOpType
AX = mybir.AxisListType

D = 1024
T_LAT = 4096
T_CTX = 256
N = T_LAT + T_CTX
NT = N // 128
DEPTH = 4
HGW = 512
INW = 3584
INX = INW + 512
FFH = 2816
EPS = 1e-6
CH = 32
NCH = N // CH
SEG = 512
LAM_INIT = [0.8 - 0.6 * math.exp(-0.3 * l) for l in range(DEPTH)]
ATT_SCALE = 32 ** -0.5

NDS = 24
NDQ = {"sp": 12, "pool": 8, "act": 2}


class T:
    __slots__ = ("w", "r")

    def __init__(self):
        self.w = None
        self.r = {}


class Buf:
    def __init__(self, t):
        self.t = t
        self.h = T()

    def __getitem__(self, k):
        return self.t[k]


class Ring:
    def __init__(self, bufs):
        self.b = bufs
        self.i = 0

    def next(self):
        b = self.b[self.i]
        self.i = (self.i + 1) % len(self.b)
        return b


class G:
    def __init__(self, nc, es):
        self.nc = nc
        self.E = {"pe": nc.tensor, "act": nc.scalar, "dve": nc.vector, "pool": nc.gpsimd, "sp": nc.sync}
        self.sem = {e: es.enter_context(nc.semaphore("c_" + e)) for e in ("pe", "act", "dve", "pool")}
        self.cnt = dict.fromkeys(self.sem, 0)
        self.seen = {e: {} for e in self.E}
        self.dsem = {q: [es.enter_context(nc.semaphore(f"d{q}{i}")) for i in range(NDQ[q])] for q in ("sp", "pool", "act")}
        self.dval = {q: [0] * NDQ[q] for q in self.dsem}
        self.dnext = dict.fromkeys(self.dsem, 0)
        self.nins = 0

    def _wait(self, e, deps):
        need = {}
        for (s, v) in deps:
            if need.get(s, 0) < v:
                need[s] = v
        seen = self.seen[e]
        own = self.sem.get(e)
        for s, v in need.items():
            if seen.get(s, 0) >= v:
                continue
            if e == "pe" and s is own:
                continue
            self.E[e].wait_ge(s, v)
            seen[s] = v

    @staticmethod
    def _deps(r, w):
        deps = []
        for t in r:
            if t.w:
                deps.append(t.w)
        for t in w:
            if t.w:
                deps.append(t.w)
            deps.extend(t.r.items())
        return deps

    @staticmethod
    def _mark(r, w, s, v):
        for t in r:
            if t.r.get(s, 0) < v:
                t.r[s] = v
        for t in w:
            t.w = (s, v)
            t.r = {}

    def op(self, e, fn, r=(), w=(), inc=True):
        r = [x.h if isinstance(x, Buf) else x for x in r]
        w = [x.h if isinstance(x, Buf) else x for x in w]
        self._wait(e, self._deps(r, w))
        ins = fn(self.E[e])
        self.nins += 1
        s = self.sem[e]
        if inc:
            self.cnt[e] += 1
            v = self.cnt[e]
            ins.then_inc(s, 1)
        else:
            v = self.cnt[e] + 1
        self._mark(r, w, s, v)
        return ins

    def dma(self, q, out, in_, r=(), w=(), **kw):
        r = [x.h if isinstance(x, Buf) else x for x in r]
        w = [x.h if isinstance(x, Buf) else x for x in w]
        i = self.dnext[q]
        self.dnext[q] = (i + 1) % NDQ[q]
        s = self.dsem[q][i]
        deps = self._deps(r, w)
        if self.dval[q][i]:
            deps.append((s, self.dval[q][i]))
        self._wait(q, deps)
        ins = self.E[q].dma_start(out=out, in_=in_, **kw)
        self.nins += 1
        self.dval[q][i] += 16
        v = self.dval[q][i]
        ins.then_inc(s, 16)
        self._mark(r, w, s, v)
        return ins

    def barrier(self, engines=("pe", "act", "dve", "pool", "sp")):
        deps = [(self.sem[x], self.cnt[x]) for x in self.sem if self.cnt[x]]
        for q in self.dsem:
            for i in range(NDQ[q]):
                if self.dval[q][i]:
                    deps.append((self.dsem[q][i], self.dval[q][i]))
        for e in engines:
            self._wait(e, deps)

    def mm(self, out, lhsT, rhs, start, stop, r=(), w=(), inc=True, **kw):
        return self.op("pe", lambda E: E.matmul(out, lhsT=lhsT, rhs=rhs, start=start, stop=stop, **kw), r, w, inc)

    def act(self, out, in_, func, r=(), w=(), e="act", **kw):
        return self.op(e, lambda E: E.activation(out=out, in_=in_, func=func, **kw), r, w)

    def tt(self, out, in0, in1, op, r=(), w=(), e="dve"):
        return self.op(e, lambda E: E.tensor_tensor(out=out, in0=in0, in1=in1, op=op), r, w)

    def ts(self, out, in0, s1, s2, op0, op1=None, r=(), w=(), e="dve"):
        if op1 is None:
            return self.op(e, lambda E: E.tensor_scalar(out=out, in0=in0, scalar1=s1, scalar2=None, op0=op0), r, w)
        return self.op(e, lambda E: E.tensor_scalar(out=out, in0=in0, scalar1=s1, scalar2=s2, op0=op0, op1=op1), r, w)

    def stt(self, out, in0, scalar, in1, op0, op1, r=(), w=(), e="dve"):
        return self.op(e, lambda E: E.scalar_tensor_tensor(out=out, in0=in0, scalar=scalar, in1=in1, op0=op0, op1=op1), r, w)

    def cp(self, out, in_, r=(), w=(), e="dve"):
        if e == "act":
            return self.op(e, lambda E: E.copy(out=out, in_=in_), r, w)
        return self.op(e, lambda E: E.tensor_copy(out=out, in_=in_), r, w)


def build(n_layers=DEPTH, stop_after=None, debug=False):
    nc = bass.Bass("TRN2", target_bir_lowering=False)
    skind = "ExternalOutput" if debug else "Internal"
    LD = n_layers if debug else DEPTH

    def din(name, shape, dt=F32):
        return nc.dram_tensor(name, list(shape), dt, kind="ExternalInput").ap()

    def dscr(name, shape, dt=F32):
        return nc.dram_tensor(name, list(shape), dt, kind=skind).ap()

    xin = din("xin", [N, D])
    cond = din("cond", [2, D])
    w_mod = din("w_mod", [LD, D, 6 * D])
    b_mod = din("b_mod", [LD, 6 * D])
    norm_g = din("norm_g", [LD, 4 * D])
    w_in = din("w_in", [LD, D, INX])
    w_out = din("w_out", [LD, D, D])
    lb_logits = din("lb_logits", [8, HGW])
    hg_onorm = din("hg_onorm", [1, DEPTH * 128])
    da_lambda = din("da_lambda", [1, DEPTH * 4 * 32])
    da_subln = din("da_subln", [1, DEPTH * 64])
    w_ffn_in = din("w_ffn_in", [LD, D, 2 * FFH])
    w_ffn_out = din("w_ffn_out", [LD, FFH, D])
    ropeC = din("ropeC", [128, N])
    ropeS = din("ropeS", [128, N])
    dftC = din("dftC", [32, 128, 32 * 128], BF16)
    dftS = din("dftS", [32, 128, 32 * 128], BF16)
    dftCc = din("dftCc", [2, 128, 2 * 128], BF16)
    dftSc = din("dftSc", [2, 128, 2 * 128], BF16)
    chC = din("chC", [128, 128], BF16)
    chS = din("chS", [128, 128], BF16)
    ident_in = din("ident", [128, 128], BF16)
    identf_in = din("identf", [128, 128], F32)
    maskf_in = din("maskf", [32, 32], F32)
    maskb_in = din("maskb", [32, 32], F32)
    yout = nc.dram_tensor("yout", [T_LAT, D], F32, kind="ExternalOutput").ap()

    XS = dscr("XS", [N, D])
    X1 = dscr("X1", [N, D])
    MODV = dscr("MODV", [DEPTH * 2 * 6, D])
    QHT = dscr("QHT", [4, 128, N])
    ZFT = dscr("ZFT", [4, 128, N])
    ZBT = dscr("ZBT", [4, 128, N])
    VH = dscr("VH", [N, HGW], BF16)
    GH = dscr("GH", [N, HGW])
    QRT = dscr("QRT", [2, 128, N], BF16)
    KRT = dscr("KRT", [2, 128, N], BF16)
    VD = dscr("VD", [N, 256], BF16)
    FAB = dscr("FAB", [N, 512], BF16)
    OF = dscr("OF", [N, HGW])
    OB = dscr("OB", [N, HGW])
    YC = dscr("YC", [N, D], BF16)
    H2T = dscr("H2T", [8, 128, N], BF16)
    QRTM = dscr("QRTM", [2, 4, 128, N], BF16)

    dh = {}

    def H(name, key=0):
        k = (name, key)
        if k not in dh:
            dh[k] = T()
        return dh[k]

    def HA(name, keys):
        return [H(name, k) for k in keys]

    es = ExitStack()
    with es:
        g = G(nc, es)
        psall = es.enter_context(nc.psum_tensor("psall", [128, 8, 512], F32))
        PSH = [T() for _ in range(8)]

        uid = [0]

        def sb(st, name, shape, dt=F32):
            uid[0] += 1
            return Buf(st.enter_context(nc.sbuf_tensor(f"s{uid[0]}_{name}", list(shape), dt)))

        ident = sb(es, "ident", [128, 128], BF16)
        identf = sb(es, "identf", [128, 128], F32)
        LB = sb(es, "LB", [128, 4, 8])
        OML = sb(es, "OML", [128, 4, 8])
        LAM = sb(es, "LAM", [128, DEPTH])
        NLAM = sb(es, "NLAM", [128, DEPTH])
        ONORM = sb(es, "ONORM", [128, DEPTH * 128])
        SUBLN = sb(es, "SUBLN", [128, DEPTH * 64])
        g.dma("sp", ident[:, :], ident_in[:, :], w=[ident])
        g.dma("sp", identf[:, :], identf_in[:, :], w=[identf])
        g.dma("sp", ONORM[:, :], hg_onorm[0:1, :].partition_broadcast(128), w=[ONORM])
        g.dma("sp", SUBLN[:, :], da_subln[0:1, :].partition_broadcast(128), w=[SUBLN])
        with ExitStack() as stz:
            zt = sb(stz, "zt", [128, N], BF16)
            g.op("pool", lambda E: E.memset(zt[:, :], 0.0), w=[zt])
            for ti in range(2):
                for gq in range(4):
                    g.dma("sp", QRTM[ti, gq, :, :], zt[:, :], r=[zt], w=[H("QRTMz")])
            g.barrier()

        with ExitStack() as st:
            lg = sb(st, "lg", [8, HGW])
            ex = sb(st, "ex", [128, 4, 8])
            sm = sb(st, "sm", [128, 4, 2])
            dl = sb(st, "dl", [128, DEPTH * 4 * 32])
            pr = sb(st, "pr", [128, DEPTH * 2 * 32])
            pe2 = sb(st, "pe2", [128, DEPTH * 2])
            g.dma("sp", lg[:, :], lb_logits[:, :], w=[lg])
            for h in range(4):
                g.mm(psall[:, 0, h * 8:(h + 1) * 8], lhsT=lg[0:8, h * 128:(h + 1) * 128], rhs=identf[0:8, 0:8],
                     start=True, stop=True, r=[lg, identf], w=[PSH[0]])
            g.act(ex[:, :, :], psall[:, 0, 0:32].rearrange("p (h x) -> p h x", h=4), AF.Exp, r=[PSH[0]], w=[ex])
            exv = ex[:, :, :].rearrange("p h (d l) -> p h d l", d=2)
            g.op("dve", lambda E: E.tensor_reduce(out=sm[:, :, :], in_=exv, axis=AX.X, op=ALU.add), r=[ex], w=[sm])
            g.op("dve", lambda E: E.reciprocal(out=sm[:, :, :], in_=sm[:, :, :]), r=[sm], w=[sm])
            g.tt(exv, exv, sm[:, :, :].unsqueeze(3).to_broadcast([128, 4, 2, 4]), ALU.mult, r=[ex, sm], w=[ex])
            lbv = LB[:, :, :].rearrange("p h (d l) -> p h d l", d=2)
            g.op("dve", lambda E: E.memset(lbv[:, :, :, 0:1], 0.0), w=[LB])
            for l in range(1, 4):
                g.tt(lbv[:, :, :, l:l + 1], lbv[:, :, :, l - 1:l], exv[:, :, :, l:l + 1], ALU.add, r=[ex, LB], w=[LB])
            g.ts(OML[:, :, :], LB[:, :, :], -1.0, 1.0, ALU.mult, ALU.add, r=[LB], w=[OML])
            g.dma("sp", dl[:, :], da_lambda[0:1, :].partition_broadcast(128), w=[dl])
            dlv = dl[:, :].rearrange("p (l a b x) -> p l a b x", l=DEPTH, a=2, b=2)
            prv = pr[:, :].rearrange("p (l a x) -> p l a x", l=DEPTH, a=2)
            g.tt(prv, dlv[:, :, :, 0, :], dlv[:, :, :, 1, :], ALU.mult, r=[dl], w=[pr])
            pe2v = pe2[:, :].rearrange("p (l a) -> p l a", a=2)
            g.op("dve", lambda E: E.tensor_reduce(out=pe2v, in_=prv, axis=AX.X, op=ALU.add), r=[pr], w=[pe2])
            g.act(pe2[:, :], pe2[:, :], AF.Exp, r=[pe2], w=[pe2])
            g.tt(LAM[:, :], pe2v[:, :, 0], pe2v[:, :, 1], ALU.subtract, r=[pe2], w=[LAM])
            for l in range(DEPTH):
                g.ts(LAM[:, l:l + 1], LAM[:, l:l + 1], float(LAM_INIT[l]), None, ALU.add, r=[LAM], w=[LAM])
            g.ts(NLAM[:, :], LAM[:, :], -1.0, None, ALU.mult, r=[LAM], w=[NLAM])
            for l in range(DEPTH):
                g.ts(SUBLN[:, l * 64:(l + 1) * 64], SUBLN[:, l * 64:(l + 1) * 64], float(1.0 - LAM_INIT[l]), None,
                     ALU.mult, r=[SUBLN], w=[SUBLN])

            cT = sb(st, "cT", [128, 8, 2])
            cstage = sb(st, "cstage", [2, D])
            g.dma("sp", cstage[:, :], cond[:, :], w=[cstage])
            g.act(cstage[:, :], cstage[:, :], AF.Silu, r=[cstage], w=[cstage])
            for kc in range(8):
                g.mm(psall[:, 1, kc * 2:(kc + 1) * 2], lhsT=cstage[0:2, kc * 128:(kc + 1) * 128], rhs=identf[0:2, 0:2],
                     start=True, stop=True, r=[cstage, identf], w=[PSH[1]])
            g.cp(cT[:, :, :], psall[:, 1, 0:16].rearrange("p (k r) -> p k r", r=2), r=[PSH[1]], w=[cT])
            wring = Ring([sb(st, f"wm{i}", [128, 8, 512]) for i in range(2)])
            modsb = sb(st, "modsb", [2, 6 * D])
            bmsb = sb(st, "bmsb", [2, 6 * D])
            ngsb = sb(st, "ngsb", [2, 4 * D])
            vst = sb(st, "vst", [2, 6 * D])
            pi = 2
            for l in range(n_layers):
                g.dma("sp", bmsb[:, :], b_mod[l:l + 1, :].partition_broadcast(2), w=[bmsb])
                g.dma("sp", ngsb[:, :], norm_g[l:l + 1, :].partition_broadcast(2), w=[ngsb])
                wv = w_mod[l].rearrange("(kc p) n -> p kc n", p=128)
                for j in range(12):
                    wb = wring.next()
                    g.dma("sp", wb[:, :, :], wv[:, :, j * 512:(j + 1) * 512], w=[wb])
                    pb = 2 + (pi % 2)
                    pi += 1
                    for kc in range(8):
                        g.mm(psall[0:2, pb, :], lhsT=cT[:, kc, :], rhs=wb[:, kc, :], start=(kc == 0), stop=(kc == 7),
                             r=[cT, wb], w=[PSH[pb]], inc=(kc == 7))
                    g.tt(modsb[:, j * 512:(j + 1) * 512], psall[0:2, pb, :], bmsb[:, j * 512:(j + 1) * 512], ALU.add,
                         r=[PSH[pb], bmsb], w=[modsb])
                m = lambda i: modsb[:, i * D:(i + 1) * D]
                ng = lambda i: ngsb[:, i * D:(i + 1) * D]
                vs = lambda i: vst[:, i * D:(i + 1) * D]
                g.stt(vs(0), m(1), 1.0, ng(0), ALU.add, ALU.mult, r=[modsb, ngsb], w=[vst])
                g.cp(vs(1), m(0), r=[modsb], w=[vst])
                g.tt(vs(2), m(2), ng(1), ALU.mult, r=[modsb, ngsb], w=[vst])
                g.stt(vs(3), m(4), 1.0, ng(2), ALU.add, ALU.mult, r=[modsb, ngsb], w=[vst])
                g.cp(vs(4), m(3), r=[modsb], w=[vst])
                g.tt(vs(5), m(5), ng(3), ALU.mult, r=[modsb, ngsb], w=[vst])
                for rr in range(2):
                    g.dma("pool", MODV[(l * 2 + rr) * 6:(l * 2 + rr) * 6 + 6, :].rearrange("(o i) d -> o i d", o=1),
                          vst[rr:rr + 1, :].rearrange("p (i d) -> p i d", i=6), r=[vst], w=[H("MODV", l)])
            g.barrier()
        if stop_after == "pro":
            return _finish(nc, g, xin, yout, es)

        def modrow(l, rr, i):
            k = (l * 2 + rr) * 6 + i
            return MODV[k:k + 1, :].partition_broadcast(128)

        def rstd_from_ss(ss, n, r, w):
            g.ts(ss, ss, 1.0 / n, EPS, ALU.mult, ALU.add, r=r, w=w)
            g.act(ss, ss, AF.Sqrt, r=w, w=w)
            g.op("dve", lambda E: E.reciprocal(out=ss, in_=ss), r=w, w=w)

        TOKBLKS = [(i * 512, 512, 0) for i in range(8)] + [(T_LAT, 256, 1)]

        for l in range(n_layers):
            xsrc = xin if l == 0 else XS
            with ExitStack() as st:
                wb = sb(st, "wb", [128, 8, INX], BF16)
                wbh = [[T() for _ in range(INX // 1024)] for _ in range(8)]
                wv = w_in[l].rearrange("(kc p) n -> p kc n", p=128)
                for c4 in range(0, INX, 1024):
                    for kc in range(8):
                        g.dma("pool", wb[:, kc, c4:c4 + 1024], wv[:, kc, c4:c4 + 1024], w=[wbh[kc][c4 // 1024]])
                G1 = [sb(st, f"G1_{rr}", [128, D]) for rr in range(2)]
                S1 = [sb(st, f"S1_{rr}", [128, D]) for rr in range(2)]
                for rr in range(2):
                    g.dma("sp", G1[rr][:, :], modrow(l, rr, 0), r=[H("MODV", l)], w=[G1[rr]])
                    g.dma("sp", S1[rr][:, :], modrow(l, rr, 1), r=[H("MODV", l)], w=[S1[rr]])
                chCs = sb(st, "chCs", [128, 128], BF16)
                chSs = sb(st, "chSs", [128, 128], BF16)
                g.dma("sp", chCs[:, :], chC[:, :], w=[chCs])
                g.dma("sp", chSs[:, :], chS[:, :], w=[chSs])
                xring = Ring([sb(st, f"xa{i}", [128, D]) for i in range(2)])
                sqj = sb(st, "sqj", [128, D])
                ssr = Ring([sb(st, f"ss{i}", [128, 1]) for i in range(2)])
                t1r = Ring([sb(st, f"t1_{i}", [128, D]) for i in range(2)])
                hbr = Ring([sb(st, f"hb{i}", [128, D], BF16) for i in range(2)])
                hTr = Ring([sb(st, f"hT{i}", [128, 8, 512], BF16) for i in range(2)])
                rcr = Ring([sb(st, f"rc{i}", [128, 512]) for i in range(2)])
                rsr = Ring([sb(st, f"rs{i}", [128, 512]) for i in range(2)])
                stf = Ring([sb(st, f"stf{i}", [128, 512]) for i in range(4)])
                stb = Ring([sb(st, f"stb{i}", [128, 512], BF16) for i in range(4)])
                rt1 = Ring([sb(st, f"rt1_{i}", [128, 512]) for i in range(2)])
                rt2 = Ring([sb(st, f"rt2_{i}", [128, 512]) for i in range(2)])
                uTr = Ring([sb(st, f"uT{i}", [128, 512], BF16) for i in range(4)])
                fmb = Ring([2, 3, 4, 5])
                tmb = Ring([6, 7])
                if stop_after == ("Aw", l):
                    return _finish(nc, g, xin, yout, es)
                for (t0, W, rr) in TOKBLKS:
                    ntl = W // 128
                    hT = hTr.next()
                    for i in range(ntl):
                        r0 = t0 + i * 128
                        xa = xring.next()
                        g.dma("sp", xa[:, :], xsrc[r0:r0 + 128, :], r=[H("XS", r0 // 128)], w=[xa])
                        ss = ssr.next()
                        g.act(sqj[:, :], xa[:, :], AF.Square, r=[xa], w=[sqj])
                        g.op("dve", lambda E: E.reduce_sum(out=ss[:, :], in_=sqj[:, :], axis=AX.X), r=[sqj], w=[ss])
                        rstd_from_ss(ss[:, :], D, [ss], [ss])
                        t1 = t1r.next()
                        g.stt(t1[:, :], xa[:, :], ss[:, 0:1], G1[rr][:, :], ALU.mult, ALU.mult, r=[xa, ss, G1[rr]], w=[t1])
                        hb = hbr.next()
                        g.tt(hb[:, :], t1[:, :], S1[rr][:, :], ALU.add, r=[t1, S1[rr]], w=[hb], e="pool")
                        for half in range(2):
                            for k4 in range(4):
                                kc = half * 4 + k4
                                g.mm(psall[:, half, k4 * 128:(k4 + 1) * 128], lhsT=hb[:, kc * 128:(kc + 1) * 128],
                                     rhs=ident[:, :], start=True, stop=True, r=[hb, ident], w=[PSH[half]], inc=(k4 == 3))
                            g.cp(hT[:, half * 4:half * 4 + 4, i * 128:(i + 1) * 128],
                                 psall[:, half, :].rearrange("p (k t) -> p k t", k=4), r=[PSH[half]], w=[hT],
                                 e=("act" if half else "dve"))
                    if stop_after == ("Ah", l):
                        return _finish(nc, g, xin, yout, es)
                    rc = rcr.next()
                    rs = rsr.next()
                    g.dma("sp", rc[:, 0:W], ropeC[:, t0:t0 + W], w=[rc])
                    g.dma("sp", rs[:, 0:W], ropeS[:, t0:t0 + W], w=[rs])

                    def fm(c0):
                        pb = fmb.next()
                        for kc in range(8):
                            g.mm(psall[:, pb, 0:W], lhsT=wb[:, kc, c0:c0 + 128], rhs=hT[:, kc, 0:W], start=(kc == 0),
                                 stop=(kc == 7), r=[wbh[kc][c0 // 1024], hT], w=[PSH[pb]], inc=(kc == 7))
                        return pb

                    for (c00, dst, fn) in ((0, QHT, AF.Silu), (1536, ZFT, AF.Copy), (2048, ZBT, AF.Copy)):
                        for h in range(4):
                            pb = fm(c00 + h * 128)
                            sf = stf.next()
                            if fn == AF.Silu:
                                g.act(sf[:, 0:W], psall[:, pb, 0:W], fn, r=[PSH[pb]], w=[sf])
                            else:
                                g.cp(sf[:, 0:W], psall[:, pb, 0:W], r=[PSH[pb]], w=[sf], e="dve")
                            g.dma("pool", dst[h, :, t0:t0 + W], sf[:, 0:W], r=[sf], w=[H(dst.name + str(h), t0)])
                    if stop_after == ("Af", l):
                        return _finish(nc, g, xin, yout, es)
                    for (c00, csw, dst) in ((2560, INW, QRT), (2816, INW + 256, KRT)):
                        for ti in range(2):
                            pa = fm(c00 + ti * 128)
                            ps_ = fm(csw + ti * 128)
                            a1 = rt1.next()
                            a2 = rt2.next()
                            g.tt(a1[:, 0:W], psall[:, pa, 0:W], rc[:, 0:W], ALU.mult, r=[PSH[pa], rc], w=[a1])
                            g.tt(a2[:, 0:W], psall[:, ps_, 0:W], rs[:, 0:W], ALU.mult, r=[PSH[ps_], rs], w=[a2])
                            sbb = stb.next()
                            g.tt(sbb[:, 0:W], a1[:, 0:W], a2[:, 0:W], ALU.add, r=[a1, a2], w=[sbb], e="pool")
                            g.dma("pool", dst[ti, :, t0:t0 + W], sbb[:, 0:W], r=[sbb], w=[H(dst.name + str(ti), t0)])
                            if dst is QRT:
                                for gq in range(4):
                                    g.dma("pool", QRTM[ti, gq, gq * 32:(gq + 1) * 32, t0:t0 + W], sbb[gq * 32:(gq + 1) * 32, 0:W],
                                          r=[sbb, H("QRTMz")], w=[H("QRTM" + str(ti) + str(gq), t0)])
                    if stop_after == ("Ar", l):
                        return _finish(nc, g, xin, yout, es)
                    uts = []
                    for ti in range(2):
                        pb = fm(3328 + ti * 128)
                        uT = uTr.next()
                        g.cp(uT[:, 0:W], psall[:, pb, 0:W], r=[PSH[pb]], w=[uT], e="act")
                        uts.append(uT)
                    for i in range(ntl):
                        pb = tmb.next()
                        for ab, tab in enumerate((chCs, chSs)):
                            for ti in range(2):
                                g.mm(psall[:, pb, ab * 256 + ti * 128: ab * 256 + (ti + 1) * 128],
                                     lhsT=uts[ti][:, i * 128:(i + 1) * 128], rhs=tab[:, :], start=True, stop=True,
                                     r=[uts[ti], tab], w=[PSH[pb]], inc=(ab == 1 and ti == 1))
                        sbb = stb.next()
                        g.cp(sbb[:, :], psall[:, pb, :], r=[PSH[pb]], w=[sbb], e="dve")
                        r0 = t0 + i * 128
                        g.dma("pool", FAB[r0:r0 + 128, :], sbb[:, :], r=[sbb], w=[H("FAB", r0 // 128)])
                    if stop_after == ("At", l):
                        return _finish(nc, g, xin, yout, es)
                    for i in range(ntl):
                        r0 = t0 + i * 128

                        def tm(c0, ncol):
                            pb = tmb.next()
                            for kc in range(8):
                                g.mm(psall[:, pb, 0:ncol], lhsT=hT[:, kc, i * 128:(i + 1) * 128], rhs=wb[:, kc, c0:c0 + ncol],
                                     start=(kc == 0), stop=(kc == 7), r=[wbh[kc][c0 // 1024], wbh[kc][(c0 + ncol - 1) // 1024], hT],
                                     w=[PSH[pb]], inc=(kc == 7))
                            return pb
                        pb = tm(512, 512)
                        sbb = stb.next()
                        g.cp(sbb[:, :], psall[:, pb, :], r=[PSH[pb]], w=[sbb], e="dve")
                        g.dma("pool", VH[r0:r0 + 128, :], sbb[:, :], r=[sbb], w=[H("VH", r0 // 128)])
                        pb = tm(1024, 512)
                        sf = stf.next()
                        g.act(sf[:, :], psall[:, pb, :], AF.Silu, r=[PSH[pb]], w=[sf])
                        g.dma("pool", GH[r0:r0 + 128, :], sf[:, :], r=[sf], w=[H("GH", r0 // 128)])
                        pb = tm(3072, 256)
                        sbb = stb.next()
                        g.cp(sbb[:, 0:256], psall[:, pb, 0:256], r=[PSH[pb]], w=[sbb], e="dve")
                        g.dma("pool", VD[r0:r0 + 128, :], sbb[:, 0:256], r=[sbb], w=[H("VD", r0 // 128)])
                g.barrier()
            if stop_after == ("A", l):
                return _finish(nc, g, xin, yout, es)

            mixer_phase(nc, g, l, psall, PSH, H, sb, ident, identf, LB, OML, ONORM, NLAM, SUBLN, QHT, ZFT, ZBT, VH, GH, OF, OB, YC,
                        QRT, KRT, VD, maskf_in, maskb_in, QRTM)
            if stop_after in (("H", l), ("T", l)):
                return _finish(nc, g, xin, yout, es)
            with ExitStack() as st:
                wo = sb(st, "wo", [128, 8, D], BF16)
                wov1 = w_out[l].rearrange("(kc p) n -> p kc n", p=128)
                woh = [T() for _ in range(8)]
                for kc in range(8):
                    g.dma("pool", wo[:, kc, :], wov1[:, kc, :], w=[woh[kc]])
                GG1 = [sb(st, f"GG1_{rr}", [128, D]) for rr in range(2)]
                G2 = [sb(st, f"G2_{rr}", [128, D]) for rr in range(2)]
                S2 = [sb(st, f"S2_{rr}", [128, D]) for rr in range(2)]
                for rr in range(2):
                    g.dma("sp", GG1[rr][:, :], modrow(l, rr, 2), r=[H("MODV", l)], w=[GG1[rr]])
                    g.dma("sp", G2[rr][:, :], modrow(l, rr, 3), r=[H("MODV", l)], w=[G2[rr]])
                    g.dma("sp", S2[rr][:, :], modrow(l, rr, 4), r=[H("MODV", l)], w=[S2[rr]])
                ycr = Ring([sb(st, f"yc{i}", [128, D], BF16) for i in range(2)])
                xr = Ring([sb(st, f"xc{i}", [128, D]) for i in range(2)])
                ycTr = Ring([sb(st, f"ycT{i}", [128, 8, 128], BF16) for i in range(2)])
                sqj = sb(st, "sqj", [128, D])
                ssr = Ring([sb(st, f"ss{i}", [128, 1]) for i in range(4)])
                t1r = Ring([sb(st, f"t1_{i}", [128, D]) for i in range(2)])
                x1r = Ring([sb(st, f"x1_{i}", [128, D]) for i in range(2)])
                h2r = Ring([sb(st, f"h2_{i}", [128, D], BF16) for i in range(2)])
                h2Tr = Ring([sb(st, f"h2T{i}", [128, 8, 128], BF16) for i in range(2)])
                tpb = Ring([(0, 1)])
                ypb = Ring([(4, 5), (6, 7)])
                fg = fft_gen(nc, g, l, st, psall, PSH, H, sb, FAB, YC, dftC, dftS, dftCc, dftSc, (2, 3))
                next(fg)
                for tt_ in range(NT):
                    if tt_ + 1 < NT:
                        next(fg)
                    r0 = tt_ * 128
                    rr = 0 if r0 < T_LAT else 1
                    yc = ycr.next()
                    xc = xr.next()
                    g.dma("sp", yc[:, :], YC[r0:r0 + 128, :], r=[H("YCh", tt_), H("YCa", tt_), H("YCf", tt_)], w=[yc])
                    g.dma("sp", xc[:, :], xsrc[r0:r0 + 128, :], r=[H("XS", tt_)], w=[xc])
                    pbs = tpb.next()
                    ycT = ycTr.next()
                    for half in range(2):
                        pb = pbs[half]
                        for k4 in range(4):
                            kc = half * 4 + k4
                            g.mm(psall[:, pb, k4 * 128:(k4 + 1) * 128], lhsT=yc[:, kc * 128:(kc + 1) * 128], rhs=ident[:, :],
                                 start=True, stop=True, r=[yc, ident], w=[PSH[pb]], inc=(k4 == 3))
                        g.cp(ycT[:, half * 4:half * 4 + 4, :], psall[:, pb, :].rearrange("p (k t) -> p k t", k=4),
                             r=[PSH[pb]], w=[ycT], e=("act" if half else "dve"))
                    yb = ypb.next()
                    for nb in range(2):
                        for kc in range(8):
                            g.mm(psall[:, yb[nb], :], lhsT=ycT[:, kc, :], rhs=wo[:, kc, nb * 512:(nb + 1) * 512],
                                 start=(kc == 0), stop=(kc == 7), r=[ycT, woh[kc]], w=[PSH[yb[nb]]], inc=(kc == 7))
                    ypv = psall[:, yb[0]:yb[0] + 2, :]
                    ss = ssr.next()
                    g.act(sqj[:, :].rearrange("p (a b) -> p a b", a=2), ypv, AF.Square, r=[PSH[yb[0]], PSH[yb[1]]], w=[sqj])
                    g.op("dve", lambda E: E.reduce_sum(out=ss[:, :], in_=sqj[:, :], axis=AX.X), r=[sqj], w=[ss])
                    rstd_from_ss(ss[:, :], D, [ss], [ss])
                    t1 = t1r.next()
                    g.stt(t1[:, :].rearrange("p (a b) -> p a b", a=2), ypv, ss[:, 0:1],
                          GG1[rr][:, :].rearrange("p (a b) -> p a b", a=2), ALU.mult, ALU.mult,
                          r=[PSH[yb[0]], PSH[yb[1]], ss, GG1[rr]], w=[t1])
                    x1 = x1r.next()
                    g.tt(x1[:, :], t1[:, :], xc[:, :], ALU.add, r=[t1, xc], w=[x1], e="pool")
                    g.dma("pool", X1[r0:r0 + 128, :], x1[:, :], r=[x1], w=[H("X1", tt_)])
                    ss2 = ssr.next()
                    g.act(sqj[:, :], x1[:, :], AF.Square, r=[x1], w=[sqj])
                    g.op("dve", lambda E: E.reduce_sum(out=ss2[:, :], in_=sqj[:, :], axis=AX.X), r=[sqj], w=[ss2])
                    rstd_from_ss(ss2[:, :], D, [ss2], [ss2])
                    t2 = t1r.next()
                    g.stt(t2[:, :], x1[:, :], ss2[:, 0:1], G2[rr][:, :], ALU.mult, ALU.mult, r=[x1, ss2, G2[rr]], w=[t2])
                    h2 = h2r.next()
                    g.tt(h2[:, :], t2[:, :], S2[rr][:, :], ALU.add, r=[t2, S2[rr]], w=[h2], e="pool")
                    pbs = tpb.next()
                    h2T = h2Tr.next()
                    for half in range(2):
                        pb = pbs[half]
                        for k4 in range(4):
                            kc = half * 4 + k4
                            g.mm(psall[:, pb, k4 * 128:(k4 + 1) * 128], lhsT=h2[:, kc * 128:(kc + 1) * 128], rhs=ident[:, :],
                                 start=True, stop=True, r=[h2, ident], w=[PSH[pb]], inc=(k4 == 3))
                        g.cp(h2T[:, half * 4:half * 4 + 4, :], psall[:, pb, :].rearrange("p (k t) -> p k t", k=4),
                             r=[PSH[pb]], w=[h2T], e=("act" if half else "dve"))
                    g.dma("pool", H2T[:, :, r0:r0 + 128].rearrange("k p t -> p k t"), h2T[:, :, :], r=[h2T], w=[H("H2T", tt_)])
                g.barrier()
            if stop_after == ("C1", l):
                return _finish(nc, g, xin, yout, es)

            with ExitStack() as st:
                wi = sb(st, "wi", [128, 8, 2 * FFH], BF16)
                wih = [[T() for _ in range(4)] for _ in range(8)]
                wiv = w_ffn_in[l].rearrange("(kc p) n -> p kc n", p=128)
                for c4 in (0, 2816, 1408, 4224):
                    for kc in range(8):
                        g.dma("pool", wi[:, kc, c4:c4 + 1408], wiv[:, kc, c4:c4 + 1408], w=[wih[kc][c4 // 1408]])
                wo2 = sb(st, "wo2", [128, 22, D], BF16)
                wo2h = [T() for _ in range(22)]
                wov = w_ffn_out[l].rearrange("(hc p) n -> p hc n", p=128)
                for hc in range(22):
                    g.dma("pool", wo2[:, hc, :], wov[:, hc, :], w=[wo2h[hc]])
                GG2 = [sb(st, f"GG2_{rr}", [128, D]) for rr in range(2)]
                for rr in range(2):
                    g.dma("sp", GG2[rr][:, :], modrow(l, rr, 5), r=[H("MODV", l)], w=[GG2[rr]])
                hr = Ring([sb(st, f"h2c{i}", [128, 8, 256], BF16) for i in range(2)])
                x1r = Ring([sb(st, f"x1c{i}", [128, D]) for i in range(3)])
                sgr = Ring([sb(st, f"sg{i}", [128, 256]) for i in range(2)])
                acr = Ring([sb(st, f"ac{i}", [128, 256], BF16) for i in range(3)])
                sqj = sb(st, "sqj", [128, D])
                ssr = Ring([sb(st, f"ss{i}", [128, 1]) for i in range(2)])
                t1r = Ring([sb(st, f"t1_{i}", [128, D]) for i in range(2)])
                x2r = Ring([sb(st, f"x2_{i}", [128, D]) for i in range(2)])
                gub = Ring([(4, 5), (6, 7)])
                for b in range(N // 256):
                    c0 = b * 256
                    rr = 0 if c0 < T_LAT else 1
                    hT = hr.next()
                    g.dma("sp", hT[:, :, :], H2T[:, :, c0:c0 + 256].rearrange("k p t -> p k t"),
                          r=[H("H2T", 2 * b), H("H2T", 2 * b + 1)], w=[hT])
                    for hc in range(22):
                        pg, pu = gub.next()
                        for kc in range(8):
                            g.mm(psall[:, pg, 0:256], lhsT=wi[:, kc, hc * 128:(hc + 1) * 128], rhs=hT[:, kc, :],
                                 start=(kc == 0), stop=(kc == 7), r=[wih[kc][(hc * 128) // 1408], hT], w=[PSH[pg]], inc=(kc == 7))
                        for kc in range(8):
                            g.mm(psall[:, pu, 0:256], lhsT=wi[:, kc, FFH + hc * 128:FFH + (hc + 1) * 128], rhs=hT[:, kc, :],
                                 start=(kc == 0), stop=(kc == 7), r=[wih[kc][(FFH + hc * 128) // 1408], hT], w=[PSH[pu]], inc=(kc == 7))
                        sg = sgr.next()
                        g.act(sg[:, :], psall[:, pg, 0:256], AF.Silu, r=[PSH[pg]], w=[sg])
                        ac = acr.next()
                        g.tt(ac[:, :], sg[:, :], psall[:, pu, 0:256], ALU.mult, r=[sg, PSH[pu]], w=[ac])
                        for i in range(2):
                            for nb in range(2):
                                g.mm(psall[:, i * 2 + nb, :], lhsT=ac[:, i * 128:(i + 1) * 128],
                                     rhs=wo2[:, hc, nb * 512:(nb + 1) * 512], start=(hc == 0), stop=(hc == 21),
                                     r=[ac, wo2h[hc]], w=[PSH[i * 2 + nb]], inc=(i == 1 and nb == 1),
                                     skip_group_check=True)
                    for i in range(2):
                        tt_ = 2 * b + i
                        r0 = tt_ * 128
                        x1 = x1r.next()
                        g.dma("sp", x1[:, :], X1[r0:r0 + 128, :], r=[H("X1", tt_)], w=[x1])
                        opv = psall[:, 2 * i:2 * i + 2, :]
                        ss = ssr.next()
                        g.act(sqj[:, :].rearrange("p (a b) -> p a b", a=2), opv, AF.Square,
                              r=[PSH[2 * i], PSH[2 * i + 1]], w=[sqj])
                        g.op("dve", lambda E: E.reduce_sum(out=ss[:, :], in_=sqj[:, :], axis=AX.X), r=[sqj], w=[ss])
                        rstd_from_ss(ss[:, :], D, [ss], [ss])
                        t1 = t1r.next()
                        g.stt(t1[:, :].rearrange("p (a b) -> p a b", a=2), opv, ss[:, 0:1],
                              GG2[rr][:, :].rearrange("p (a b) -> p a b", a=2), ALU.mult, ALU.mult,
                              r=[PSH[2 * i], PSH[2 * i + 1], ss, GG2[rr]], w=[t1])
                        x2 = x2r.next()
                        g.tt(x2[:, :], t1[:, :], x1[:, :], ALU.add, r=[t1, x1], w=[x2], e="pool")
                        if l == n_layers - 1:
                            if r0 < T_LAT:
                                g.dma("pool", yout[r0:r0 + 128, :], x2[:, :], r=[x2], w=[H("yout", tt_)])
                        else:
                            g.dma("pool", XS[r0:r0 + 128, :], x2[:, :], r=[x2], w=[H("XS", tt_)])
                g.barrier()
        return _finish(nc, g, xin, yout, es)


def _finish(nc, g, xin, yout, es):
    g.barrier()
    return nc


HSEG = 256
DEBUG_PROG = False


def mixer_phase(nc, g, l, psall, PSH, H, sb, ident, identf, LB, OML, ONORM, NLAM, SUBLN, QHT, ZFT, ZBT, VH, GH, OF, OB, YC,
                QRT, KRT, VD, maskf_in, maskb_in, QRTM):
    with ExitStack() as st:
        gens = [attn_gen(nc, g, l, st, psall, PSH, H, sb, NLAM, SUBLN, QRT, KRT, VD, YC, identf, QRTM),
                hgrn_gen(nc, g, l, st, psall, PSH, H, sb, ident, LB, OML, ONORM, QHT, ZFT, ZBT, VH, GH, OF, OB, YC,
                         maskf_in, maskb_in)]
        prog = [0.0 for _ in gens]
        if DEBUG_PROG:
            print('sbuf remaining before gens start', nc.sbuf_bytes_remaining)
        alive = [True for _ in gens]
        while any(alive):
            i = min((p, k) for k, p in enumerate(prog) if alive[k])[1]
            try:
                first = prog[i] == 0.0
                prog[i] = next(gens[i])
                if DEBUG_PROG and first:
                    print('gen', i, 'allocated; sbuf remaining', nc.sbuf_bytes_remaining)
            except StopIteration:
                alive[i] = False
                if DEBUG_PROG:
                    print("gen", i, "finished at prog", prog)
        g.barrier()


def hgrn_gen(nc, g, l, st, psall, PSH, H, sb, ident, LB, OML, ONORM, QHT, ZFT, ZBT, VH, GH, OF, OB, YC,
             maskf_in, maskb_in):
    SG = HSEG
    NCS = SG // CH
    lat_segs = [(i * SG, SG) for i in range(T_LAT // SG)]
    segs = {0: [(T_LAT, T_CTX)] + lat_segs, 1: [(T_LAT, T_CTX)] + lat_segs[::-1]}
    nsteps = len(segs[0])
    mask = [sb(st, "maskf", [32, 4, 32]), sb(st, "maskb", [32, 4, 32])]
    ones = sb(st, "ones", [128, SG])
    S = [sb(st, f"S{d_}", [128, 4, 128]) for d_ in range(2)]
    Sb = [sb(st, f"Sb{d_}", [128, 4, 128], BF16) for d_ in range(2)]
    qs = Ring([sb(st, f"q{i}", [128, SG]) for i in range(2)])
    A1 = Ring([sb(st, f"A1_{i}", [128, SG]) for i in range(2)])
    A2 = Ring([sb(st, f"A2_{i}", [128, SG]) for i in range(2)])
    A3 = Ring([sb(st, f"A3_{i}", [128, SG]) for i in range(2)])
    A4 = Ring([sb(st, f"A4_{i}", [128, SG]) for i in range(2)])
    A5 = Ring([sb(st, f"A5_{i}", [128, SG]) for i in range(2)])
    keys = [(h, d_) for h in range(4) for d_ in range(2)]
    qt_ = {k: sb(st, f"qt{k[0]}{k[1]}", [128, SG], BF16) for k in keys}
    kt_ = {k: sb(st, f"kt{k[0]}{k[1]}", [128, SG], BF16) for k in keys}
    kh_ = {k: sb(st, f"kh{k[0]}{k[1]}", [128, SG], BF16) for k in keys}
    ktok = {k: sb(st, f"ktok{k[0]}{k[1]}", [32, NCS, 128], BF16) for k in keys}
    ed = {k: sb(st, f"ed{k[0]}{k[1]}", [128, NCS]) for k in keys}
    vck = [sb(st, f"vck{d_}", [32, NCS, HGW], BF16) for d_ in range(2)]
    ost = [sb(st, f"ost{d_}", [32, NCS, HGW]) for d_ in range(2)]
    scs = [Ring([sb(st, f"scs{d_}{i}", [32, 128], BF16) for i in range(2)]) for d_ in range(2)]
    ofr = Ring([sb(st, f"of{i}", [128, HGW]) for i in range(2)])
    obr = Ring([sb(st, f"ob{i}", [128, HGW]) for i in range(2)])
    ggr = Ring([sb(st, f"gg{i}", [128, HGW]) for i in range(2)])
    sqr = Ring([sb(st, f"sq{i}", [128, HGW]) for i in range(2)])
    ssr = Ring([sb(st, f"ss{i}", [128, 4]) for i in range(2)])
    ybr = Ring([sb(st, f"yb{i}", [128, HGW], BF16) for i in range(2)])
    BK = {0: (4, 5, 6), 1: (4, 5, 6)}
    ODST = (OF, OB)
    ZSRC = (ZFT, ZBT)
    TOTAL = float(nsteps + 2)
    ucnt = [0]
    UT = 4470 + 272

    def P():
        ucnt[0] += 1
        return ucnt[0] / UT

    for d_, src in enumerate((maskf_in, maskb_in)):
        for h in range(4):
            g.dma("sp", mask[d_][:, h, :], src[:, :], w=[mask[d_]])
            yield P()
    g.op("pool", lambda E: E.memset(ones[:, :], 1.0), w=[ones])
    yield P()
    for d_ in range(2):
        g.op("pool", lambda E: E.memset(S[d_][:, :, :], 0.0), w=[S[d_]])
        yield P()
        g.op("pool", lambda E: E.memset(Sb[d_][:, :, :], 0.0), w=[Sb[d_]])
        yield P()
    for step in range(nsteps):
        base = float(step)
        for d_ in range(2):
            t0, W = segs[d_][step]
            nc_ = W // CH
            g.dma("sp", vck[d_][:, 0:nc_, :], VH[t0:t0 + W, :].rearrange("(c s) n -> s c n", s=CH),
                  r=[H("VH", k) for k in range(t0 // 128, (t0 + W) // 128)], w=[vck[d_]])
            yield P()
        for ki_, (h, d_) in enumerate(keys):
            t0, W = segs[d_][step]
            nc_ = W // CH
            ld = d_ * 4 + l
            q = qs.next()
            a1 = A1.next(); a2 = A2.next(); a3 = A3.next(); a4 = A4.next(); a5 = A5.next()
            hk = (t0 // 512) * 512 if t0 < T_LAT else T_LAT
            g.dma("sp", q[:, 0:W], QHT[h, :, t0:t0 + W], r=[H("QHT" + str(h), hk)], w=[q])
            yield P()
            g.dma("sp", a1[:, 0:W], ZSRC[d_][h, :, t0:t0 + W], r=[H(ZSRC[d_].name + str(h), hk)], w=[a1])
            yield P()
            g.act(a1[:, 0:W], a1[:, 0:W], AF.Exp, r=[a1], w=[a1], scale=-1.0)
            yield P()
            g.ts(a1[:, 0:W], a1[:, 0:W], 1.0, None, ALU.add, r=[a1], w=[a1])
            yield P()
            g.op("dve", lambda E: E.reciprocal(out=a1[:, 0:W], in_=a1[:, 0:W]), r=[a1], w=[a1])
            yield P()
            g.ts(a1[:, 0:W], a1[:, 0:W], OML[:, h, ld:ld + 1], LB[:, h, ld:ld + 1], ALU.mult, ALU.add,
                 r=[a1, OML, LB], w=[a1])
            yield P()
            g.ts(a2[:, 0:W], a1[:, 0:W], -1.0, 1.0, ALU.mult, ALU.add, r=[a1], w=[a2], e="pool")
            yield P()
            g.act(a1[:, 0:W], a1[:, 0:W], AF.Ln, r=[a1], w=[a1])
            yield P()
            g.op("dve", lambda E: E.tensor_tensor_scan(out=a3[:, 0:W], data0=ones[:, 0:W], data1=a1[:, 0:W],
                                                       initial=0.0, op0=ALU.mult, op1=ALU.add),
                 r=[ones, a1], w=[a3])
            yield P()
            g.tt(a1[:, 0:W], a3[:, 0:W], a1[:, 0:W], ALU.subtract, r=[a3, a1], w=[a1], e="pool")
            yield P()
            Pv = a3[:, 0:W].rearrange("p (c j) -> p c j", j=CH)
            Qv = a1[:, 0:W].rearrange("p (c j) -> p c j", j=CH)
            a4v = a4[:, 0:W].rearrange("p (c j) -> p c j", j=CH)
            a5v = a5[:, 0:W].rearrange("p (c j) -> p c j", j=CH)
            bc = lambda ap: ap.to_broadcast([128, nc_, CH])
            if d_ == 0:
                g.tt(a4v, Pv, bc(Qv[:, :, 0:1]), ALU.subtract, r=[a3, a1], w=[a4])
                yield P()
                g.tt(a5v, bc(Pv[:, :, CH - 1:CH]), Pv, ALU.subtract, r=[a3], w=[a5], e="pool")
                yield P()
            else:
                g.tt(a4v, bc(Pv[:, :, CH - 1:CH]), Qv, ALU.subtract, r=[a3, a1], w=[a4])
                yield P()
                g.tt(a5v, Qv, bc(Qv[:, :, 0:1]), ALU.subtract, r=[a1], w=[a5], e="pool")
                yield P()
            e_ = ed[h, d_]
            g.tt(e_[:, 0:nc_], Pv[:, :, CH - 1], Qv[:, :, 0], ALU.subtract, r=[a3, a1], w=[e_])
            yield P()
            g.act(e_[:, 0:nc_], e_[:, 0:nc_], AF.Exp, r=[e_], w=[e_])
            yield P()
            g.ts(a4[:, 0:W], a4[:, 0:W], -80.0, None, ALU.max, r=[a4], w=[a4])
            yield P()
            g.act(a3[:, 0:W], a4[:, 0:W], AF.Exp, r=[a4], w=[a3])
            yield P()
            g.act(a1[:, 0:W], a4[:, 0:W], AF.Exp, r=[a4], w=[a1], scale=-1.0)
            yield P()
            g.act(a5[:, 0:W], a5[:, 0:W], AF.Exp, r=[a5], w=[a5])
            yield P()
            g.tt(qt_[h, d_][:, 0:W], q[:, 0:W], a3[:, 0:W], ALU.mult, r=[q, a3], w=[qt_[h, d_]])
            yield P()
            g.tt(kt_[h, d_][:, 0:W], a2[:, 0:W], a1[:, 0:W], ALU.mult, r=[a2, a1], w=[kt_[h, d_]], e="pool")
            yield P()
            g.tt(kh_[h, d_][:, 0:W], a2[:, 0:W], a5[:, 0:W], ALU.mult, r=[a2, a5], w=[kh_[h, d_]])
            yield P()
            for c4 in range(0, nc_, 4):
                pb = 4
                for cc in range(4):
                    c = c4 + cc
                    g.mm(psall[0:32, pb, cc * 128:(cc + 1) * 128], lhsT=kh_[h, d_][:, c * CH:(c + 1) * CH],
                         rhs=ident[:, :], start=True, stop=True, r=[kh_[h, d_], ident], w=[PSH[pb]], inc=(cc == 3))
                g.cp(ktok[h, d_][:, c4:c4 + 4, :], psall[0:32, pb, :].rearrange("p (c k) -> p c k", c=4),
                     r=[PSH[pb]], w=[ktok[h, d_]], e="dve")
                yield P()
        ncs = [segs[d_][step][1] // CH for d_ in range(2)]
        for ci in range(max(ncs)):
            for d_ in range(2):
                nc_ = ncs[d_]
                if ci >= nc_:
                    continue
                c = ci if d_ == 0 else nc_ - 1 - ci
                cs = slice(c * CH, (c + 1) * CH)
                bx, by, bz = BK[d_]
                for h in range(4):
                    g.mm(psall[0:32, bx, h * 32:(h + 1) * 32], lhsT=kt_[h, d_][:, cs], rhs=qt_[h, d_][:, cs],
                         start=True, stop=True, r=[kt_[h, d_], qt_[h, d_]], w=[PSH[bx]], inc=(h == 3))
                for h in range(4):
                    g.mm(psall[:, by, h * 128:(h + 1) * 128], lhsT=ktok[h, d_][:, c, :],
                         rhs=vck[d_][:, c, h * 128:(h + 1) * 128], start=True, stop=True,
                         r=[ktok[h, d_], vck[d_]], w=[PSH[by]], inc=(h == 3))
                yield P()
                sc = scs[d_].next()
                g.tt(sc[:, :], psall[0:32, bx, 0:128], mask[d_][:, :, :].rearrange("p h t -> p (h t)"), ALU.mult,
                     r=[PSH[bx], mask[d_]], w=[sc])
                for h in range(4):
                    g.stt(S[d_][:, h, :], S[d_][:, h, :], ed[h, d_][:, c:c + 1], psall[:, by, h * 128:(h + 1) * 128],
                          ALU.mult, ALU.add, r=[S[d_], ed[h, d_], PSH[by]], w=[S[d_]])
                yield P()
                for h in range(4):
                    g.mm(psall[0:32, bz, h * 128:(h + 1) * 128], lhsT=qt_[h, d_][:, cs], rhs=Sb[d_][:, h, :],
                         start=True, stop=False, r=[qt_[h, d_], Sb[d_]], w=[PSH[bz]], inc=False)
                    g.mm(psall[0:32, bz, h * 128:(h + 1) * 128], lhsT=sc[:, h * 32:(h + 1) * 32],
                         rhs=vck[d_][:, c, h * 128:(h + 1) * 128], start=False, stop=True,
                         r=[sc, vck[d_]], w=[PSH[bz]], inc=(h == 3))
                yield P()
                g.cp(ost[d_][:, c, :], psall[0:32, bz, :], r=[PSH[bz]], w=[ost[d_]], e="dve")
                g.cp(Sb[d_][:, :, :], S[d_][:, :, :], r=[S[d_]], w=[Sb[d_]], e="pool")
                yield P()
        for d_ in range(2):
            t0, W = segs[d_][step]
            nc_ = W // CH
            g.dma("pool", ODST[d_][t0:t0 + W, :].rearrange("(c s) v -> s c v", s=CH),
                  ost[d_][:, 0:nc_, :], r=[ost[d_]],
                  w=[H(ODST[d_].name, k) for k in range(t0 // 128, (t0 + W) // 128)])
            yield P()
    for tt_ in range(NT):
        r0 = tt_ * 128
        of = ofr.next(); ob = obr.next(); gg = ggr.next(); sq = sqr.next(); ss = ssr.next(); yb = ybr.next()
        g.dma("sp", of[:, :], OF[r0:r0 + 128, :], r=[H("OF", tt_)], w=[of])
        yield P()
        g.dma("sp", ob[:, :], OB[r0:r0 + 128, :], r=[H("OB", tt_)], w=[ob])
        yield P()
        g.dma("sp", gg[:, :], GH[r0:r0 + 128, :], r=[H("GH", tt_)], w=[gg])
        yield P()
        g.tt(of[:, :], of[:, :], ob[:, :], ALU.add, r=[of, ob], w=[of], e="pool")
        yield P()
        g.tt(sq[:, :], of[:, :], of[:, :], ALU.mult, r=[of], w=[sq], e="pool")
        yield P()
        g.op("dve", lambda E: E.tensor_reduce(out=ss[:, :], in_=sq[:, :].rearrange("p (h v) -> p h v", h=4),
                                              axis=AX.X, op=ALU.add), r=[sq], w=[ss])
        yield P()
        g.ts(ss[:, :], ss[:, :], 1.0 / 128, EPS, ALU.mult, ALU.add, r=[ss], w=[ss])
        yield P()
        g.act(ss[:, :], ss[:, :], AF.Sqrt, r=[ss], w=[ss])
        yield P()
        g.op("dve", lambda E: E.reciprocal(out=ss[:, :], in_=ss[:, :]), r=[ss], w=[ss])
        yield P()
        ofv = of[:, :].rearrange("p (h v) -> p h v", h=4)
        g.tt(ofv, ofv, ss[:, :].unsqueeze(2).to_broadcast([128, 4, 128]), ALU.mult, r=[of, ss], w=[of])
        yield P()
        g.tt(ofv, ofv, ONORM[:, l * 128:(l + 1) * 128].unsqueeze(1).to_broadcast([128, 4, 128]), ALU.mult,
             r=[of, ONORM], w=[of], e="pool")
        yield P()
        g.tt(yb[:, :], of[:, :], gg[:, :], ALU.mult, r=[of, gg], w=[yb])
        yield P()
        g.dma("pool", YC[r0:r0 + 128, 0:HGW], yb[:, :], r=[yb], w=[H("YCh", tt_)])
        yield P()


def attn_gen(nc, g, l, st, psall, PSH, H, sb, NLAM, SUBLN, QRT, KRT, VD, YC, identf, QRTM):
    kr = sb(st, "kr", [128, 2, N], BF16)
    qmr = Ring([sb(st, f"qm{i}", [128, 2, 4, 512], BF16) for i in range(2)])
    vp = sb(st, "vp", [128, NT, 4, 65], BF16)
    ptr = Ring([sb(st, f"pt{i}", [128, 512], BF16) for i in range(3)])
    accS = [Ring([sb(st, f"accS{m}{i}", [65, 512]) for i in range(2)]) for m in range(2)]
    rsr = Ring([sb(st, f"rs{i}", [128, 8]) for i in range(2)])
    tmr = Ring([sb(st, f"tm{i}", [128, 4, 64]) for i in range(2)])
    O2r = Ring([sb(st, f"O2_{i}", [128, 16, 64]) for i in range(2)])
    SQr = Ring([sb(st, f"SQ_{i}", [128, 16, 64]) for i in range(1)])
    SSr = Ring([sb(st, f"SS_{i}", [128, 16]) for i in range(2)])
    ysr = Ring([sb(st, f"ys{i}", [128, 4, 256], BF16) for i in range(2)])
    allblk = [b * 512 for b in range(8)] + [T_LAT]
    for ti in range(2):
        g.dma("sp", kr[:, ti, :], KRT[ti, :, :], r=[H("KRT" + str(ti), t0) for t0 in allblk], w=[kr])
    g.op("dve", lambda E: E.memset(vp[:, :, :, 64:65], 1.0), w=[vp])
    vph = [T() for _ in range(NT)]
    qmb = {}

    def load_qm(bi):
        q0, W, _ = qblocks[bi]
        qm = qmr.next()
        for ti in range(2):
            g.dma("sp", qm[:, ti, :, 0:W], QRTM[ti, :, :, q0:q0 + W].rearrange("g p n -> p g n"),
                  r=[H("QRTM" + str(ti) + str(gq), q0) for gq in range(4)] + [H("QRTMz")], w=[qm])
        qmb[bi] = qm

    for kt in range(NT):
        g.dma("sp", vp[:, kt, :, 0:64], VD[kt * 128:(kt + 1) * 128, :].rearrange("p (h d) -> p h d", h=4),
              r=[H("VD", kt), vp.h], w=[vph[kt]])
    qblocks = [(b * 512, 512, list(range(32)) + [32, 33]) for b in range(8)] + [(T_LAT, 256, [32, 33])]
    load_qm(0)
    yield 0.0
    spb = Ring([0, 1])
    acc = (2, 3)
    TB = 7
    LA = 1
    steps = []
    for bi, (q0, W, kts) in enumerate(qblocks):
        for h in range(4):
            for ki, kt in enumerate(kts):
                for m in range(2):
                    steps.append((bi, h, ki, kt, m))
    NS = float(len(steps))
    state = {}

    def issue_score(i):
        bi, h, ki, kt, m = steps[i]
        q0, W, kts = qblocks[bi]
        ti = h // 2
        gq = 2 * (h % 2) + m
        pb = spb.next()
        if bi not in qmb:
            load_qm(bi)
        if h == 0 and ki == 0 and m == 0 and bi + 1 < len(qblocks) and (bi + 1) not in qmb:
            load_qm(bi + 1)
        qm = qmb[bi]
        g.mm(psall[:, pb, 0:W], lhsT=kr[:, ti, kt * 128:(kt + 1) * 128], rhs=qm[:, ti, gq, 0:W],
             start=True, stop=True, r=[kr, qm], w=[PSH[pb]])
        state[i] = pb

    for i in range(min(LA, len(steps))):
        issue_score(i)
    cur = {}
    for i, (bi, h, ki, kt, m) in enumerate(steps):
        q0, W, kts = qblocks[bi]
        nq = W // 128
        if ki == 0 and m == 0 and h == 0:
            cur["O2"] = O2r.next(); cur["SQ"] = SQr.next(); cur["SS"] = SSr.next(); cur["ys"] = ysr.next()
        if i + LA < len(steps):
            issue_score(i + LA)
        pb = state.pop(i)
        pt = ptr.next()
        g.act(pt[:, 0:W], psall[:, pb, 0:W], AF.Exp, r=[PSH[pb]], w=[pt], scale=ATT_SCALE)
        g.mm(psall[0:65, acc[m], 0:W], lhsT=vp[:, kt, h, :], rhs=pt[:, 0:W], start=(ki == 0), stop=(ki == len(kts) - 1),
             r=[pt, vph[kt]], w=[PSH[acc[m]]])
        if ki == len(kts) - 1 and m == 1:
            O2 = cur["O2"]; SQ = cur["SQ"]; SS = cur["SS"]; ys = cur["ys"]
            yield (i + 0.5) / NS
            aS = [accS[0].next(), accS[1].next()]
            for mm_ in range(2):
                g.cp(aS[mm_][:, 0:W], psall[0:65, acc[mm_], 0:W], r=[PSH[acc[mm_]]], w=[aS[mm_]], e="dve")
            rs = rsr.next()
            tm = tmr.next()
            O2h = O2[:, :, :].rearrange("p (q h) d -> p q h d", h=4)[:, 0:nq, h, :]
            for mm_ in range(2):
                yield (i + 0.6 + 0.2 * mm_) / NS
                for qt in range(nq):
                    g.mm(psall[:, TB, qt * 65:(qt + 1) * 65], lhsT=aS[mm_][0:65, qt * 128:(qt + 1) * 128],
                         rhs=identf[0:65, 0:65], start=True, stop=True, r=[aS[mm_], identf], w=[PSH[TB]],
                         inc=(qt == nq - 1))
                yield (i + 0.7 + 0.2 * mm_) / NS
                tv = psall[:, TB, 0:nq * 65].rearrange("p (q c) -> p q c", c=65)
                g.op("dve", lambda E: E.reciprocal(out=rs[:, mm_ * 4:mm_ * 4 + nq].unsqueeze(2), in_=tv[:, :, 64:65]),
                     r=[PSH[TB]], w=[rs])
                if mm_ == 0:
                    g.tt(O2h, tv[:, :, 0:64], rs[:, 0:nq].unsqueeze(2).to_broadcast([128, nq, 64]), ALU.mult,
                         r=[PSH[TB], rs], w=[O2])
                else:
                    g.ts(rs[:, 4:4 + nq], rs[:, 4:4 + nq], NLAM[:, l:l + 1], None, ALU.mult, r=[rs, NLAM], w=[rs])
                    g.tt(tm[:, 0:nq, :], tv[:, :, 0:64], rs[:, 4:4 + nq].unsqueeze(2).to_broadcast([128, nq, 64]), ALU.mult,
                         r=[PSH[TB], rs], w=[tm])
                    g.tt(O2h, O2h, tm[:, 0:nq, :], ALU.add, r=[O2, tm], w=[O2], e="pool")
            if h == 3:
                n16 = nq * 4
                g.tt(SQ[:, 0:n16, :], O2[:, 0:n16, :], O2[:, 0:n16, :], ALU.mult, r=[O2], w=[SQ], e="pool")
                g.op("dve", lambda E: E.tensor_reduce(out=SS[:, 0:n16], in_=SQ[:, 0:n16, :], axis=AX.X, op=ALU.add),
                     r=[SQ], w=[SS])
                g.ts(SS[:, 0:n16], SS[:, 0:n16], 1.0 / 64, EPS, ALU.mult, ALU.add, r=[SS], w=[SS])
                g.act(SS[:, 0:n16], SS[:, 0:n16], AF.Sqrt, r=[SS], w=[SS])
                g.op("dve", lambda E: E.reciprocal(out=SS[:, 0:n16], in_=SS[:, 0:n16]), r=[SS], w=[SS])
                g.tt(O2[:, 0:n16, :], O2[:, 0:n16, :], SS[:, 0:n16].unsqueeze(2).to_broadcast([128, n16, 64]), ALU.mult,
                     r=[O2, SS], w=[O2])
                g.tt(ys[:, 0:nq, :].rearrange("p q (h d) -> p (q h) d", h=4), O2[:, 0:n16, :],
                     SUBLN[:, l * 64:(l + 1) * 64].unsqueeze(1).to_broadcast([128, n16, 64]), ALU.mult,
                     r=[O2, SUBLN], w=[ys], e="pool")
                g.dma("pool", YC[q0:q0 + W, 512:768].rearrange("(q p) c -> p q c", p=128), ys[:, 0:nq, :], r=[ys],
                      w=[H("YCa", k) for k in range(q0 // 128, (q0 + W) // 128)])
        yield (i + 1) / NS


def fft_gen(nc, g, l, st, psall, PSH, H, sb, FAB, YC, dftC, dftS, dftCc, dftSc, banks):
    ab = sb(st, "ab", [128, NT, 512], BF16)
    g.dma("sp", ab[:, :, :], FAB[:, :].rearrange("(t p) c -> p t c", p=128), r=[H("FAB", k) for k in range(NT)], w=[ab])
    ctr = Ring([sb(st, f"ct{i}", [128, 32, 128], BF16) for i in range(2)])
    str_ = Ring([sb(st, f"st{i}", [128, 32, 128], BF16) for i in range(2)])
    ybr = Ring([sb(st, f"yb{i}", [128, 256], BF16) for i in range(2)])
    pbr = Ring(list(banks))
    for j in range(NT):
        ct = ctr.next()
        sn = str_.next()
        if j < 32:
            nti, tb = 32, 0
            g.dma("sp", ct[:, :, :], dftC[j].rearrange("p (t c) -> p t c", t=32), w=[ct])
            g.dma("sp", sn[:, :, :], dftS[j].rearrange("p (t c) -> p t c", t=32), w=[sn])
        else:
            nti, tb = 2, 32
            g.dma("sp", ct[:, 0:2, :], dftCc[j - 32].rearrange("p (t c) -> p t c", t=2), w=[ct])
            g.dma("sp", sn[:, 0:2, :], dftSc[j - 32].rearrange("p (t c) -> p t c", t=2), w=[sn])
        pb = pbr.next()
        for ti in range(nti):
            g.mm(psall[:, pb, 0:256], lhsT=ct[:, ti, :], rhs=ab[:, tb + ti, 0:256], start=(ti == 0), stop=False,
                 r=[ct, ab], w=[PSH[pb]], inc=False)
            g.mm(psall[:, pb, 0:256], lhsT=sn[:, ti, :], rhs=ab[:, tb + ti, 256:512], start=False, stop=(ti == nti - 1),
                 r=[sn, ab], w=[PSH[pb]], inc=(ti == nti - 1))
        yb = ybr.next()
        g.cp(yb[:, :], psall[:, pb, 0:256], r=[PSH[pb]], w=[yb], e="act")
        g.dma("pool", YC[j * 128:(j + 1) * 128, 768:1024], yb[:, :], r=[yb], w=[H("YCf", j)])
        yield j


_CONST = {}


def _consts():
    if _CONST:
        return _CONST
    bf = ml_dtypes.bfloat16
    inv_freq = 1.0 / (10000.0 ** (np.arange(0, 16, 2, dtype=np.float32) / 16.0))
    t = np.arange(T_LAT)
    pos = np.stack([t // 64, t % 64], axis=0).astype(np.float32)
    ang = pos[:, None, :] * inv_freq[None, :, None].astype(np.float32)
    cos = np.cos(ang).astype(np.float32)
    sin = np.sin(ang).astype(np.float32)
    C = np.ones((128, N), np.float32)
    S = np.zeros((128, N), np.float32)
    for p in range(128):
        d = p % 32
        axis, half, F = d // 16, (d // 8) % 2, d % 8
        C[p, :T_LAT] = cos[axis, F]
        S[p, :T_LAT] = sin[axis, F] * (-1.0 if half == 0 else 1.0)
    _CONST["ropeC"] = C
    _CONST["ropeS"] = S

    def dft_tables(T_, scale):
        tt = np.arange(T_, dtype=np.int64)
        prod = (tt[:, None] * tt[None, :]) % T_
        angm = 2.0 * np.pi * prod.astype(np.float64) / T_
        Cm = (np.cos(angm) * scale).astype(np.float32)
        Sm = (-np.sin(angm) * scale).astype(np.float32)
        nt = T_ // 128

        def blk(M):
            return np.ascontiguousarray(M.reshape(nt, 128, nt, 128).transpose(2, 1, 0, 3)).reshape(nt, 128, nt * 128).astype(bf)
        return blk(Cm), blk(Sm)
    _CONST["dftC"], _CONST["dftS"] = dft_tables(T_LAT, 1.0 / 64.0)
    _CONST["dftCc"], _CONST["dftSc"] = dft_tables(T_CTX, 1.0 / 16.0)
    cc = np.arange(64)
    a64 = 2.0 * np.pi * ((cc[:, None] * cc[None, :]) % 64) / 64.0
    chC = np.zeros((128, 128), np.float32)
    chS = np.zeros((128, 128), np.float32)
    for gI in range(2):
        chC[gI * 64:(gI + 1) * 64, gI * 64:(gI + 1) * 64] = np.cos(a64) / 8.0
        chS[gI * 64:(gI + 1) * 64, gI * 64:(gI + 1) * 64] = np.sin(a64) / 8.0
    _CONST["chC"] = chC.astype(bf)
    _CONST["chS"] = chS.astype(bf)
    _CONST["ident"] = np.eye(128, dtype=np.float32).astype(bf)
    _CONST["identf"] = np.eye(128, dtype=np.float32)
    s_i = np.arange(32)
    _CONST["maskf"] = (s_i[:, None] <= s_i[None, :]).astype(np.float32)
    _CONST["maskb"] = (s_i[:, None] >= s_i[None, :]).astype(np.float32)
    return _CONST


def _swap_cols():
    idx = np.arange(256)
    d = idx % 32
    half = (d // 8) % 2
    return idx + np.where(half == 0, 8, -8)


def prepare_inputs(x, c, ctx, c_ctx, w_mod, b_mod, norm_g, w_in, w_out, hg_lb_logits, hg_onorm,
                   da_lambda, da_subln, w_ffn_in, w_ffn_out):
    f = lambda a: np.ascontiguousarray(np.asarray(a, dtype=np.float32))
    x, c, ctx, c_ctx = f(x), f(c), f(ctx), f(c_ctx)
    w_in = f(w_in)
    sw = _swap_cols()
    w_in_x = np.concatenate([w_in, w_in[:, :, 2560 + sw], w_in[:, :, 2816 + sw]], axis=2)
    shared = {
        "w_mod": f(w_mod), "b_mod": f(b_mod), "norm_g": f(norm_g).reshape(DEPTH, 4 * D),
        "w_in": np.ascontiguousarray(w_in_x), "w_out": f(w_out),
        "lb_logits": f(hg_lb_logits).reshape(8, HGW), "hg_onorm": f(hg_onorm).reshape(1, -1),
        "da_lambda": f(da_lambda).reshape(1, -1), "da_subln": f(da_subln).reshape(1, -1),
        "w_ffn_in": f(w_ffn_in), "w_ffn_out": f(w_ffn_out),
    }
    shared.update(_consts())
    in_maps = []
    for core in range(8):
        b = core % 4
        m = dict(shared)
        m["xin"] = np.ascontiguousarray(np.concatenate([x[b], ctx[b]], axis=0))
        m["cond"] = np.ascontiguousarray(np.stack([c[b], c_ctx], axis=0))
        in_maps.append(m)
    return in_maps


_NC_CACHE = {}


def kernel(x, c, ctx, c_ctx, w_mod, b_mod, norm_g, w_in, w_out, hg_lb_logits, hg_onorm,
           da_lambda, da_subln, w_ffn_in, w_ffn_out):
    in_maps = prepare_inputs(x, c, ctx, c_ctx, w_mod, b_mod, norm_g, w_in, w_out, hg_lb_logits, hg_onorm,
                             da_lambda, da_subln, w_ffn_in, w_ffn_out)
    nc = build()
    res = run_bass_kernel_spmd(nc, in_maps, core_ids=list(range(8)))
    out = np.stack([np.asarray(res.results[b]["yout"], dtype=np.float32) for b in range(4)], axis=0)
    return out
```

```python
import math
from contextlib import ExitStack

import numpy as np
import ml_dtypes
import concourse.bass as bass
import concourse.mybir as mybir
from concourse.bass_utils import run_bass_kernel_spmd

F32 = mybir.dt.float32
BF16 = mybir.dt.bfloat16
AF = mybir.ActivationFunctionType
ALU = mybir.AluOpType
AX = mybir.AxisListType

D = 1024
T_LAT = 4096
T_CTX = 256
N = T_LAT + T_CTX
NT = N // 128
DEPTH = 4
HGW = 512
INW = 3584
INX = INW + 512
FFH = 2816
EPS = 1e-6
CH = 32
NCH = N // CH
SEG = 512
LAM_INIT = [0.8 - 0.6 * math.exp(-0.3 * l) for l in range(DEPTH)]
ATT_SCALE = 32 ** -0.5

NDS = 24
NDQ = {"sp": 12, "pool": 8, "act": 2}


class T:
    __slots__ = ("w", "r")

    def __init__(self):
        self.w = None
        self.r = {}


class Buf:
    def __init__(self, t):
        self.t = t
        self.h = T()

    def __getitem__(self, k):
        return self.t[k]


class Ring:
    def __init__(self, bufs):
        self.b = bufs
        self.i = 0

    def next(self):
        b = self.b[self.i]
        self.i = (self.i + 1) % len(self.b)
        return b


class G:
    def __init__(self, nc, es):
        self.nc = nc
        self.E = {"pe": nc.tensor, "act": nc.scalar, "dve": nc.vector, "pool": nc.gpsimd, "sp": nc.sync}
        self.sem = {e: es.enter_context(nc.semaphore("c_" + e)) for e in ("pe", "act", "dve", "pool")}
        self.cnt = dict.fromkeys(self.sem, 0)
        self.seen = {e: {} for e in self.E}
        self.dsem = {q: [es.enter_context(nc.semaphore(f"d{q}{i}")) for i in range(NDQ[q])] for q in ("sp", "pool", "act")}
        self.dval = {q: [0] * NDQ[q] for q in self.dsem}
        self.dnext = dict.fromkeys(self.dsem, 0)
        self.nins = 0

    def _wait(self, e, deps):
        need = {}
        for (s, v) in deps:
            if need.get(s, 0) < v:
                need[s] = v
        seen = self.seen[e]
        own = self.sem.get(e)
        for s, v in need.items():
            if seen.get(s, 0) >= v:
                continue
            if e == "pe" and s is own:
                continue
            self.E[e].wait_ge(s, v)
            seen[s] = v

    @staticmethod
    def _deps(r, w):
        deps = []
        for t in r:
            if t.w:
                deps.append(t.w)
        for t in w:
            if t.w:
                deps.append(t.w)
            deps.extend(t.r.items())
        return deps

    @staticmethod
    def _mark(r, w, s, v):
        for t in r:
            if t.r.get(s, 0) < v:
                t.r[s] = v
        for t in w:
            t.w = (s, v)
            t.r = {}

    def op(self, e, fn, r=(), w=(), inc=True):
        r = [x.h if isinstance(x, Buf) else x for x in r]
        w = [x.h if isinstance(x, Buf) else x for x in w]
        self._wait(e, self._deps(r, w))
        ins = fn(self.E[e])
        self.nins += 1
        s = self.sem[e]
        if inc:
            self.cnt[e] += 1
            v = self.cnt[e]
            ins.then_inc(s, 1)
        else:
            v = self.cnt[e] + 1
        self._mark(r, w, s, v)
        return ins

    def dma(self, q, out, in_, r=(), w=(), **kw):
        r = [x.h if isinstance(x, Buf) else x for x in r]
        w = [x.h if isinstance(x, Buf) else x for x in w]
        i = self.dnext[q]
        self.dnext[q] = (i + 1) % NDQ[q]
        s = self.dsem[q][i]
        deps = self._deps(r, w)
        if self.dval[q][i]:
            deps.append((s, self.dval[q][i]))
        self._wait(q, deps)
        ins = self.E[q].dma_start(out=out, in_=in_, **kw)
        self.nins += 1
        self.dval[q][i] += 16
        v = self.dval[q][i]
        ins.then_inc(s, 16)
        self._mark(r, w, s, v)
        return ins

    def barrier(self, engines=("pe", "act", "dve", "pool", "sp")):
        deps = [(self.sem[x], self.cnt[x]) for x in self.sem if self.cnt[x]]
        for q in self.dsem:
            for i in range(NDQ[q]):
                if self.dval[q][i]:
                    deps.append((self.dsem[q][i], self.dval[q][i]))
        for e in engines:
            self._wait(e, deps)

    def mm(self, out, lhsT, rhs, start, stop, r=(), w=(), inc=True, **kw):
        return self.op("pe", lambda E: E.matmul(out, lhsT=lhsT, rhs=rhs, start=start, stop=stop, **kw), r, w, inc)

    def act(self, out, in_, func, r=(), w=(), e="act", **kw):
        return self.op(e, lambda E: E.activation(out=out, in_=in_, func=func, **kw), r, w)

    def tt(self, out, in0, in1, op, r=(), w=(), e="dve"):
        return self.op(e, lambda E: E.tensor_tensor(out=out, in0=in0, in1=in1, op=op), r, w)

    def ts(self, out, in0, s1, s2, op0, op1=None, r=(), w=(), e="dve"):
        if op1 is None:
            return self.op(e, lambda E: E.tensor_scalar(out=out, in0=in0, scalar1=s1, scalar2=None, op0=op0), r, w)
        return self.op(e, lambda E: E.tensor_scalar(out=out, in0=in0, scalar1=s1, scalar2=s2, op0=op0, op1=op1), r, w)

    def stt(self, out, in0, scalar, in1, op0, op1, r=(), w=(), e="dve"):
        return self.op(e, lambda E: E.scalar_tensor_tensor(out=out, in0=in0, scalar=scalar, in1=in1, op0=op0, op1=op1), r, w)

    def cp(self, out, in_, r=(), w=(), e="dve"):
        if e == "act":
            return self.op(e, lambda E: E.copy(out=out, in_=in_), r, w)
        return self.op(e, lambda E: E.tensor_copy(out=out, in_=in_), r, w)


def build(n_layers=DEPTH, stop_after=None, debug=False):
    nc = bass.Bass("TRN2", target_bir_lowering=False)
    skind = "ExternalOutput" if debug else "Internal"
    LD = n_layers if debug else DEPTH

    def din(name, shape, dt=F32):
        return nc.dram_tensor(name, list(shape), dt, kind="ExternalInput").ap()

    def dscr(name, shape, dt=F32):
        return nc.dram_tensor(name, list(shape), dt, kind=skind).ap()

    xin = din("xin", [N, D])
    cond = din("cond", [2, D])
    w_mod = din("w_mod", [LD, D, 6 * D])
    b_mod = din("b_mod", [LD, 6 * D])
    norm_g = din("norm_g", [LD, 4 * D])
    w_in = din("w_in", [LD, D, INX])
    w_out = din("w_out", [LD, D, D])
    lb_logits = din("lb_logits", [8, HGW])
    hg_onorm = din("hg_onorm", [1, DEPTH * 128])
    da_lambda = din("da_lambda", [1, DEPTH * 4 * 32])
    da_subln = din("da_subln", [1, DEPTH * 64])
    w_ffn_in = din("w_ffn_in", [LD, D, 2 * FFH])
    w_ffn_out = din("w_ffn_out", [LD, FFH, D])
    ropeC = din("ropeC", [128, N])
    ropeS = din("ropeS", [128, N])
    dftC = din("dftC", [32, 128, 32 * 128], BF16)
    dftS = din("dftS", [32, 128, 32 * 128], BF16)
    dftCc = din("dftCc", [2, 128, 2 * 128], BF16)
    dftSc = din("dftSc", [2, 128, 2 * 128], BF16)
    chC = din("chC", [128, 128], BF16)
    chS = din("chS", [128, 128], BF16)
    ident_in = din("ident", [128, 128], BF16)
    identf_in = din("identf", [128, 128], F32)
    maskf_in = din("maskf", [32, 32], F32)
    maskb_in = din("maskb", [32, 32], F32)
    yout = nc.dram_tensor("yout", [T_LAT, D], F32, kind="ExternalOutput").ap()

    XS = dscr("XS", [N, D])
    X1 = dscr("X1", [N, D])
    MODV = dscr("MODV", [DEPTH * 2 * 6, D])
    QHT = dscr("QHT", [4, 128, N])
    ZFT = dscr("ZFT", [4, 128, N])
    ZBT = dscr("ZBT", [4, 128, N])
    VH = dscr("VH", [N, HGW], BF16)
    GH = dscr("GH", [N, HGW])
    QRT = dscr("QRT", [2, 128, N], BF16)
    KRT = dscr("KRT", [2, 128, N], BF16)
    VD = dscr("VD", [N, 256], BF16)
    FAB = dscr("FAB", [N, 512], BF16)
    OF = dscr("OF", [N, HGW])
    OB = dscr("OB", [N, HGW])
    YC = dscr("YC", [N, D], BF16)
    H2T = dscr("H2T", [8, 128, N], BF16)
    QRTM = dscr("QRTM", [2, 4, 128, N], BF16)

    dh = {}

    def H(name, key=0):
        k = (name, key)
        if k not in dh:
            dh[k] = T()
        return dh[k]

    def HA(name, keys):
        return [H(name, k) for k in keys]

    es = ExitStack()
    with es:
        g = G(nc, es)
        psall = es.enter_context(nc.psum_tensor("psall", [128, 8, 512], F32))
        PSH = [T() for _ in range(8)]

        uid = [0]

        def sb(st, name, shape, dt=F32):
            uid[0] += 1
            return Buf(st.enter_context(nc.sbuf_tensor(f"s{uid[0]}_{name}", list(shape), dt)))

        ident = sb(es, "ident", [128, 128], BF16)
        identf = sb(es, "identf", [128, 128], F32)
        LB = sb(es, "LB", [128, 4, 8])
        OML = sb(es, "OML", [128, 4, 8])
        LAM = sb(es, "LAM", [128, DEPTH])
        NLAM = sb(es, "NLAM", [128, DEPTH])
        ONORM = sb(es, "ONORM", [128, DEPTH * 128])
        SUBLN = sb(es, "SUBLN", [128, DEPTH * 64])
        g.dma("sp", ident[:, :], ident_in[:, :], w=[ident])
        g.dma("sp", identf[:, :], identf_in[:, :], w=[identf])
        g.dma("sp", ONORM[:, :], hg_onorm[0:1, :].partition_broadcast(128), w=[ONORM])
        g.dma("sp", SUBLN[:, :], da_subln[0:1, :].partition_broadcast(128), w=[SUBLN])
        with ExitStack() as stz:
            zt = sb(stz, "zt", [128, N], BF16)
            g.op("pool", lambda E: E.memset(zt[:, :], 0.0), w=[zt])
            for ti in range(2):
                for gq in range(4):
                    g.dma("sp", QRTM[ti, gq, :, :], zt[:, :], r=[zt], w=[H("QRTMz")])
            g.barrier()

        with ExitStack() as st:
            lg = sb(st, "lg", [8, HGW])
            ex = sb(st, "ex", [128, 4, 8])
            sm = sb(st, "sm", [128, 4, 2])
            dl = sb(st, "dl", [128, DEPTH * 4 * 32])
            pr = sb(st, "pr", [128, DEPTH * 2 * 32])
            pe2 = sb(st, "pe2", [128, DEPTH * 2])
            g.dma("sp", lg[:, :], lb_logits[:, :], w=[lg])
            for h in range(4):
                g.mm(psall[:, 0, h * 8:(h + 1) * 8], lhsT=lg[0:8, h * 128:(h + 1) * 128], rhs=identf[0:8, 0:8],
                     start=True, stop=True, r=[lg, identf], w=[PSH[0]])
            g.act(ex[:, :, :], psall[:, 0, 0:32].rearrange("p (h x) -> p h x", h=4), AF.Exp, r=[PSH[0]], w=[ex])
            exv = ex[:, :, :].rearrange("p h (d l) -> p h d l", d=2)
            g.op("dve", lambda E: E.tensor_reduce(out=sm[:, :, :], in_=exv, axis=AX.X, op=ALU.add), r=[ex], w=[sm])
            g.op("dve", lambda E: E.reciprocal(out=sm[:, :, :], in_=sm[:, :, :]), r=[sm], w=[sm])
            g.tt(exv, exv, sm[:, :, :].unsqueeze(3).to_broadcast([128, 4, 2, 4]), ALU.mult, r=[ex, sm], w=[ex])
            lbv = LB[:, :, :].rearrange("p h (d l) -> p h d l", d=2)
            g.op("dve", lambda E: E.memset(lbv[:, :, :, 0:1], 0.0), w=[LB])
            for l in range(1, 4):
                g.tt(lbv[:, :, :, l:l + 1], lbv[:, :, :, l - 1:l], exv[:, :, :, l:l + 1], ALU.add, r=[ex, LB], w=[LB])
            g.ts(OML[:, :, :], LB[:, :, :], -1.0, 1.0, ALU.mult, ALU.add, r=[LB], w=[OML])
            g.dma("sp", dl[:, :], da_lambda[0:1, :].partition_broadcast(128), w=[dl])
            dlv = dl[:, :].rearrange("p (l a b x) -> p l a b x", l=DEPTH, a=2, b=2)
            prv = pr[:, :].rearrange("p (l a x) -> p l a x", l=DEPTH, a=2)
            g.tt(prv, dlv[:, :, :, 0, :], dlv[:, :, :, 1, :], ALU.mult, r=[dl], w=[pr])
            pe2v = pe2[:, :].rearrange("p (l a) -> p l a", a=2)
            g.op("dve", lambda E: E.tensor_reduce(out=pe2v, in_=prv, axis=AX.X, op=ALU.add), r=[pr], w=[pe2])
            g.act(pe2[:, :], pe2[:, :], AF.Exp, r=[pe2], w=[pe2])
            g.tt(LAM[:, :], pe2v[:, :, 0], pe2v[:, :, 1], ALU.subtract, r=[pe2], w=[LAM])
            for l in range(DEPTH):
                g.ts(LAM[:, l:l + 1], LAM[:, l:l + 1], float(LAM_INIT[l]), None, ALU.add, r=[LAM], w=[LAM])
            g.ts(NLAM[:, :], LAM[:, :], -1.0, None, ALU.mult, r=[LAM], w=[NLAM])
            for l in range(DEPTH):
                g.ts(SUBLN[:, l * 64:(l + 1) * 64], SUBLN[:, l * 64:(l + 1) * 64], float(1.0 - LAM_INIT[l]), None,
                     ALU.mult, r=[SUBLN], w=[SUBLN])

            cT = sb(st, "cT", [128, 8, 2])
            cstage = sb(st, "cstage", [2, D])
            g.dma("sp", cstage[:, :], cond[:, :], w=[cstage])
            g.act(cstage[:, :], cstage[:, :], AF.Silu, r=[cstage], w=[cstage])
            for kc in range(8):
                g.mm(psall[:, 1, kc * 2:(kc + 1) * 2], lhsT=cstage[0:2, kc * 128:(kc + 1) * 128], rhs=identf[0:2, 0:2],
                     start=True, stop=True, r=[cstage, identf], w=[PSH[1]])
            g.cp(cT[:, :, :], psall[:, 1, 0:16].rearrange("p (k r) -> p k r", r=2), r=[PSH[1]], w=[cT])
            wring = Ring([sb(st, f"wm{i}", [128, 8, 512]) for i in range(2)])
            modsb = sb(st, "modsb", [2, 6 * D])
            bmsb = sb(st, "bmsb", [2, 6 * D])
            ngsb = sb(st, "ngsb", [2, 4 * D])
            vst = sb(st, "vst", [2, 6 * D])
            pi = 2
            for l in range(n_layers):
                g.dma("sp", bmsb[:, :], b_mod[l:l + 1, :].partition_broadcast(2), w=[bmsb])
                g.dma("sp", ngsb[:, :], norm_g[l:l + 1, :].partition_broadcast(2), w=[ngsb])
                wv = w_mod[l].rearrange("(kc p) n -> p kc n", p=128)
                for j in range(12):
                    wb = wring.next()
                    g.dma("sp", wb[:, :, :], wv[:, :, j * 512:(j + 1) * 512], w=[wb])
                    pb = 2 + (pi % 2)
                    pi += 1
                    for kc in range(8):
                        g.mm(psall[0:2, pb, :], lhsT=cT[:, kc, :], rhs=wb[:, kc, :], start=(kc == 0), stop=(kc == 7),
                             r=[cT, wb], w=[PSH[pb]], inc=(kc == 7))
                    g.tt(modsb[:, j * 512:(j + 1) * 512], psall[0:2, pb, :], bmsb[:, j * 512:(j + 1) * 512], ALU.add,
                         r=[PSH[pb], bmsb], w=[modsb])
                m = lambda i: modsb[:, i * D:(i + 1) * D]
                ng = lambda i: ngsb[:, i * D:(i + 1) * D]
                vs = lambda i: vst[:, i * D:(i + 1) * D]
                g.stt(vs(0), m(1), 1.0, ng(0), ALU.add, ALU.mult, r=[modsb, ngsb], w=[vst])
                g.cp(vs(1), m(0), r=[modsb], w=[vst])
                g.tt(vs(2), m(2), ng(1), ALU.mult, r=[modsb, ngsb], w=[vst])
                g.stt(vs(3), m(4), 1.0, ng(2), ALU.add, ALU.mult, r=[modsb, ngsb], w=[vst])
                g.cp(vs(4), m(3), r=[modsb], w=[vst])
                g.tt(vs(5), m(5), ng(3), ALU.mult, r=[modsb, ngsb], w=[vst])
                for rr in range(2):
                    g.dma("pool", MODV[(l * 2 + rr) * 6:(l * 2 + rr) * 6 + 6, :].rearrange("(o i) d -> o i d", o=1),
                          vst[rr:rr + 1, :].rearrange("p (i d) -> p i d", i=6), r=[vst], w=[H("MODV", l)])
            g.barrier()
        if stop_after == "pro":
            return _finish(nc, g, xin, yout, es)

        def modrow(l, rr, i):
            k = (l * 2 + rr) * 6 + i
            return MODV[k:k + 1, :].partition_broadcast(128)

        def rstd_from_ss(ss, n, r, w):
            g.ts(ss, ss, 1.0 / n, EPS, ALU.mult, ALU.add, r=r, w=w)
            g.act(ss, ss, AF.Sqrt, r=w, w=w)
            g.op("dve", lambda E: E.reciprocal(out=ss, in_=ss), r=w, w=w)

        TOKBLKS = [(i * 512, 512, 0) for i in range(8)] + [(T_LAT, 256, 1)]

        for l in range(n_layers):
            xsrc = xin if l == 0 else XS
            with ExitStack() as st:
                wb = sb(st, "wb", [128, 8, INX], BF16)
                wbh = [[T() for _ in range(INX // 1024)] for _ in range(8)]
                wv = w_in[l].rearrange("(kc p) n -> p kc n", p=128)
                for c4 in range(0, INX, 1024):
                    for kc in range(8):
                        g.dma("pool", wb[:, kc, c4:c4 + 1024], wv[:, kc, c4:c4 + 1024], w=[wbh[kc][c4 // 1024]])
                G1 = [sb(st, f"G1_{rr}", [128, D]) for rr in range(2)]
                S1 = [sb(st, f"S1_{rr}", [128, D]) for rr in range(2)]
                for rr in range(2):
                    g.dma("sp", G1[rr][:, :], modrow(l, rr, 0), r=[H("MODV", l)], w=[G1[rr]])
                    g.dma("sp", S1[rr][:, :], modrow(l, rr, 1), r=[H("MODV", l)], w=[S1[rr]])
                chCs = sb(st, "chCs", [128, 128], BF16)
                chSs = sb(st, "chSs", [128, 128], BF16)
                g.dma("sp", chCs[:, :], chC[:, :], w=[chCs])
                g.dma("sp", chSs[:, :], chS[:, :], w=[chSs])
                xring = Ring([sb(st, f"xa{i}", [128, D]) for i in range(2)])
                sqj = sb(st, "sqj", [128, D])
                ssr = Ring([sb(st, f"ss{i}", [128, 1]) for i in range(2)])
                t1r = Ring([sb(st, f"t1_{i}", [128, D]) for i in range(2)])
                hbr = Ring([sb(st, f"hb{i}", [128, D], BF16) for i in range(2)])
                hTr = Ring([sb(st, f"hT{i}", [128, 8, 512], BF16) for i in range(2)])
                rcr = Ring([sb(st, f"rc{i}", [128, 512]) for i in range(2)])
                rsr = Ring([sb(st, f"rs{i}", [128, 512]) for i in range(2)])
                stf = Ring([sb(st, f"stf{i}", [128, 512]) for i in range(4)])
                stb = Ring([sb(st, f"stb{i}", [128, 512], BF16) for i in range(4)])
                rt1 = Ring([sb(st, f"rt1_{i}", [128, 512]) for i in range(2)])
                rt2 = Ring([sb(st, f"rt2_{i}", [128, 512]) for i in range(2)])
                uTr = Ring([sb(st, f"uT{i}", [128, 512], BF16) for i in range(4)])
                fmb = Ring([2, 3, 4, 5])
                tmb = Ring([6, 7])
                if stop_after == ("Aw", l):
                    return _finish(nc, g, xin, yout, es)
                for (t0, W, rr) in TOKBLKS:
                    ntl = W // 128
                    hT = hTr.next()
                    for i in range(ntl):
                        r0 = t0 + i * 128
                        xa = xring.next()
                        g.dma("sp", xa[:, :], xsrc[r0:r0 + 128, :], r=[H("XS", r0 // 128)], w=[xa])
                        ss = ssr.next()
                        g.act(sqj[:, :], xa[:, :], AF.Square, r=[xa], w=[sqj])
                        g.op("dve", lambda E: E.reduce_sum(out=ss[:, :], in_=sqj[:, :], axis=AX.X), r=[sqj], w=[ss])
                        rstd_from_ss(ss[:, :], D, [ss], [ss])
                        t1 = t1r.next()
                        g.stt(t1[:, :], xa[:, :], ss[:, 0:1], G1[rr][:, :], ALU.mult, ALU.mult, r=[xa, ss, G1[rr]], w=[t1])
                        hb = hbr.next()
                        g.tt(hb[:, :], t1[:, :], S1[rr][:, :], ALU.add, r=[t1, S1[rr]], w=[hb], e="pool")
                        for half in range(2):
                            for k4 in range(4):
                                kc = half * 4 + k4
                                g.mm(psall[:, half, k4 * 128:(k4 + 1) * 128], lhsT=hb[:, kc * 128:(kc + 1) * 128],
                                     rhs=ident[:, :], start=True, stop=True, r=[hb, ident], w=[PSH[half]], inc=(k4 == 3))
                            g.cp(hT[:, half * 4:half * 4 + 4, i * 128:(i + 1) * 128],
                                 psall[:, half, :].rearrange("p (k t) -> p k t", k=4), r=[PSH[half]], w=[hT],
                                 e=("act" if half else "dve"))
                    if stop_after == ("Ah", l):
                        return _finish(nc, g, xin, yout, es)
                    rc = rcr.next()
                    rs = rsr.next()
                    g.dma("sp", rc[:, 0:W], ropeC[:, t0:t0 + W], w=[rc])
                    g.dma("sp", rs[:, 0:W], ropeS[:, t0:t0 + W], w=[rs])

                    def fm(c0):
                        pb = fmb.next()
                        for kc in range(8):
                            g.mm(psall[:, pb, 0:W], lhsT=wb[:, kc, c0:c0 + 128], rhs=hT[:, kc, 0:W], start=(kc == 0),
                                 stop=(kc == 7), r=[wbh[kc][c0 // 1024], hT], w=[PSH[pb]], inc=(kc == 7))
                        return pb

                    for (c00, dst, fn) in ((0, QHT, AF.Silu), (1536, ZFT, AF.Copy), (2048, ZBT, AF.Copy)):
                        for h in range(4):
                            pb = fm(c00 + h * 128)
                            sf = stf.next()
                            if fn == AF.Silu:
                                g.act(sf[:, 0:W], psall[:, pb, 0:W], fn, r=[PSH[pb]], w=[sf])
                            else:
                                g.cp(sf[:, 0:W], psall[:, pb, 0:W], r=[PSH[pb]], w=[sf], e="dve")
                            g.dma("pool", dst[h, :, t0:t0 + W], sf[:, 0:W], r=[sf], w=[H(dst.name + str(h), t0)])
                    if stop_after == ("Af", l):
                        return _finish(nc, g, xin, yout, es)
                    for (c00, csw, dst) in ((2560, INW, QRT), (2816, INW + 256, KRT)):
                        for ti in range(2):
                            pa = fm(c00 + ti * 128)
                            ps_ = fm(csw + ti * 128)
                            a1 = rt1.next()
                            a2 = rt2.next()
                            g.tt(a1[:, 0:W], psall[:, pa, 0:W], rc[:, 0:W], ALU.mult, r=[PSH[pa], rc], w=[a1])
                            g.tt(a2[:, 0:W], psall[:, ps_, 0:W], rs[:, 0:W], ALU.mult, r=[PSH[ps_], rs], w=[a2])
                            sbb = stb.next()
                            g.tt(sbb[:, 0:W], a1[:, 0:W], a2[:, 0:W], ALU.add, r=[a1, a2], w=[sbb], e="pool")
                            g.dma("pool", dst[ti, :, t0:t0 + W], sbb[:, 0:W], r=[sbb], w=[H(dst.name + str(ti), t0)])
                            if dst is QRT:
                                for gq in range(4):
                                    g.dma("pool", QRTM[ti, gq, gq * 32:(gq + 1) * 32, t0:t0 + W], sbb[gq * 32:(gq + 1) * 32, 0:W],
                                          r=[sbb, H("QRTMz")], w=[H("QRTM" + str(ti) + str(gq), t0)])
                    if stop_after == ("Ar", l):
                        return _finish(nc, g, xin, yout, es)
                    uts = []
                    for ti in range(2):
                        pb = fm(3328 + ti * 128)
                        uT = uTr.next()
                        g.cp(uT[:, 0:W], psall[:, pb, 0:W], r=[PSH[pb]], w=[uT], e="act")
                        uts.append(uT)
                    for i in range(ntl):
                        pb = tmb.next()
                        for ab, tab in enumerate((chCs, chSs)):
                            for ti in range(2):
                                g.mm(psall[:, pb, ab * 256 + ti * 128: ab * 256 + (ti + 1) * 128],
                                     lhsT=uts[ti][:, i * 128:(i + 1) * 128], rhs=tab[:, :], start=True, stop=True,
                                     r=[uts[ti], tab], w=[PSH[pb]], inc=(ab == 1 and ti == 1))
                        sbb = stb.next()
                        g.cp(sbb[:, :], psall[:, pb, :], r=[PSH[pb]], w=[sbb], e="dve")
                        r0 = t0 + i * 128
                        g.dma("pool", FAB[r0:r0 + 128, :], sbb[:, :], r=[sbb], w=[H("FAB", r0 // 128)])
                    if stop_after == ("At", l):
                        return _finish(nc, g, xin, yout, es)
                    for i in range(ntl):
                        r0 = t0 + i * 128

                        def tm(c0, ncol):
                            pb = tmb.next()
                            for kc in range(8):
                                g.mm(psall[:, pb, 0:ncol], lhsT=hT[:, kc, i * 128:(i + 1) * 128], rhs=wb[:, kc, c0:c0 + ncol],
                                     start=(kc == 0), stop=(kc == 7), r=[wbh[kc][c0 // 1024], wbh[kc][(c0 + ncol - 1) // 1024], hT],
                                     w=[PSH[pb]], inc=(kc == 7))
                            return pb
                        pb = tm(512, 512)
                        sbb = stb.next()
                        g.cp(sbb[:, :], psall[:, pb, :], r=[PSH[pb]], w=[sbb], e="dve")
                        g.dma("pool", VH[r0:r0 + 128, :], sbb[:, :], r=[sbb], w=[H("VH", r0 // 128)])
                        pb = tm(1024, 512)
                        sf = stf.next()
                        g.act(sf[:, :], psall[:, pb, :], AF.Silu, r=[PSH[pb]], w=[sf])
                        g.dma("pool", GH[r0:r0 + 128, :], sf[:, :], r=[sf], w=[H("GH", r0 // 128)])
                        pb = tm(3072, 256)
                        sbb = stb.next()
                        g.cp(sbb[:, 0:256], psall[:, pb, 0:256], r=[PSH[pb]], w=[sbb], e="dve")
                        g.dma("pool", VD[r0:r0 + 128, :], sbb[:, 0:256], r=[sbb], w=[H("VD", r0 // 128)])
                g.barrier()
            if stop_after == ("A", l):
                return _finish(nc, g, xin, yout, es)

            mixer_phase(nc, g, l, psall, PSH, H, sb, ident, identf, LB, OML, ONORM, NLAM, SUBLN, QHT, ZFT, ZBT, VH, GH, OF, OB, YC,
                        QRT, KRT, VD, maskf_in, maskb_in, QRTM)
            if stop_after in (("H", l), ("T", l)):
                return _finish(nc, g, xin, yout, es)
            with ExitStack() as st:
                wo = sb(st, "wo", [128, 8, D], BF16)
                wov1 = w_out[l].rearrange("(kc p) n -> p kc n", p=128)
                woh = [T() for _ in range(8)]
                for kc in range(8):
                    g.dma("pool", wo[:, kc, :], wov1[:, kc, :], w=[woh[kc]])
                GG1 = [sb(st, f"GG1_{rr}", [128, D]) for rr in range(2)]
                G2 = [sb(st, f"G2_{rr}", [128, D]) for rr in range(2)]
                S2 = [sb(st, f"S2_{rr}", [128, D]) for rr in range(2)]
                for rr in range(2):
                    g.dma("sp", GG1[rr][:, :], modrow(l, rr, 2), r=[H("MODV", l)], w=[GG1[rr]])
                    g.dma("sp", G2[rr][:, :], modrow(l, rr, 3), r=[H("MODV", l)], w=[G2[rr]])
                    g.dma("sp", S2[rr][:, :], modrow(l, rr, 4), r=[H("MODV", l)], w=[S2[rr]])
                ycr = Ring([sb(st, f"yc{i}", [128, D], BF16) for i in range(2)])
                xr = Ring([sb(st, f"xc{i}", [128, D]) for i in range(2)])
                ycTr = Ring([sb(st, f"ycT{i}", [128, 8, 128], BF16) for i in range(2)])
                sqj = sb(st, "sqj", [128, D])
                ssr = Ring([sb(st, f"ss{i}", [128, 1]) for i in range(4)])
                t1r = Ring([sb(st, f"t1_{i}", [128, D]) for i in range(2)])
                x1r = Ring([sb(st, f"x1_{i}", [128, D]) for i in range(2)])
                h2r = Ring([sb(st, f"h2_{i}", [128, D], BF16) for i in range(2)])
                h2Tr = Ring([sb(st, f"h2T{i}", [128, 8, 128], BF16) for i in range(2)])
                tpb = Ring([(0, 1)])
                ypb = Ring([(4, 5), (6, 7)])
                fg = fft_gen(nc, g, l, st, psall, PSH, H, sb, FAB, YC, dftC, dftS, dftCc, dftSc, (2, 3))
                next(fg)
                for tt_ in range(NT):
                    if tt_ + 1 < NT:
                        next(fg)
                    r0 = tt_ * 128
                    rr = 0 if r0 < T_LAT else 1
                    yc = ycr.next()
                    xc = xr.next()
                    g.dma("sp", yc[:, :], YC[r0:r0 + 128, :], r=[H("YCh", tt_), H("YCa", tt_), H("YCf", tt_)], w=[yc])
                    g.dma("sp", xc[:, :], xsrc[r0:r0 + 128, :], r=[H("XS", tt_)], w=[xc])
                    pbs = tpb.next()
                    ycT = ycTr.next()
                    for half in range(2):
                        pb = pbs[half]
                        for k4 in range(4):
                            kc = half * 4 + k4
                            g.mm(psall[:, pb, k4 * 128:(k4 + 1) * 128], lhsT=yc[:, kc * 128:(kc + 1) * 128], rhs=ident[:, :],
                                 start=True, stop=True, r=[yc, ident], w=[PSH[pb]], inc=(k4 == 3))
                        g.cp(ycT[:, half * 4:half * 4 + 4, :], psall[:, pb, :].rearrange("p (k t) -> p k t", k=4),
                             r=[PSH[pb]], w=[ycT], e=("act" if half else "dve"))
                    yb = ypb.next()
                    for nb in range(2):
                        for kc in range(8):
                            g.mm(psall[:, yb[nb], :], lhsT=ycT[:, kc, :], rhs=wo[:, kc, nb * 512:(nb + 1) * 512],
                                 start=(kc == 0), stop=(kc == 7), r=[ycT, woh[kc]], w=[PSH[yb[nb]]], inc=(kc == 7))
                    ypv = psall[:, yb[0]:yb[0] + 2, :]
                    ss = ssr.next()
                    g.act(sqj[:, :].rearrange("p (a b) -> p a b", a=2), ypv, AF.Square, r=[PSH[yb[0]], PSH[yb[1]]], w=[sqj])
                    g.op("dve", lambda E: E.reduce_sum(out=ss[:, :], in_=sqj[:, :], axis=AX.X), r=[sqj], w=[ss])
                    rstd_from_ss(ss[:, :], D, [ss], [ss])
                    t1 = t1r.next()
                    g.stt(t1[:, :].rearrange("p (a b) -> p a b", a=2), ypv, ss[:, 0:1],
                          GG1[rr][:, :].rearrange("p (a b) -> p a b", a=2), ALU.mult, ALU.mult,
                          r=[PSH[yb[0]], PSH[yb[1]], ss, GG1[rr]], w=[t1])
                    x1 = x1r.next()
                    g.tt(x1[:, :], t1[:, :], xc[:, :], ALU.add, r=[t1, xc], w=[x1], e="pool")
                    g.dma("pool", X1[r0:r0 + 128, :], x1[:, :], r=[x1], w=[H("X1", tt_)])
                    ss2 = ssr.next()
                    g.act(sqj[:, :], x1[:, :], AF.Square, r=[x1], w=[sqj])
                    g.op("dve", lambda E: E.reduce_sum(out=ss2[:, :], in_=sqj[:, :], axis=AX.X), r=[sqj], w=[ss2])
                    rstd_from_ss(ss2[:, :], D, [ss2], [ss2])
                    t2 = t1r.next()
                    g.stt(t2[:, :], x1[:, :], ss2[:, 0:1], G2[rr][:, :], ALU.mult, ALU.mult, r=[x1, ss2, G2[rr]], w=[t2])
                    h2 = h2r.next()
                    g.tt(h2[:, :], t2[:, :], S2[rr][:, :], ALU.add, r=[t2, S2[rr]], w=[h2], e="pool")
                    pbs = tpb.next()
                    h2T = h2Tr.next()
                    for half in range(2):
                        pb = pbs[half]
                        for k4 in range(4):
                            kc = half * 4 + k4
                            g.mm(psall[:, pb, k4 * 128:(k4 + 1) * 128], lhsT=h2[:, kc * 128:(kc + 1) * 128], rhs=ident[:, :],
                                 start=True, stop=True, r=[h2, ident], w=[PSH[pb]], inc=(k4 == 3))
                        g.cp(h2T[:, half * 4:half * 4 + 4, :], psall[:, pb, :].rearrange("p (k t) -> p k t", k=4),
                             r=[PSH[pb]], w=[h2T], e=("act" if half else "dve"))
                    g.dma("pool", H2T[:, :, r0:r0 + 128].rearrange("k p t -> p k t"), h2T[:, :, :], r=[h2T], w=[H("H2T", tt_)])
                g.barrier()
            if stop_after == ("C1", l):
                return _finish(nc, g, xin, yout, es)

            with ExitStack() as st:
                wi = sb(st, "wi", [128, 8, 2 * FFH], BF16)
                wih = [[T() for _ in range(4)] for _ in range(8)]
                wiv = w_ffn_in[l].rearrange("(kc p) n -> p kc n", p=128)
                for c4 in (0, 2816, 1408, 4224):
                    for kc in range(8):
                        g.dma("pool", wi[:, kc, c4:c4 + 1408], wiv[:, kc, c4:c4 + 1408], w=[wih[kc][c4 // 1408]])
                wo2 = sb(st, "wo2", [128, 22, D], BF16)
                wo2h = [T() for _ in range(22)]
                wov = w_ffn_out[l].rearrange("(hc p) n -> p hc n", p=128)
                for hc in range(22):
                    g.dma("pool", wo2[:, hc, :], wov[:, hc, :], w=[wo2h[hc]])
                GG2 = [sb(st, f"GG2_{rr}", [128, D]) for rr in range(2)]
                for rr in range(2):
                    g.dma("sp", GG2[rr][:, :], modrow(l, rr, 5), r=[H("MODV", l)], w=[GG2[rr]])
                hr = Ring([sb(st, f"h2c{i}", [128, 8, 256], BF16) for i in range(2)])
                x1r = Ring([sb(st, f"x1c{i}", [128, D]) for i in range(3)])
                sgr = Ring([sb(st, f"sg{i}", [128, 256]) for i in range(2)])
                acr = Ring([sb(st, f"ac{i}", [128, 256], BF16) for i in range(3)])
                sqj = sb(st, "sqj", [128, D])
                ssr = Ring([sb(st, f"ss{i}", [128, 1]) for i in range(2)])
                t1r = Ring([sb(st, f"t1_{i}", [128, D]) for i in range(2)])
                x2r = Ring([sb(st, f"x2_{i}", [128, D]) for i in range(2)])
                gub = Ring([(4, 5), (6, 7)])
                for b in range(N // 256):
                    c0 = b * 256
                    rr = 0 if c0 < T_LAT else 1
                    hT = hr.next()
                    g.dma("sp", hT[:, :, :], H2T[:, :, c0:c0 + 256].rearrange("k p t -> p k t"),
                          r=[H("H2T", 2 * b), H("H2T", 2 * b + 1)], w=[hT])
                    for hc in range(22):
                        pg, pu = gub.next()
                        for kc in range(8):
                            g.mm(psall[:, pg, 0:256], lhsT=wi[:, kc, hc * 128:(hc + 1) * 128], rhs=hT[:, kc, :],
                                 start=(kc == 0), stop=(kc == 7), r=[wih[kc][(hc * 128) // 1408], hT], w=[PSH[pg]], inc=(kc == 7))
                        for kc in range(8):
                            g.mm(psall[:, pu, 0:256], lhsT=wi[:, kc, FFH + hc * 128:FFH + (hc + 1) * 128], rhs=hT[:, kc, :],
                                 start=(kc == 0), stop=(kc == 7), r=[wih[kc][(FFH + hc * 128) // 1408], hT], w=[PSH[pu]], inc=(kc == 7))
                        sg = sgr.next()
                        g.act(sg[:, :], psall[:, pg, 0:256], AF.Silu, r=[PSH[pg]], w=[sg])
                        ac = acr.next()
                        g.tt(ac[:, :], sg[:, :], psall[:, pu, 0:256], ALU.mult, r=[sg, PSH[pu]], w=[ac])
                        for i in range(2):
                            for nb in range(2):
                                g.mm(psall[:, i * 2 + nb, :], lhsT=ac[:, i * 128:(i + 1) * 128],
                                     rhs=wo2[:, hc, nb * 512:(nb + 1) * 512], start=(hc == 0), stop=(hc == 21),
                                     r=[ac, wo2h[hc]], w=[PSH[i * 2 + nb]], inc=(i == 1 and nb == 1),
                                     skip_group_check=True)
                    for i in range(2):
                        tt_ = 2 * b + i
                        r0 = tt_ * 128
                        x1 = x1r.next()
                        g.dma("sp", x1[:, :], X1[r0:r0 + 128, :], r=[H("X1", tt_)], w=[x1])
                        opv = psall[:, 2 * i:2 * i + 2, :]
                        ss = ssr.next()
                        g.act(sqj[:, :].rearrange("p (a b) -> p a b", a=2), opv, AF.Square,
                              r=[PSH[2 * i], PSH[2 * i + 1]], w=[sqj])
                        g.op("dve", lambda E: E.reduce_sum(out=ss[:, :], in_=sqj[:, :], axis=AX.X), r=[sqj], w=[ss])
                        rstd_from_ss(ss[:, :], D, [ss], [ss])
                        t1 = t1r.next()
                        g.stt(t1[:, :].rearrange("p (a b) -> p a b", a=2), opv, ss[:, 0:1],
                              GG2[rr][:, :].rearrange("p (a b) -> p a b", a=2), ALU.mult, ALU.mult,
                              r=[PSH[2 * i], PSH[2 * i + 1], ss, GG2[rr]], w=[t1])
                        x2 = x2r.next()
                        g.tt(x2[:, :], t1[:, :], x1[:, :], ALU.add, r=[t1, x1], w=[x2], e="pool")
                        if l == n_layers - 1:
                            if r0 < T_LAT:
                                g.dma("pool", yout[r0:r0 + 128, :], x2[:, :], r=[x2], w=[H("yout", tt_)])
                        else:
                            g.dma("pool", XS[r0:r0 + 128, :], x2[:, :], r=[x2], w=[H("XS", tt_)])
                g.barrier()
        return _finish(nc, g, xin, yout, es)


def _finish(nc, g, xin, yout, es):
    g.barrier()
    return nc


HSEG = 256
DEBUG_PROG = False


def mixer_phase(nc, g, l, psall, PSH, H, sb, ident, identf, LB, OML, ONORM, NLAM, SUBLN, QHT, ZFT, ZBT, VH, GH, OF, OB, YC,
                QRT, KRT, VD, maskf_in, maskb_in, QRTM):
    with ExitStack() as st:
        gens = [attn_gen(nc, g, l, st, psall, PSH, H, sb, NLAM, SUBLN, QRT, KRT, VD, YC, identf, QRTM),
                hgrn_gen(nc, g, l, st, psall, PSH, H, sb, ident, LB, OML, ONORM, QHT, ZFT, ZBT, VH, GH, OF, OB, YC,
                         maskf_in, maskb_in)]
        prog = [0.0 for _ in gens]
        if DEBUG_PROG:
            print('sbuf remaining before gens start', nc.sbuf_bytes_remaining)
        alive = [True for _ in gens]
        while any(alive):
            i = min((p, k) for k, p in enumerate(prog) if alive[k])[1]
            try:
                first = prog[i] == 0.0
                prog[i] = next(gens[i])
                if DEBUG_PROG and first:
                    print('gen', i, 'allocated; sbuf remaining', nc.sbuf_bytes_remaining)
            except StopIteration:
                alive[i] = False
                if DEBUG_PROG:
                    print("gen", i, "finished at prog", prog)
        g.barrier()


def hgrn_gen(nc, g, l, st, psall, PSH, H, sb, ident, LB, OML, ONORM, QHT, ZFT, ZBT, VH, GH, OF, OB, YC,
             maskf_in, maskb_in):
    SG = HSEG
    NCS = SG // CH
    lat_segs = [(i * SG, SG) for i in range(T_LAT // SG)]
    segs = {0: [(T_LAT, T_CTX)] + lat_segs, 1: [(T_LAT, T_CTX)] + lat_segs[::-1]}
    nsteps = len(segs[0])
    mask = [sb(st, "maskf", [32, 4, 32]), sb(st, "maskb", [32, 4, 32])]
    ones = sb(st, "ones", [128, SG])
    S = [sb(st, f"S{d_}", [128, 4, 128]) for d_ in range(2)]
    Sb = [sb(st, f"Sb{d_}", [128, 4, 128], BF16) for d_ in range(2)]
    qs = Ring([sb(st, f"q{i}", [128, SG]) for i in range(2)])
    A1 = Ring([sb(st, f"A1_{i}", [128, SG]) for i in range(2)])
    A2 = Ring([sb(st, f"A2_{i}", [128, SG]) for i in range(2)])
    A3 = Ring([sb(st, f"A3_{i}", [128, SG]) for i in range(2)])
    A4 = Ring([sb(st, f"A4_{i}", [128, SG]) for i in range(2)])
    A5 = Ring([sb(st, f"A5_{i}", [128, SG]) for i in range(2)])
    keys = [(h, d_) for h in range(4) for d_ in range(2)]
    qt_ = {k: sb(st, f"qt{k[0]}{k[1]}", [128, SG], BF16) for k in keys}
    kt_ = {k: sb(st, f"kt{k[0]}{k[1]}", [128, SG], BF16) for k in keys}
    kh_ = {k: sb(st, f"kh{k[0]}{k[1]}", [128, SG], BF16) for k in keys}
    ktok = {k: sb(st, f"ktok{k[0]}{k[1]}", [32, NCS, 128], BF16) for k in keys}
    ed = {k: sb(st, f"ed{k[0]}{k[1]}", [128, NCS]) for k in keys}
    vck = [sb(st, f"vck{d_}", [32, NCS, HGW], BF16) for d_ in range(2)]
    ost = [sb(st, f"ost{d_}", [32, NCS, HGW]) for d_ in range(2)]
    scs = [Ring([sb(st, f"scs{d_}{i}", [32, 128], BF16) for i in range(2)]) for d_ in range(2)]
    ofr = Ring([sb(st, f"of{i}", [128, HGW]) for i in range(2)])
    obr = Ring([sb(st, f"ob{i}", [128, HGW]) for i in range(2)])
    ggr = Ring([sb(st, f"gg{i}", [128, HGW]) for i in range(2)])
    sqr = Ring([sb(st, f"sq{i}", [128, HGW]) for i in range(2)])
    ssr = Ring([sb(st, f"ss{i}", [128, 4]) for i in range(2)])
    ybr = Ring([sb(st, f"yb{i}", [128, HGW], BF16) for i in range(2)])
    BK = {0: (4, 5, 6), 1: (4, 5, 6)}
    ODST = (OF, OB)
    ZSRC = (ZFT, ZBT)
    TOTAL = float(nsteps + 2)
    ucnt = [0]
    UT = 4470 + 272

    def P():
        ucnt[0] += 1
        return ucnt[0] / UT

    for d_, src in enumerate((maskf_in, maskb_in)):
        for h in range(4):
            g.dma("sp", mask[d_][:, h, :], src[:, :], w=[mask[d_]])
            yield P()
    g.op("pool", lambda E: E.memset(ones[:, :], 1.0), w=[ones])
    yield P()
    for d_ in range(2):
        g.op("pool", lambda E: E.memset(S[d_][:, :, :], 0.0), w=[S[d_]])
        yield P()
        g.op("pool", lambda E: E.memset(Sb[d_][:, :, :], 0.0), w=[Sb[d_]])
        yield P()
    for step in range(nsteps):
        base = float(step)
        for d_ in range(2):
            t0, W = segs[d_][step]
            nc_ = W // CH
            g.dma("sp", vck[d_][:, 0:nc_, :], VH[t0:t0 + W, :].rearrange("(c s) n -> s c n", s=CH),
                  r=[H("VH", k) for k in range(t0 // 128, (t0 + W) // 128)], w=[vck[d_]])
            yield P()
        for ki_, (h, d_) in enumerate(keys):
            t0, W = segs[d_][step]
            nc_ = W // CH
            ld = d_ * 4 + l
            q = qs.next()
            a1 = A1.next(); a2 = A2.next(); a3 = A3.next(); a4 = A4.next(); a5 = A5.next()
            hk = (t0 // 512) * 512 if t0 < T_LAT else T_LAT
            g.dma("sp", q[:, 0:W], QHT[h, :, t0:t0 + W], r=[H("QHT" + str(h), hk)], w=[q])
            yield P()
            g.dma("sp", a1[:, 0:W], ZSRC[d_][h, :, t0:t0 + W], r=[H(ZSRC[d_].name + str(h), hk)], w=[a1])
            yield P()
            g.act(a1[:, 0:W], a1[:, 0:W], AF.Exp, r=[a1], w=[a1], scale=-1.0)
            yield P()
            g.ts(a1[:, 0:W], a1[:, 0:W], 1.0, None, ALU.add, r=[a1], w=[a1])
            yield P()
            g.op("dve", lambda E: E.reciprocal(out=a1[:, 0:W], in_=a1[:, 0:W]), r=[a1], w=[a1])
            yield P()
            g.ts(a1[:, 0:W], a1[:, 0:W], OML[:, h, ld:ld + 1], LB[:, h, ld:ld + 1], ALU.mult, ALU.add,
                 r=[a1, OML, LB], w=[a1])
            yield P()
            g.ts(a2[:, 0:W], a1[:, 0:W], -1.0, 1.0, ALU.mult, ALU.add, r=[a1], w=[a2], e="pool")
            yield P()
            g.act(a1[:, 0:W], a1[:, 0:W], AF.Ln, r=[a1], w=[a1])
            yield P()
            g.op("dve", lambda E: E.tensor_tensor_scan(out=a3[:, 0:W], data0=ones[:, 0:W], data1=a1[:, 0:W],
                                                       initial=0.0, op0=ALU.mult, op1=ALU.add),
                 r=[ones, a1], w=[a3])
            yield P()
            g.tt(a1[:, 0:W], a3[:, 0:W], a1[:, 0:W], ALU.subtract, r=[a3, a1], w=[a1], e="pool")
            yield P()
            Pv = a3[:, 0:W].rearrange("p (c j) -> p c j", j=CH)
            Qv = a1[:, 0:W].rearrange("p (c j) -> p c j", j=CH)
            a4v = a4[:, 0:W].rearrange("p (c j) -> p c j", j=CH)
            a5v = a5[:, 0:W].rearrange("p (c j) -> p c j", j=CH)
            bc = lambda ap: ap.to_broadcast([128, nc_, CH])
            if d_ == 0:
                g.tt(a4v, Pv, bc(Qv[:, :, 0:1]), ALU.subtract, r=[a3, a1], w=[a4])
                yield P()
                g.tt(a5v, bc(Pv[:, :, CH - 1:CH]), Pv, ALU.subtract, r=[a3], w=[a5], e="pool")
                yield P()
            else:
                g.tt(a4v, bc(Pv[:, :, CH - 1:CH]), Qv, ALU.subtract, r=[a3, a1], w=[a4])
                yield P()
                g.tt(a5v, Qv, bc(Qv[:, :, 0:1]), ALU.subtract, r=[a1], w=[a5], e="pool")
                yield P()
            e_ = ed[h, d_]
            g.tt(e_[:, 0:nc_], Pv[:, :, CH - 1], Qv[:, :, 0], ALU.subtract, r=[a3, a1], w=[e_])
            yield P()
            g.act(e_[:, 0:nc_], e_[:, 0:nc_], AF.Exp, r=[e_], w=[e_])
            yield P()
            g.ts(a4[:, 0:W], a4[:, 0:W], -80.0, None, ALU.max, r=[a4], w=[a4])
            yield P()
            g.act(a3[:, 0:W], a4[:, 0:W], AF.Exp, r=[a4], w=[a3])
            yield P()
            g.act(a1[:, 0:W], a4[:, 0:W], AF.Exp, r=[a4], w=[a1], scale=-1.0)
            yield P()
            g.act(a5[:, 0:W], a5[:, 0:W], AF.Exp, r=[a5], w=[a5])
            yield P()
            g.tt(qt_[h, d_][:, 0:W], q[:, 0:W], a3[:, 0:W], ALU.mult, r=[q, a3], w=[qt_[h, d_]])
            yield P()
            g.tt(kt_[h, d_][:, 0:W], a2[:, 0:W], a1[:, 0:W], ALU.mult, r=[a2, a1], w=[kt_[h, d_]], e="pool")
            yield P()
            g.tt(kh_[h, d_][:, 0:W], a2[:, 0:W], a5[:, 0:W], ALU.mult, r=[a2, a5], w=[kh_[h, d_]])
            yield P()
            for c4 in range(0, nc_, 4):
                pb = 4
                for cc in range(4):
                    c = c4 + cc
                    g.mm(psall[0:32, pb, cc * 128:(cc + 1) * 128], lhsT=kh_[h, d_][:, c * CH:(c + 1) * CH],
                         rhs=ident[:, :], start=True, stop=True, r=[kh_[h, d_], ident], w=[PSH[pb]], inc=(cc == 3))
                g.cp(ktok[h, d_][:, c4:c4 + 4, :], psall[0:32, pb, :].rearrange("p (c k) -> p c k", c=4),
                     r=[PSH[pb]], w=[ktok[h, d_]], e="dve")
                yield P()
        ncs = [segs[d_][step][1] // CH for d_ in range(2)]
        for ci in range(max(ncs)):
            for d_ in range(2):
                nc_ = ncs[d_]
                if ci >= nc_:
                    continue
                c = ci if d_ == 0 else nc_ - 1 - ci
                cs = slice(c * CH, (c + 1) * CH)
                bx, by, bz = BK[d_]
                for h in range(4):
                    g.mm(psall[0:32, bx, h * 32:(h + 1) * 32], lhsT=kt_[h, d_][:, cs], rhs=qt_[h, d_][:, cs],
                         start=True, stop=True, r=[kt_[h, d_], qt_[h, d_]], w=[PSH[bx]], inc=(h == 3))
                for h in range(4):
                    g.mm(psall[:, by, h * 128:(h + 1) * 128], lhsT=ktok[h, d_][:, c, :],
                         rhs=vck[d_][:, c, h * 128:(h + 1) * 128], start=True, stop=True,
                         r=[ktok[h, d_], vck[d_]], w=[PSH[by]], inc=(h == 3))
                yield P()
                sc = scs[d_].next()
                g.tt(sc[:, :], psall[0:32, bx, 0:128], mask[d_][:, :, :].rearrange("p h t -> p (h t)"), ALU.mult,
                     r=[PSH[bx], mask[d_]], w=[sc])
                for h in range(4):
                    g.stt(S[d_][:, h, :], S[d_][:, h, :], ed[h, d_][:, c:c + 1], psall[:, by, h * 128:(h + 1) * 128],
                          ALU.mult, ALU.add, r=[S[d_], ed[h, d_], PSH[by]], w=[S[d_]])
                yield P()
                for h in range(4):
                    g.mm(psall[0:32, bz, h * 128:(h + 1) * 128], lhsT=qt_[h, d_][:, cs], rhs=Sb[d_][:, h, :],
                         start=True, stop=False, r=[qt_[h, d_], Sb[d_]], w=[PSH[bz]], inc=False)
                    g.mm(psall[0:32, bz, h * 128:(h + 1) * 128], lhsT=sc[:, h * 32:(h + 1) * 32],
                         rhs=vck[d_][:, c, h * 128:(h + 1) * 128], start=False, stop=True,
                         r=[sc, vck[d_]], w=[PSH[bz]], inc=(h == 3))
                yield P()
                g.cp(ost[d_][:, c, :], psall[0:32, bz, :], r=[PSH[bz]], w=[ost[d_]], e="dve")
                g.cp(Sb[d_][:, 0:2, :], S[d_][:, 0:2, :], r=[S[d_]], w=[Sb[d_]], e="act")
                g.cp(Sb[d_][:, 2:4, :], S[d_][:, 2:4, :], r=[S[d_]], w=[Sb[d_]], e="dve")
                yield P()
        for d_ in range(2):
            t0, W = segs[d_][step]
            nc_ = W // CH
            g.dma("pool", ODST[d_][t0:t0 + W, :].rearrange("(c s) v -> s c v", s=CH),
                  ost[d_][:, 0:nc_, :], r=[ost[d_]],
                  w=[H(ODST[d_].name, k) for k in range(t0 // 128, (t0 + W) // 128)])
            yield P()
    for tt_ in range(NT):
        r0 = tt_ * 128
        of = ofr.next(); ob = obr.next(); gg = ggr.next(); sq = sqr.next(); ss = ssr.next(); yb = ybr.next()
        g.dma("sp", of[:, :], OF[r0:r0 + 128, :], r=[H("OF", tt_)], w=[of])
        yield P()
        g.dma("sp", ob[:, :], OB[r0:r0 + 128, :], r=[H("OB", tt_)], w=[ob])
        yield P()
        g.dma("sp", gg[:, :], GH[r0:r0 + 128, :], r=[H("GH", tt_)], w=[gg])
        yield P()
        g.tt(of[:, :], of[:, :], ob[:, :], ALU.add, r=[of, ob], w=[of], e="pool")
        yield P()
        g.tt(sq[:, :], of[:, :], of[:, :], ALU.mult, r=[of], w=[sq], e="pool")
        yield P()
        g.op("dve", lambda E: E.tensor_reduce(out=ss[:, :], in_=sq[:, :].rearrange("p (h v) -> p h v", h=4),
                                              axis=AX.X, op=ALU.add), r=[sq], w=[ss])
        yield P()
        g.ts(ss[:, :], ss[:, :], 1.0 / 128, EPS, ALU.mult, ALU.add, r=[ss], w=[ss])
        yield P()
        g.act(ss[:, :], ss[:, :], AF.Sqrt, r=[ss], w=[ss])
        yield P()
        g.op("dve", lambda E: E.reciprocal(out=ss[:, :], in_=ss[:, :]), r=[ss], w=[ss])
        yield P()
        ofv = of[:, :].rearrange("p (h v) -> p h v", h=4)
        g.tt(ofv, ofv, ss[:, :].unsqueeze(2).to_broadcast([128, 4, 128]), ALU.mult, r=[of, ss], w=[of])
        yield P()
        g.tt(ofv, ofv, ONORM[:, l * 128:(l + 1) * 128].unsqueeze(1).to_broadcast([128, 4, 128]), ALU.mult,
             r=[of, ONORM], w=[of], e="pool")
        yield P()
        g.tt(yb[:, :], of[:, :], gg[:, :], ALU.mult, r=[of, gg], w=[yb])
        yield P()
        g.dma("pool", YC[r0:r0 + 128, 0:HGW], yb[:, :], r=[yb], w=[H("YCh", tt_)])
        yield P()


def attn_gen(nc, g, l, st, psall, PSH, H, sb, NLAM, SUBLN, QRT, KRT, VD, YC, identf, QRTM):
    kr = sb(st, "kr", [128, 2, N], BF16)
    qmr = Ring([sb(st, f"qm{i}", [128, 2, 4, 512], BF16) for i in range(2)])
    vp = sb(st, "vp", [128, NT, 4, 65], BF16)
    ptr = Ring([sb(st, f"pt{i}", [128, 512], BF16) for i in range(3)])
    accS = [Ring([sb(st, f"accS{m}{i}", [65, 512]) for i in range(2)]) for m in range(2)]
    rsr = Ring([sb(st, f"rs{i}", [128, 8]) for i in range(2)])
    tmr = Ring([sb(st, f"tm{i}", [128, 4, 64]) for i in range(2)])
    O2r = Ring([sb(st, f"O2_{i}", [128, 16, 64]) for i in range(2)])
    SQr = Ring([sb(st, f"SQ_{i}", [128, 16, 64]) for i in range(1)])
    SSr = Ring([sb(st, f"SS_{i}", [128, 16]) for i in range(2)])
    ysr = Ring([sb(st, f"ys{i}", [128, 4, 256], BF16) for i in range(2)])
    allblk = [b * 512 for b in range(8)] + [T_LAT]
    for ti in range(2):
        g.dma("sp", kr[:, ti, :], KRT[ti, :, :], r=[H("KRT" + str(ti), t0) for t0 in allblk], w=[kr])
    g.op("dve", lambda E: E.memset(vp[:, :, :, 64:65], 1.0), w=[vp])
    vph = [T() for _ in range(NT)]
    qmb = {}

    def load_qm(bi):
        q0, W, _ = qblocks[bi]
        qm = qmr.next()
        for ti in range(2):
            g.dma("sp", qm[:, ti, :, 0:W], QRTM[ti, :, :, q0:q0 + W].rearrange("g p n -> p g n"),
                  r=[H("QRTM" + str(ti) + str(gq), q0) for gq in range(4)] + [H("QRTMz")], w=[qm])
        qmb[bi] = qm

    for kt in range(NT):
        g.dma("sp", vp[:, kt, :, 0:64], VD[kt * 128:(kt + 1) * 128, :].rearrange("p (h d) -> p h d", h=4),
              r=[H("VD", kt), vp.h], w=[vph[kt]])
    qblocks = [(b * 512, 512, list(range(32)) + [32, 33]) for b in range(8)] + [(T_LAT, 256, [32, 33])]
    load_qm(0)
    yield 0.0
    spb = Ring([0, 1])
    acc = (2, 3)
    TB = 7
    LA = 1
    steps = []
    for bi, (q0, W, kts) in enumerate(qblocks):
        for h in range(4):
            for ki, kt in enumerate(kts):
                for m in range(2):
                    steps.append((bi, h, ki, kt, m))
    NS = float(len(steps))
    state = {}

    def issue_score(i):
        bi, h, ki, kt, m = steps[i]
        q0, W, kts = qblocks[bi]
        ti = h // 2
        gq = 2 * (h % 2) + m
        pb = spb.next()
        if bi not in qmb:
            load_qm(bi)
        if h == 0 and ki == 0 and m == 0 and bi + 1 < len(qblocks) and (bi + 1) not in qmb:
            load_qm(bi + 1)
        qm = qmb[bi]
        g.mm(psall[:, pb, 0:W], lhsT=kr[:, ti, kt * 128:(kt + 1) * 128], rhs=qm[:, ti, gq, 0:W],
             start=True, stop=True, r=[kr, qm], w=[PSH[pb]])
        state[i] = pb

    for i in range(min(LA, len(steps))):
        issue_score(i)
    cur = {}
    for i, (bi, h, ki, kt, m) in enumerate(steps):
        q0, W, kts = qblocks[bi]
        nq = W // 128
        if ki == 0 and m == 0 and h == 0:
            cur["O2"] = O2r.next(); cur["SQ"] = SQr.next(); cur["SS"] = SSr.next(); cur["ys"] = ysr.next()
        if i + LA < len(steps):
            issue_score(i + LA)
        pb = state.pop(i)
        pt = ptr.next()
        g.act(pt[:, 0:W], psall[:, pb, 0:W], AF.Exp, r=[PSH[pb]], w=[pt], scale=ATT_SCALE)
        g.mm(psall[0:65, acc[m], 0:W], lhsT=vp[:, kt, h, :], rhs=pt[:, 0:W], start=(ki == 0), stop=(ki == len(kts) - 1),
             r=[pt, vph[kt]], w=[PSH[acc[m]]])
        if ki == len(kts) - 1 and m == 1:
            O2 = cur["O2"]; SQ = cur["SQ"]; SS = cur["SS"]; ys = cur["ys"]
            yield (i + 0.5) / NS
            aS = [accS[0].next(), accS[1].next()]
            for mm_ in range(2):
                g.cp(aS[mm_][:, 0:W], psall[0:65, acc[mm_], 0:W], r=[PSH[acc[mm_]]], w=[aS[mm_]], e="dve")
            rs = rsr.next()
            tm = tmr.next()
            O2h = O2[:, :, :].rearrange("p (q h) d -> p q h d", h=4)[:, 0:nq, h, :]
            for mm_ in range(2):
                yield (i + 0.6 + 0.2 * mm_) / NS
                for qt in range(nq):
                    g.mm(psall[:, TB, qt * 65:(qt + 1) * 65], lhsT=aS[mm_][0:65, qt * 128:(qt + 1) * 128],
                         rhs=identf[0:65, 0:65], start=True, stop=True, r=[aS[mm_], identf], w=[PSH[TB]],
                         inc=(qt == nq - 1))
                yield (i + 0.7 + 0.2 * mm_) / NS
                tv = psall[:, TB, 0:nq * 65].rearrange("p (q c) -> p q c", c=65)
                g.op("dve", lambda E: E.reciprocal(out=rs[:, mm_ * 4:mm_ * 4 + nq].unsqueeze(2), in_=tv[:, :, 64:65]),
                     r=[PSH[TB]], w=[rs])
                if mm_ == 0:
                    g.tt(O2h, tv[:, :, 0:64], rs[:, 0:nq].unsqueeze(2).to_broadcast([128, nq, 64]), ALU.mult,
                         r=[PSH[TB], rs], w=[O2])
                else:
                    g.ts(rs[:, 4:4 + nq], rs[:, 4:4 + nq], NLAM[:, l:l + 1], None, ALU.mult, r=[rs, NLAM], w=[rs])
                    g.tt(tm[:, 0:nq, :], tv[:, :, 0:64], rs[:, 4:4 + nq].unsqueeze(2).to_broadcast([128, nq, 64]), ALU.mult,
                         r=[PSH[TB], rs], w=[tm])
                    g.tt(O2h, O2h, tm[:, 0:nq, :], ALU.add, r=[O2, tm], w=[O2], e="pool")
            if h == 3:
                n16 = nq * 4
                g.tt(SQ[:, 0:n16, :], O2[:, 0:n16, :], O2[:, 0:n16, :], ALU.mult, r=[O2], w=[SQ], e="pool")
                g.op("dve", lambda E: E.tensor_reduce(out=SS[:, 0:n16], in_=SQ[:, 0:n16, :], axis=AX.X, op=ALU.add),
                     r=[SQ], w=[SS])
                g.ts(SS[:, 0:n16], SS[:, 0:n16], 1.0 / 64, EPS, ALU.mult, ALU.add, r=[SS], w=[SS])
                g.act(SS[:, 0:n16], SS[:, 0:n16], AF.Sqrt, r=[SS], w=[SS])
                g.op("dve", lambda E: E.reciprocal(out=SS[:, 0:n16], in_=SS[:, 0:n16]), r=[SS], w=[SS])
                g.tt(O2[:, 0:n16, :], O2[:, 0:n16, :], SS[:, 0:n16].unsqueeze(2).to_broadcast([128, n16, 64]), ALU.mult,
                     r=[O2, SS], w=[O2])
                g.tt(ys[:, 0:nq, :].rearrange("p q (h d) -> p (q h) d", h=4), O2[:, 0:n16, :],
                     SUBLN[:, l * 64:(l + 1) * 64].unsqueeze(1).to_broadcast([128, n16, 64]), ALU.mult,
                     r=[O2, SUBLN], w=[ys], e="pool")
                g.dma("pool", YC[q0:q0 + W, 512:768].rearrange("(q p) c -> p q c", p=128), ys[:, 0:nq, :], r=[ys],
                      w=[H("YCa", k) for k in range(q0 // 128, (q0 + W) // 128)])
        yield (i + 1) / NS


def fft_gen(nc, g, l, st, psall, PSH, H, sb, FAB, YC, dftC, dftS, dftCc, dftSc, banks):
    ab = sb(st, "ab", [128, NT, 512], BF16)
    g.dma("sp", ab[:, :, :], FAB[:, :].rearrange("(t p) c -> p t c", p=128), r=[H("FAB", k) for k in range(NT)], w=[ab])
    ctr = Ring([sb(st, f"ct{i}", [128, 32, 128], BF16) for i in range(2)])
    str_ = Ring([sb(st, f"st{i}", [128, 32, 128], BF16) for i in range(2)])
    ybr = Ring([sb(st, f"yb{i}", [128, 256], BF16) for i in range(2)])
    pbr = Ring(list(banks))
    for j in range(NT):
        ct = ctr.next()
        sn = str_.next()
        if j < 32:
            nti, tb = 32, 0
            g.dma("sp", ct[:, :, :], dftC[j].rearrange("p (t c) -> p t c", t=32), w=[ct])
            g.dma("sp", sn[:, :, :], dftS[j].rearrange("p (t c) -> p t c", t=32), w=[sn])
        else:
            nti, tb = 2, 32
            g.dma("sp", ct[:, 0:2, :], dftCc[j - 32].rearrange("p (t c) -> p t c", t=2), w=[ct])
            g.dma("sp", sn[:, 0:2, :], dftSc[j - 32].rearrange("p (t c) -> p t c", t=2), w=[sn])
        pb = pbr.next()
        for ti in range(nti):
            g.mm(psall[:, pb, 0:256], lhsT=ct[:, ti, :], rhs=ab[:, tb + ti, 0:256], start=(ti == 0), stop=False,
                 r=[ct, ab], w=[PSH[pb]], inc=False)
            g.mm(psall[:, pb, 0:256], lhsT=sn[:, ti, :], rhs=ab[:, tb + ti, 256:512], start=False, stop=(ti == nti - 1),
                 r=[sn, ab], w=[PSH[pb]], inc=(ti == nti - 1))
        yb = ybr.next()
        g.cp(yb[:, :], psall[:, pb, 0:256], r=[PSH[pb]], w=[yb], e="act")
        g.dma("pool", YC[j * 128:(j + 1) * 128, 768:1024], yb[:, :], r=[yb], w=[H("YCf", j)])
        yield j


_CONST = {}


def _consts():
    if _CONST:
        return _CONST
    bf = ml_dtypes.bfloat16
    inv_freq = 1.0 / (10000.0 ** (np.arange(0, 16, 2, dtype=np.float32) / 16.0))
    t = np.arange(T_LAT)
    pos = np.stack([t // 64, t % 64], axis=0).astype(np.float32)
    ang = pos[:, None, :] * inv_freq[None, :, None].astype(np.float32)
    cos = np.cos(ang).astype(np.float32)
    sin = np.sin(ang).astype(np.float32)
    C = np.ones((128, N), np.float32)
    S = np.zeros((128, N), np.float32)
    for p in range(128):
        d = p % 32
        axis, half, F = d // 16, (d // 8) % 2, d % 8
        C[p, :T_LAT] = cos[axis, F]
        S[p, :T_LAT] = sin[axis, F] * (-1.0 if half == 0 else 1.0)
    _CONST["ropeC"] = C
    _CONST["ropeS"] = S

    def dft_tables(T_, scale):
        tt = np.arange(T_, dtype=np.int64)
        prod = (tt[:, None] * tt[None, :]) % T_
        angm = 2.0 * np.pi * prod.astype(np.float64) / T_
        Cm = (np.cos(angm) * scale).astype(np.float32)
        Sm = (-np.sin(angm) * scale).astype(np.float32)
        nt = T_ // 128

        def blk(M):
            return np.ascontiguousarray(M.reshape(nt, 128, nt, 128).transpose(2, 1, 0, 3)).reshape(nt, 128, nt * 128).astype(bf)
        return blk(Cm), blk(Sm)
    _CONST["dftC"], _CONST["dftS"] = dft_tables(T_LAT, 1.0 / 64.0)
    _CONST["dftCc"], _CONST["dftSc"] = dft_tables(T_CTX, 1.0 / 16.0)
    cc = np.arange(64)
    a64 = 2.0 * np.pi * ((cc[:, None] * cc[None, :]) % 64) / 64.0
    chC = np.zeros((128, 128), np.float32)
    chS = np.zeros((128, 128), np.float32)
    for gI in range(2):
        chC[gI * 64:(gI + 1) * 64, gI * 64:(gI + 1) * 64] = np.cos(a64) / 8.0
        chS[gI * 64:(gI + 1) * 64, gI * 64:(gI + 1) * 64] = np.sin(a64) / 8.0
    _CONST["chC"] = chC.astype(bf)
    _CONST["chS"] = chS.astype(bf)
    _CONST["ident"] = np.eye(128, dtype=np.float32).astype(bf)
    _CONST["identf"] = np.eye(128, dtype=np.float32)
    s_i = np.arange(32)
    _CONST["maskf"] = (s_i[:, None] <= s_i[None, :]).astype(np.float32)
    _CONST["maskb"] = (s_i[:, None] >= s_i[None, :]).astype(np.float32)
    return _CONST


def _swap_cols():
    idx = np.arange(256)
    d = idx % 32
    half = (d // 8) % 2
    return idx + np.where(half == 0, 8, -8)


def prepare_inputs(x, c, ctx, c_ctx, w_mod, b_mod, norm_g, w_in, w_out, hg_lb_logits, hg_onorm,
                   da_lambda, da_subln, w_ffn_in, w_ffn_out):
    f = lambda a: np.ascontiguousarray(np.asarray(a, dtype=np.float32))
    x, c, ctx, c_ctx = f(x), f(c), f(ctx), f(c_ctx)
    w_in = f(w_in)
    sw = _swap_cols()
    w_in_x = np.concatenate([w_in, w_in[:, :, 2560 + sw], w_in[:, :, 2816 + sw]], axis=2)
    shared = {
        "w_mod": f(w_mod), "b_mod": f(b_mod), "norm_g": f(norm_g).reshape(DEPTH, 4 * D),
        "w_in": np.ascontiguousarray(w_in_x), "w_out": f(w_out),
        "lb_logits": f(hg_lb_logits).reshape(8, HGW), "hg_onorm": f(hg_onorm).reshape(1, -1),
        "da_lambda": f(da_lambda).reshape(1, -1), "da_subln": f(da_subln).reshape(1, -1),
        "w_ffn_in": f(w_ffn_in), "w_ffn_out": f(w_ffn_out),
    }
    shared.update(_consts())
    in_maps = []
    for core in range(8):
        b = core % 4
        m = dict(shared)
        m["xin"] = np.ascontiguousarray(np.concatenate([x[b], ctx[b]], axis=0))
        m["cond"] = np.ascontiguousarray(np.stack([c[b], c_ctx], axis=0))
        in_maps.append(m)
    return in_maps


_NC_CACHE = {}


def kernel(x, c, ctx, c_ctx, w_mod, b_mod, norm_g, w_in, w_out, hg_lb_logits, hg_onorm,
           da_lambda, da_subln, w_ffn_in, w_ffn_out):
    in_maps = prepare_inputs(x, c, ctx, c_ctx, w_mod, b_mod, norm_g, w_in, w_out, hg_lb_logits, hg_onorm,
                             da_lambda, da_subln, w_ffn_in, w_ffn_out)
    nc = build()
    res = run_bass_kernel_spmd(nc, in_maps, core_ids=list(range(8)))
    out = np.stack([np.asarray(res.results[b]["yout"], dtype=np.float32) for b in range(4)], axis=0)
    return out
```
